# Optimizing a Trainium2 kernel written in Bass

```python
import math
import jax, jax.numpy as jnp
from jax import lax
import numpy as np

D_MODEL = 4096
BATCH = 2
SEQ = 4096
DEPTH = 2

HEAD_DIM = 128
MIX_WIDTH = D_MODEL
DIFF_WIDTH = MIX_WIDTH // 2
DIL_WIDTH = MIX_WIDTH - DIFF_WIDTH
N_DIFF_HEADS = DIFF_WIDTH // (2 * HEAD_DIM)
N_DIL_HEADS = DIL_WIDTH // HEAD_DIM
IN_WIDTH = 3 * DIFF_WIDTH + 3 * DIL_WIDTH
IN_SPLITS = (DIFF_WIDTH, 2 * DIFF_WIDTH, 3 * DIFF_WIDTH,
             3 * DIFF_WIDTH + DIL_WIDTH, 3 * DIFF_WIDTH + 2 * DIL_WIDTH)
DIL_PATTERNS = ((128, 1), (512, 4), (2048, 16))
D_FF = 256 * ((8 * D_MODEL // 3 + 255) // 256)
CONV_WIDTH = 3
NUM_BUCKETS = 32
MAX_DISTANCE = 2048
Q_BLOCK = 128
LN_EPS = 1e-5
NEG_INF = -1e30
DEEPNORM_ALPHA = (2 * DEPTH) ** 0.25
DEEPNORM_BETA = (8 * DEPTH) ** -0.25

kernel_name = "hymba_style_diff_dilated_convglu_deepnorm_adaln"


def ln_plain(x):
    xf = x.astype(jnp.float32)
    mu = jnp.mean(xf, axis=-1, keepdims=True)
    var = jnp.mean(jnp.square(xf - mu), axis=-1, keepdims=True)
    return ((xf - mu) * lax.rsqrt(var + LN_EPS)).astype(x.dtype)


def layer_norm(x, g, b):
    return ln_plain(x) * g + b


def rms_norm(x, g):
    xf = x.astype(jnp.float32)
    y = xf * lax.rsqrt(jnp.mean(jnp.square(xf), axis=-1, keepdims=True) + LN_EPS)
    return y.astype(x.dtype) * g


def t5_bucket(dist):
    n = jnp.maximum(dist, 0)
    max_exact = NUM_BUCKETS // 2
    nf = jnp.maximum(n, max_exact).astype(jnp.float32)
    large = max_exact + (jnp.log(nf / max_exact) / math.log(MAX_DISTANCE / max_exact)
                         * (NUM_BUCKETS - max_exact)).astype(jnp.int32)
    large = jnp.minimum(large, NUM_BUCKETS - 1)
    return jnp.where(n < max_exact, n, large)


def diff_attention(q, k, v, lam, bias_tab):
    B, S, H, _, E = q.shape
    scale = E ** -0.5
    kpos = jnp.arange(S)

    def block(n):
        start = n * Q_BLOCK
        qb = lax.dynamic_slice_in_dim(q, start, Q_BLOCK, axis=1)
        logits = jnp.einsum("bqhme,bkhme->bhmqk", qb, k).astype(jnp.float32) * scale
        dist = (start + jnp.arange(Q_BLOCK))[:, None] - kpos[None, :]
        bias = jnp.transpose(bias_tab[t5_bucket(dist)], (2, 0, 1)).astype(jnp.float32)
        logits = jnp.where(dist >= 0, logits + bias[None, :, None], NEG_INF)
        p = jax.nn.softmax(logits, axis=-1)
        attn = p[:, :, 0] - lam * p[:, :, 1]
        return jnp.einsum("bhqk,bkhe->bqhe", attn.astype(v.dtype), v)

    out = lax.map(block, jnp.arange(S // Q_BLOCK))
    return jnp.transpose(out, (1, 0, 2, 3, 4)).reshape(B, S, H, v.shape[-1])


def dilated_branch(q, k, v, window, dil, bias_tab):
    B, S, H, E = q.shape
    span = window // dil
    blk = span
    chunk = dil * blk
    Sp = -(-S // chunk) * chunk
    L = Sp // dil
    nb = L // blk

    def to_classes(t):
        t = jnp.pad(t, ((0, 0), (0, Sp - S), (0, 0), (0, 0)))
        t = jnp.swapaxes(t.reshape(B, L, dil, H, E), 1, 2)
        return t.reshape(B, dil, nb, blk, H, E)

    def with_prev(t):
        prev = jnp.concatenate([jnp.zeros_like(t[:, :, :1]), t[:, :, :-1]], axis=2)
        return jnp.concatenate([prev, t], axis=3)

    qc = to_classes(q)
    kw = with_prev(to_classes(k))
    vw = with_prev(to_classes(v))
    logits = jnp.einsum("bdnqhe,bdnkhe->bdnhqk", qc, kw).astype(jnp.float32) * (E ** -0.5)
    i = jnp.arange(blk)[:, None]
    j = jnp.arange(2 * blk)[None, :]
    cdist = i + blk - j
    bias = jnp.transpose(bias_tab[t5_bucket(cdist * dil)], (2, 0, 1)).astype(jnp.float32)
    band = (cdist >= 0) & (cdist <= span)
    has_prev = (jnp.arange(nb)[:, None, None] > 0) | (j[None] >= blk)
    valid = band[None] & has_prev
    logits = jnp.where(valid[None, None, :, None], logits + bias[None, None, None], NEG_INF)
    m = jnp.max(logits, axis=-1, keepdims=True)
    p = jnp.exp(logits - m)
    s = jnp.sum(p, axis=-1)
    o = jnp.einsum("bdnhqk,bdnkhe->bdnqhe", p.astype(v.dtype), vw).astype(jnp.float32)
    o = o / jnp.swapaxes(s, 3, 4)[..., None]
    lse = jnp.swapaxes(m[..., 0] + jnp.log(s), 3, 4)

    def from_classes(t):
        rest = t.shape[4:]
        t = jnp.swapaxes(t.reshape((B, dil, L) + rest), 1, 2)
        return t.reshape((B, Sp) + rest)[:, :S]

    return from_classes(o), from_classes(lse)


def dilated_attention(q, k, v, bias_tab):
    outs, lses = [], []
    for window, dil in DIL_PATTERNS:
        o, lse = dilated_branch(q, k, v, window, dil, bias_tab)
        outs.append(o)
        lses.append(lse)
    w = jax.nn.softmax(jnp.stack(lses), axis=0)
    return jnp.einsum("pbsh,pbshe->bshe", w, jnp.stack(outs)).astype(q.dtype)


def mixer_sublayer(u, w_in, lq1, lk1, lq2, lk2, lam_init, g_diff, g_dil, w_o, rel_bias):
    B, S, _ = u.shape
    qa, ka, va, qb, kb, vb = jnp.split(u @ w_in, IN_SPLITS, axis=-1)
    f32 = jnp.float32
    lam = (jnp.exp(jnp.sum(lq1.astype(f32) * lk1.astype(f32)))
           - jnp.exp(jnp.sum(lq2.astype(f32) * lk2.astype(f32))) + lam_init)
    ya = diff_attention(qa.reshape(B, S, N_DIFF_HEADS, 2, HEAD_DIM),
                        ka.reshape(B, S, N_DIFF_HEADS, 2, HEAD_DIM),
                        va.reshape(B, S, N_DIFF_HEADS, 2 * HEAD_DIM),
                        lam, rel_bias[:, :N_DIFF_HEADS])
    ya = rms_norm(ya, g_diff) * (1.0 - lam_init)
    yb = dilated_attention(qb.reshape(B, S, N_DIL_HEADS, HEAD_DIM),
                           kb.reshape(B, S, N_DIL_HEADS, HEAD_DIM),
                           vb.reshape(B, S, N_DIL_HEADS, HEAD_DIM),
                           rel_bias[:, N_DIFF_HEADS:])
    yb = rms_norm(yb, g_dil)
    y = jnp.concatenate([ya.reshape(B, S, DIFF_WIDTH), yb.reshape(B, S, DIL_WIDTH)], axis=-1)
    return y @ w_o


def conv_glu(u, w_up, conv_w, conv_b, w_down):
    g, v = jnp.split(u @ w_up, 2, axis=-1)
    g = lax.conv_general_dilated(g, conv_w[:, None, :], window_strides=(1,),
                                 padding=[(CONV_WIDTH - 1, 0)],
                                 dimension_numbers=("NWC", "WIO", "NWC"),
                                 feature_group_count=D_FF) + conv_b
    return (jax.nn.silu(g) * v) @ w_down


def setup_inputs(seed: int = 0) -> dict:
    key = jax.random.key(seed)
    ks = jax.random.split(key, 24)
    D, F = D_MODEL, D_FF

    def nrm(k, shape, s):
        return jax.random.normal(k, shape, jnp.float32) * s

    col_scale = jnp.concatenate([
        jnp.ones((2 * DIFF_WIDTH,), jnp.float32),
        jnp.full((DIFF_WIDTH,), DEEPNORM_BETA, jnp.float32),
        jnp.ones((2 * DIL_WIDTH,), jnp.float32),
        jnp.full((DIL_WIDTH,), DEEPNORM_BETA, jnp.float32)])
    return {
        "x": nrm(ks[0], (BATCH, SEQ, D), 1.0),
        "c": nrm(ks[1], (BATCH, D), 1.0),
        "w_ada": nrm(ks[2], (DEPTH, D, 6 * D), D ** -0.5),
        "b_ada": nrm(ks[3], (DEPTH, 6 * D), 0.01),
        "w_in": nrm(ks[4], (DEPTH, D, IN_WIDTH), D ** -0.5) * col_scale,
        "lambda_q1": nrm(ks[5], (DEPTH, HEAD_DIM), 0.1),
        "lambda_k1": nrm(ks[6], (DEPTH, HEAD_DIM), 0.1),
        "lambda_q2": nrm(ks[7], (DEPTH, HEAD_DIM), 0.1),
        "lambda_k2": nrm(ks[8], (DEPTH, HEAD_DIM), 0.1),
        "g_diff": 1.0 + nrm(ks[9], (DEPTH, 2 * HEAD_DIM), 0.02),
        "g_dil": 1.0 + nrm(ks[10], (DEPTH, HEAD_DIM), 0.02),
        "w_o": nrm(ks[11], (DEPTH, MIX_WIDTH, D), MIX_WIDTH ** -0.5 * DEEPNORM_BETA),
        "ln1_g": 1.0 + nrm(ks[12], (DEPTH, D), 0.02),
        "ln1_b": nrm(ks[13], (DEPTH, D), 0.02),
        "w_up": nrm(ks[14], (DEPTH, D, 2 * F), D ** -0.5),
        "conv_w": nrm(ks[15], (DEPTH, CONV_WIDTH, F), CONV_WIDTH ** -0.5),
        "conv_b": nrm(ks[16], (DEPTH, F), 0.02),
        "w_down": nrm(ks[17], (DEPTH, F, D), F ** -0.5 * DEEPNORM_BETA),
        "ln2_g": 1.0 + nrm(ks[18], (DEPTH, D), 0.02),
        "ln2_b": nrm(ks[19], (DEPTH, D), 0.02),
        "rel_bias": nrm(ks[20], (NUM_BUCKETS, N_DIFF_HEADS + N_DIL_HEADS), 0.5),
    }


def reference(x, c, w_ada, b_ada, w_in, lambda_q1, lambda_k1, lambda_q2, lambda_k2,
              g_diff, g_dil, w_o, ln1_g, ln1_b, w_up, conv_w, conv_b, w_down,
              ln2_g, ln2_b, rel_bias):
    for l in range(DEPTH):
        lam_init = 0.8 - 0.6 * math.exp(-0.3 * l)
        mod = jax.nn.silu(c) @ w_ada[l] + b_ada[l]
        sh_a, sc_a, gt_a, sh_f, sc_f, gt_f = [m[:, None, :] for m in jnp.split(mod, 6, axis=-1)]
        u = ln_plain(x) * (1.0 + sc_a) + sh_a
        y = mixer_sublayer(u, w_in[l], lambda_q1[l], lambda_k1[l], lambda_q2[l], lambda_k2[l],
                           lam_init, g_diff[l], g_dil[l], w_o[l], rel_bias)
        x = layer_norm(DEEPNORM_ALPHA * x + gt_a * y, ln1_g[l], ln1_b[l])
        u = ln_plain(x) * (1.0 + sc_f) + sh_f
        y = conv_glu(u, w_up[l], conv_w[l], conv_b[l], w_down[l])
        x = layer_norm(DEEPNORM_ALPHA * x + gt_f * y, ln2_g[l], ln2_b[l])
    return x
```

```python
import math
from contextlib import ExitStack

import numpy as np
import concourse.bass as bass
import concourse.mybir as mybir
from concourse.bass_utils import run_bass_kernel_spmd

F32 = mybir.dt.float32
BF16 = mybir.dt.bfloat16
AF = mybir.ActivationFunctionType
ALU = mybir.AluOpType
AX = mybir.AxisListType

ENGS = ["sync", "scalar", "vector", "gpsimd", "tensor"]
LN_EPS = 1e-5
NEG = -30000.0
DMAXT = 2559
TL = 3072
TW = 2944
DCL = 2175
SAME_SYNC = True


class Cfg:
    def __init__(self, S=4096, D=4096, depth=2, TBA=1024, NWB=256):
        self.S, self.D, self.depth = S, D, depth
        self.DW = D // 2
        self.LW = D - self.DW
        self.NDH = self.DW // 256
        self.NLH = self.LW // 128
        self.F = 256 * ((8 * D // 3 + 255) // 256)
        self.KC = D // 128
        self.FC = self.F // 128
        self.TBA = min(TBA, S)
        self.NWB = NWB
        self.INW = 3 * self.DW + 3 * self.LW
        self.alpha = (2 * depth) ** 0.25
        self.NM = 2 * self.NDH + self.NLH
        self.ARENA = 150 * 1024
        self.stop_after = None
        self.debug = None


class Buf:
    __slots__ = ("name", "w", "r")

    def __init__(self, name):
        self.name = name
        self.w = None
        self.r = {}


class Prog:
    def __init__(self, nc, stack):
        self.nc, self.stack = nc, stack
        self.ops = {e: [] for e in ENGS}
        self.sems, self.cnt = {}, {}
        self.waited = {e: {} for e in ENGS}
        self.live = {}
        self.nsem = 0
        self.epoch = 0

    def _sem(self, key):
        if key not in self.sems:
            self.sems[key] = self.stack.enter_context(self.nc.semaphore("s%d" % self.nsem))
            self.nsem += 1
            self.cnt[key] = 0
        return self.sems[key]

    def op(self, eng, fn, reads=(), writes=(), dma=None):
        waits = {}

        def need(ev):
            if ev is None:
                return
            k, v = ev
            if dma is None and k[0] == "eng" and k[1] == eng and (eng == "tensor" or not SAME_SYNC):
                return
            if v > waits.get(k, 0):
                waits[k] = v

        for b in reads:
            need(b.w)
        for b in writes:
            need(b.w)
            for k, v in b.r.items():
                need((k, v))
        wl = []
        for k, v in waits.items():
            if self.waited[eng].get(k, 0) >= v:
                continue
            self.waited[eng][k] = v
            wl.append((self._sem(k), v))
        key = ("dma", dma.name) if dma is not None else ("eng", eng, self.epoch)
        sem = self._sem(key)
        inc = 16 if dma is not None else 1
        self.cnt[key] += inc
        ev = (key, self.cnt[key])
        self.live[key] = self.cnt[key]
        self.ops[eng].append((fn, wl, sem, inc))
        for b in writes:
            b.w = ev
            b.r = {}
        for b in reads:
            if b.r.get(key, 0) < ev[1]:
                b.r[key] = ev[1]
        return ev

    def barrier(self):
        for e in ENGS:
            wl = []
            for k, v in self.live.items():
                if self.waited[e].get(k, 0) >= v:
                    continue
                self.waited[e][k] = v
                wl.append((self._sem(k), v))
            if wl:
                self.ops[e].append((None, wl, None, 0))
        self.live = {}
        for k in list(self.sems.keys()):
            if self.cnt[k] > 24000:
                if k[0] == "eng":
                    pass
                else:
                    del self.sems[k]
        if any(self.cnt.get(("eng", e, self.epoch), 0) > 24000 for e in ENGS):
            self.epoch += 1

    def replay(self, block):
        def mk(ename):
            lst = self.ops[ename]

            def body(e):
                for fn, wl, sem, inc in lst:
                    for s, v in wl:
                        e.wait_ge(s, v)
                    if fn is not None:
                        ins = fn(e)
                        ins.then_inc(sem, inc)
            return body

        block.sync(mk("sync"))
        block.scalar(mk("scalar"))
        block.vector(mk("vector"))
        block.gpsimd(mk("gpsimd"))
        block.tensor(mk("tensor"))


class Ring:
    def __init__(self, items):
        self.items = items
        self.i = 0

    def next(self):
        it = self.items[self.i % len(self.items)]
        self.i += 1
        return it


def t5_bucket_np(d):
    n = np.maximum(d, 0)
    max_exact = 16
    nf = np.maximum(n, max_exact).astype(np.float32)
    large = max_exact + (np.log(nf / max_exact) / math.log(2048 / max_exact) * (32 - max_exact)).astype(np.int32)
    large = np.minimum(large, 31)
    return np.where(n < max_exact, n, large)


def build(C):
    nc = bass.Bass("TRN2", target_bir_lowering=False)
    S, D, F, KC, FC, L = C.S, C.D, C.F, C.KC, C.FC, C.depth
    NDH, NLH, NM, DW, LW, INW = C.NDH, C.NLH, C.NM, C.DW, C.LW, C.INW
    TBA, NWB = C.TBA, C.NWB
    NT5 = S // 512
    QA0, KA0, VA0, QB0, KB0, VB0 = 0, DW, 2 * DW, 3 * DW, 3 * DW + LW, 3 * DW + 2 * LW

    def din(name, shape):
        return nc.dram_tensor(name, list(shape), F32, kind="ExternalInput").ap()

    x_in = din("x", [S, D])
    c_fm = din("c_fm", [128, KC])
    w_ada = din("w_ada", [L, D, 6 * D])
    b_ada_fm = din("b_ada_fm", [L, 128, 6 * KC])
    w_in = din("w_in", [L, D, INW])
    lamv = din("lamv", [L, 4 * 128])
    g_diff = din("g_diff", [L, 256])
    g_dil = din("g_dil", [L, 128])
    w_o = din("w_o", [L, D, D])
    vec_fm = din("vec_fm", [L, 128, 4 * KC])
    w_up = din("w_up", [L, D, 2 * F])
    conv_fm = din("conv_fm", [L, 128, 4 * FC])
    w_down = din("w_down", [L, F, D])
    tabT = din("tabT", [NDH + NLH, TL])
    cmask = din("cmask", [2, TL])
    ident_in = din("ident", [128, 128])
    jmat_in = din("jmat", [128, 128])
    out = nc.dram_tensor("out", [S, D], F32, kind="ExternalOutput").ap()

    def dscr(name, shape, dt):
        return nc.dram_tensor(name, list(shape), dt).ap()

    xT0 = dscr("xT0", [D, S], F32)
    zs = [[dscr("r%d_%d" % (l, i), [D, S], F32) for i in range(4)] for l in range(L)]
    qrT = dscr("qrT", [NM * 128, S], BF16)
    kTd = dscr("kTd", [NM * 128, S], BF16)
    Vd = dscr("Vd", [S, D], BF16)
    yT = dscr("yT", [D, S], BF16)

    PER = NWB // 128
    NUPT = 2 * ((FC + PER - 1) // PER)
    KSPLIT = 2 if FC % 2 == 0 else 1
    KPER = FC // KSPLIT
    NWD = 256 if KPER * 256 * 2 <= KC * NWB * 2 else 128
    wc_in = [dscr("wc_in%d" % l, [INW // NWB, 128, KC * NWB], BF16) for l in range(L)]
    wc_o = [dscr("wc_o%d" % l, [D // NWB, 128, KC * NWB], BF16) for l in range(L)]
    wc_up = [dscr("wc_up%d" % l, [NUPT, 128, KC * NWB], BF16) for l in range(L)]
    wc_dn = [dscr("wc_dn%d" % l, [(D // NWD) * KSPLIT, 128, KPER * NWD], BF16) for l in range(L)]

    stack = ExitStack()
    with stack:
        p = Prog(nc, stack)
        sb = lambda name, shape, dt: stack.enter_context(nc.sbuf_tensor(name, list(shape), dt))
        ident_b = sb("ident_b", [128, 128], BF16)
        jmat_b = sb("jmat_b", [128, 128], BF16)
        ident_f = sb("ident_f", [128, 128], F32)
        ones_f = sb("ones_f", [128, 128], F32)
        modt = sb("modt", [128, 6 * KC], F32)
        badat = sb("badat", [128, 6 * KC], F32)
        vect = sb("vect", [128, 4 * KC], F32)
        convt = sb("convt", [128, 4 * FC], F32)
        halo = sb("halo", [128, FC, 2], F32)
        cfm_t = sb("cfm_t", [128, KC], F32)
        csil = sb("csil", [128, KC], BF16)
        lamb = sb("lamb", [128, 4 * 128], F32)
        lamp = sb("lamp", [128, 2 * 128], F32)
        lams = sb("lams", [128, 8], F32)
        gdf = sb("gdf", [128, 256], F32)
        gdl = sb("gdl", [128, 128], F32)
        NXR = 4
        xr = [sb("xr%d" % i, [128, 512], F32) for i in range(NXR)]
        stf = [sb("stf%d" % i, [128, 512], F32) for i in range(3)]
        stb = [sb("stb%d" % i, [128, 512], BF16) for i in range(8)]
        ptr = [sb("ptr%d" % i, [128, 512], BF16) for i in range(3)]
        sm = [sb("sm%d" % i, [128, 8], F32) for i in range(8)]
        arena2 = sb("arena2", [128, 4096], F32)

        def a2(off_bytes, nbytes, dt):
            a = arena2[:, off_bytes // 4:(off_bytes + nbytes) // 4]
            return a.bitcast(BF16) if dt == BF16 else a
        sqr = [a2(i * 2048, 2048, F32) for i in range(2)]
        t1r = [a2(4096 + i * 2048, 2048, F32) for i in range(2)]
        mean_t, rstd_t, mean2_t, rstd2_t = [a2(8192 + i * 2048, 2048, F32) for i in range(4)]
        gext = [a2(i * 2056, 2056, F32) for i in range(2)]
        ca = [a2(4112 + i * 2048, 2048, F32) for i in range(2)]
        cb_ = [a2(8208 + i * 2048, 2048, F32) for i in range(2)]
        y1t = [a2(i * 1024, 1024, F32) for i in range(4)]
        y2t = [a2(4096 + i * 1024, 1024, F32) for i in range(2)]
        ysq = [a2(6144 + i * 1024, 1024, F32) for i in range(2)]
        ynb = [a2(8192 + i * 512, 512, BF16) for i in range(4)]
        ARW = C.ARENA // 4
        arena = sb("arena", [128, ARW], F32)
        banks = [stack.enter_context(nc.psum_tensor("bank%d" % i, [128, 512], F32)) for i in range(8)]

        B = {}

        def buf(name):
            if name not in B:
                B[name] = Buf(name)
            return B[name]

        def bufs(prefix, n):
            return [buf("%s%d" % (prefix, i)) for i in range(n)]

        def ring_of(tiles, prefix):
            return Ring(list(zip(tiles, bufs(prefix, len(tiles)))))

        XR = ring_of(xr, "xr")
        SQR = ring_of(sqr, "sqr")
        T1R = ring_of(t1r, "t1r")
        STF = ring_of(stf, "stf")
        STB = ring_of(stb, "stb")
        PTR = ring_of(ptr, "ptr")
        GEXT = ring_of(gext, "gext")
        CA = ring_of(ca, "ca")
        CB = ring_of(cb_, "cb")
        SM = ring_of(sm, "sm")
        Y1 = ring_of(y1t, "y1t")
        Y2 = ring_of(y2t, "y2t")
        YSQ = ring_of(ysq, "ysq")
        YNB = ring_of(ynb, "ynb")
        BK = [(banks[i], buf("bank%d" % i)) for i in range(8)]
        b_const = buf("const")
        b_mod = buf("mod")
        b_vec = buf("vec")
        b_halo = buf("halo")
        b_lam = buf("lam")
        b_g = buf("g")
        b_mean, b_rstd = buf("mean"), buf("rstd")
        b_mean2, b_rstd2 = buf("mean2"), buf("rstd2")

        def aview(off_bytes, nbytes, dt, pattern=None, **kw):
            a = arena[:, off_bytes // 4:(off_bytes + nbytes) // 4]
            if dt == BF16:
                a = a.bitcast(BF16)
            if pattern:
                a = a.rearrange(pattern, **kw)
            return a

        evict_flip = [0]

        def evict(out_ap, in_ap, reads, writes, scale=None):
            evict_flip[0] ^= 1
            if evict_flip[0]:
                if scale is None:
                    p.op("scalar", lambda e: e.activation(out=out_ap, in_=in_ap, func=AF.Copy), reads, writes)
                else:
                    p.op("scalar", lambda e: e.activation(out=out_ap, in_=in_ap, func=AF.Copy, scale=scale), reads, writes)
            else:
                if scale is None:
                    p.op("vector", lambda e: e.tensor_copy(out=out_ap, in_=in_ap), reads, writes)
                else:
                    p.op("vector", lambda e: e.tensor_scalar(out=out_ap, in0=in_ap, scalar1=scale, scalar2=None, op0=ALU.mult), reads, writes)

        def mm_group(bank, pairs, reads, writes, first=True, last=True):
            n = len(pairs)

            def fn(e):
                ins = None
                for i, (o, l_, r_) in enumerate(pairs):
                    ins = e.matmul(o, lhsT=l_, rhs=r_, start=(first and i == 0), stop=(last and i == n - 1))
                return ins
            p.op("tensor", fn, reads, writes)

        p.op("gpsimd", lambda e: e.dma_start(out=ident_b[:], in_=ident_in[:, :]), [], [b_const], dma=buf("d_identb"))
        p.op("gpsimd", lambda e: e.dma_start(out=jmat_b[:], in_=jmat_in[:, :]), [], [b_const], dma=buf("d_jb"))
        p.op("sync", lambda e: e.dma_start(out=ident_f[:], in_=ident_in[:, :]), [], [b_const], dma=buf("d_identf"))
        p.op("sync", lambda e: e.dma_start(out=cfm_t[:], in_=c_fm[:, :]), [], [b_const], dma=buf("d_cfm"))
        p.op("vector", lambda e: e.memset(ones_f[:], 1.0), [], [b_const])
        p.op("scalar", lambda e: e.activation(out=csil[:], in_=cfm_t[:], func=AF.Silu), [b_const], [b_const])
        p.barrier()

        def conv_w_in(l):
            b = buf("wc_in%d" % l)
            for wt in range(INW // NWB):
                c0 = wt * NWB
                p.op("gpsimd", lambda e, wt=wt, c0=c0: e.dma_start(
                    out=wc_in[l][wt].rearrange("p (k n) -> p k n", n=NWB),
                    in_=w_in[l, :, c0:c0 + NWB].rearrange("(k p) n -> p k n", p=128)), [], [b], dma=b)

        def conv_w_rest(l):
            b = buf("wc_o%d" % l)
            for wt in range(D // NWB):
                c0 = wt * NWB
                p.op("gpsimd", lambda e, wt=wt, c0=c0: e.dma_start(
                    out=wc_o[l][wt].rearrange("p (k n) -> p k n", n=NWB),
                    in_=w_o[l, :, c0:c0 + NWB].rearrange("(k p) n -> p k n", p=128)), [], [b], dma=b)
            b = buf("wc_up%d" % l)
            for ti in range(NUPT):
                fg, half = (ti // 2) * PER, ti % 2
                nch = min(PER, FC - fg)
                c0 = half * F + fg * 128
                p.op("gpsimd", lambda e, ti=ti, c0=c0, nch=nch: e.dma_start(
                    out=wc_up[l][ti].rearrange("p (k n) -> p k n", n=NWB)[:, :, 0:nch * 128],
                    in_=w_up[l, :, c0:c0 + nch * 128].rearrange("(k p) n -> p k n", p=128)), [], [b], dma=b)
            b = buf("wc_dn%d" % l)
            for ct in range(D // NWD):
                for ks in range(KSPLIT):
                    k_lo = ks * KPER * 128
                    p.op("gpsimd", lambda e, ct=ct, ks=ks, k_lo=k_lo: e.dma_start(
                        out=wc_dn[l][ct * KSPLIT + ks].rearrange("p (k n) -> p k n", n=NWD),
                        in_=w_down[l, k_lo:k_lo + KPER * 128, ct * NWD:(ct + 1) * NWD].rearrange("(k p) n -> p k n", p=128)),
                        [], [b], dma=b)

        def phase_transpose_in():
            xt_tiles = [aview(i * D * 4, D * 4, F32) for i in range(4)]
            xt_b = bufs("xt", 4)
            bk = Ring(BK[0:4])
            for g in range(S // 512):
                for i in range(4):
                    r0 = g * 512 + i * 128
                    p.op("sync", lambda e, i=i, r0=r0: e.dma_start(out=xt_tiles[i], in_=x_in[r0:r0 + 128, :]),
                         [], [xt_b[i]], dma=xt_b[i])
                for kc in range(KC):
                    bank, bb = bk.next()

                    def fn(e, kc=kc, bank=bank):
                        ins = None
                        for i in range(4):
                            ins = e.transpose(bank[:, i * 128:(i + 1) * 128], xt_tiles[i][:, kc * 128:(kc + 1) * 128], ident_f[:])
                        return ins
                    p.op("tensor", fn, xt_b + [b_const], [bb])
                    st, sbf = STF.next()
                    evict(st[:], bank[:], [bb], [sbf])
                    p.op("sync", lambda e, st=st, kc=kc, g=g: e.dma_start(out=xT0[kc * 128:(kc + 1) * 128, g * 512:(g + 1) * 512], in_=st[:]),
                         [sbf], [], dma=sbf)
            p.barrier()

        def ln_stats_begin():
            return BK[4], BK[5]

        def stats_accum(xs, xsb, kc, bs, bq):
            sq, sqb = SQR.next()
            p.op("scalar", lambda e: e.activation(out=sq, in_=xs[:], func=AF.Square), [xsb], [sqb])
            p.op("tensor", lambda e: e.matmul(bs[0][:], lhsT=ones_f[:], rhs=xs[:], start=(kc == 0), stop=(kc == KC - 1)),
                 [xsb, b_const], [bs[1]])
            p.op("tensor", lambda e: e.matmul(bq[0][:], lhsT=ones_f[:], rhs=sq, start=(kc == 0), stop=(kc == KC - 1)),
                 [sqb, b_const], [bq[1]])

        def stats_finish(bs, bq, eps, mean_ap, rstd_ap, bm, br):
            msq, b_msq = SQR.next()
            var, b_var = SQR.next()
            p.op("scalar", lambda e: e.activation(out=mean_ap, in_=bs[0][:], func=AF.Copy, scale=1.0 / D), [bs[1]], [bm])
            p.op("vector", lambda e: e.tensor_tensor(out=msq, in0=mean_ap, in1=mean_ap, op=ALU.mult), [bm], [b_msq])
            p.op("vector", lambda e: e.scalar_tensor_tensor(out=var, in0=bq[0][:], scalar=1.0 / D, in1=msq,
                                                            op0=ALU.mult, op1=ALU.subtract), [bq[1], b_msq], [b_var])
            p.op("vector", lambda e: e.tensor_scalar(out=var, in0=var, scalar1=float(eps), scalar2=None, op0=ALU.add),
                 [b_var], [b_var])
            p.op("scalar", lambda e: e.activation(out=var, in_=var, func=AF.Sqrt), [b_var], [b_var])
            p.op("vector", lambda e: e.reciprocal(out=rstd_ap, in_=var), [b_var], [br])

        def ln_stats(src, t0, eps, mean_ap, rstd_ap, bm, br):
            bs, bq = ln_stats_begin()
            for kc in range(KC):
                xs, xsb = XR.next()
                p.op("sync", lambda e, xs=xs, kc=kc: e.dma_start(out=xs[:], in_=src[kc * 128:(kc + 1) * 128, t0:t0 + 512]),
                     [], [xsb], dma=xsb)
                stats_accum(xs, xsb, kc, bs, bq)
            stats_finish(bs, bq, eps, mean_ap, rstd_ap, bm, br)

        def ln_apply(src, t0, mean_ap, rstd_ap, bm, br, sc_col, bi_col, sbuf_for, dst_dram=None, dst_sb=None, dst_sb_buf=None,
                     stats2=None, src_bufs=None, dst_bufs=None):
            for kc in range(KC):
                xs, xsb = XR.next()
                rd = [src_bufs[kc]] if src_bufs is not None else []
                p.op("sync", lambda e, xs=xs, kc=kc: e.dma_start(out=xs[:], in_=src[kc * 128:(kc + 1) * 128, t0:t0 + 512]),
                     rd, [xsb], dma=xsb)
                t1, t1b = T1R.next()
                p.op("vector", lambda e, xs=xs, t1=t1: e.tensor_tensor(out=t1, in0=xs[:], in1=mean_ap, op=ALU.subtract),
                     [xsb, bm], [t1b])
                t2, t2b = t1, t1b
                p.op("vector", lambda e, t1=t1: e.tensor_tensor(out=t1, in0=t1, in1=rstd_ap, op=ALU.mult),
                     [t1b, br], [t1b])
                sc_ap = sbuf_for[0][:, sc_col + kc:sc_col + kc + 1]
                bi_ap = sbuf_for[0][:, bi_col + kc:bi_col + kc + 1]
                if dst_dram is not None:
                    st, sbf = STF.next()
                    p.op("scalar", lambda e, st=st, t2=t2, sc_ap=sc_ap, bi_ap=bi_ap: e.activation(
                        out=st[:], in_=t2, func=AF.Identity, scale=sc_ap, bias=bi_ap), [t2b, sbuf_for[1]], [sbf])
                    if stats2 is not None:
                        stats_accum(st, sbf, kc, stats2[0], stats2[1])
                    wr = [dst_bufs[kc]] if dst_bufs is not None else []
                    p.op("sync", lambda e, st=st, kc=kc: e.dma_start(out=dst_dram[kc * 128:(kc + 1) * 128, t0:t0 + 512], in_=st[:]),
                         [sbf], wr, dma=sbf)
                else:
                    o_ap = dst_sb(kc)
                    p.op("scalar", lambda e, o_ap=o_ap, t2=t2, sc_ap=sc_ap, bi_ap=bi_ap: e.activation(
                        out=o_ap, in_=t2, func=AF.Identity, scale=sc_ap, bias=bi_ap), [t2b, sbuf_for[1]], [dst_sb_buf])

        def phase_params(l):
            lam_init = 0.8 - 0.6 * math.exp(-0.3 * l)
            p.op("sync", lambda e: e.dma_start(out=badat[:], in_=b_ada_fm[l]), [], [b_mod], dma=buf("d_bada"))
            p.op("sync", lambda e: e.dma_start(out=vect[:], in_=vec_fm[l]), [], [b_vec], dma=buf("d_vec"))
            p.op("sync", lambda e: e.dma_start(out=convt[:], in_=conv_fm[l]), [], [b_vec], dma=buf("d_conv"))
            src = bass.AP(tensor=lamv.tensor, offset=l * 512, ap=[[0, 128], [1, 512]])
            p.op("sync", lambda e: e.dma_start(out=lamb[:], in_=src), [], [b_lam], dma=buf("d_lam"))
            srcg = bass.AP(tensor=g_diff.tensor, offset=l * 256, ap=[[0, 128], [1, 256]])
            p.op("sync", lambda e: e.dma_start(out=gdf[:], in_=srcg), [], [b_g], dma=buf("d_gdf"))
            srcl = bass.AP(tensor=g_dil.tensor, offset=l * 128, ap=[[0, 128], [1, 128]])
            p.op("sync", lambda e: e.dma_start(out=gdl[:], in_=srcl), [], [b_g], dma=buf("d_gdl"))
            p.op("vector", lambda e: e.memset(halo[:], 0.0), [], [b_halo])
            p.op("vector", lambda e: e.tensor_tensor(out=lamp[:, 0:128], in0=lamb[:, 0:128], in1=lamb[:, 128:256], op=ALU.mult), [b_lam], [b_lam])
            p.op("vector", lambda e: e.tensor_tensor(out=lamp[:, 128:256], in0=lamb[:, 256:384], in1=lamb[:, 384:512], op=ALU.mult), [b_lam], [b_lam])
            p.op("vector", lambda e: e.reduce_sum(out=lams[:, 0:1], in_=lamp[:, 0:128], axis=AX.X), [b_lam], [b_lam])
            p.op("vector", lambda e: e.reduce_sum(out=lams[:, 1:2], in_=lamp[:, 128:256], axis=AX.X), [b_lam], [b_lam])
            p.op("scalar", lambda e: e.activation(out=lams[:, 2:4], in_=lams[:, 0:2], func=AF.Exp), [b_lam], [b_lam])
            p.op("vector", lambda e: e.tensor_tensor(out=lams[:, 4:5], in0=lams[:, 2:3], in1=lams[:, 3:4], op=ALU.subtract), [b_lam], [b_lam])
            p.op("vector", lambda e: e.tensor_scalar(out=lams[:, 5:6], in0=lams[:, 4:5], scalar1=float(lam_init), scalar2=-1.0,
                                                     op0=ALU.add, op1=ALU.mult), [b_lam], [b_lam])
            p.op("vector", lambda e: e.tensor_scalar(out=gdf[:], in0=gdf[:], scalar1=float(1.0 - lam_init), scalar2=None, op0=ALU.mult),
                 [b_g], [b_g])
            NWM = 512
            wslot = [aview(i * KC * NWM * 2, KC * NWM * 2, BF16, "p (k n) -> p k n", n=NWM) for i in range(2)]
            wb = bufs("wsl", 2)
            bank, bb = BK[0]
            ntile = 6 * D // NWM
            for t in range(ntile):
                ws, wsb = wslot[t % 2], wb[t % 2]
                p.op("gpsimd", lambda e, ws=ws, t=t: e.dma_start(
                    out=ws, in_=w_ada[l, :, t * NWM:(t + 1) * NWM].rearrange("(k p) n -> p k n", p=128)), [], [wsb], dma=wsb)
                for j in range(NWM // 128):
                    col = t * (NWM // 128) + j
                    pairs = [(bank[:, col:col + 1], ws[:, kc, j * 128:(j + 1) * 128], csil[:, kc:kc + 1]) for kc in range(KC)]
                    mm_group(bank, pairs, [wsb, b_const], [bb])
            p.op("vector", lambda e: e.tensor_tensor(out=modt[:], in0=bank[:, 0:6 * KC], in1=badat[:], op=ALU.add), [bb, b_mod], [b_mod])
            for base in (KC, 4 * KC):
                p.op("vector", lambda e, base=base: e.tensor_scalar(out=modt[:, base:base + KC], in0=modt[:, base:base + KC],
                                                                    scalar1=1.0, scalar2=None, op0=ALU.add), [b_mod], [b_mod])
            for base in (2 * KC, 5 * KC):
                p.op("vector", lambda e, base=base: e.tensor_scalar(out=modt[:, base:base + KC], in0=modt[:, base:base + KC],
                                                                    scalar1=float(1.0 / C.alpha), scalar2=None, op0=ALU.mult), [b_mod], [b_mod])
            p.barrier()

        def phase_qkv(l, cur):
            NTB = S // TBA
            uT = aview(0, KC * TBA * 2, BF16, "p (k t) -> p k t", t=TBA)
            b_uT = buf("uT")
            WOFF = KC * TBA * 2
            wslot = [aview(WOFF + i * KC * NWB * 2, KC * NWB * 2, BF16, "p (k n) -> p k n", n=NWB) for i in range(2)]
            wb = bufs("wsl", 2)
            bk = Ring(BK[0:4])
            bk2 = Ring(BK[6:8])
            nwt = INW // NWB
            for tb in range(NTB):
                T0 = tb * TBA
                for sbk in range(TBA // 512):
                    t0 = T0 + sbk * 512
                    ln_stats(cur, t0, LN_EPS, mean_t, rstd_t, b_mean, b_rstd)
                    ln_apply(cur, t0, mean_t, rstd_t, b_mean, b_rstd, KC, 0, (modt, b_mod),
                             dst_sb=lambda kc, sbk=sbk: uT[:, kc, sbk * 512:(sbk + 1) * 512], dst_sb_buf=b_uT)
                for wt in range(nwt):
                    c0 = wt * NWB
                    ws, wsb = wslot[wt % 2], wb[wt % 2]
                    p.op("gpsimd", lambda e, ws=ws, wt=wt: e.dma_start(
                        out=ws.rearrange("p k n -> p (k n)"), in_=wc_in[l][wt]), [buf("wc_in%d" % l)], [wsb], dma=wsb)
                    if c0 < KA0:
                        kind, mbase = "q", (c0 - QA0) // 128
                    elif c0 < VA0:
                        kind, mbase = "k", (c0 - KA0) // 128
                    elif c0 < QB0:
                        kind, vcol = "v", c0 - VA0
                    elif c0 < KB0:
                        kind, mbase = "q", 2 * NDH + (c0 - QB0) // 128
                    elif c0 < VB0:
                        kind, mbase = "k", 2 * NDH + (c0 - KB0) // 128
                    else:
                        kind, vcol = "v", DW + (c0 - VB0)
                    for g5 in range(TBA // 512):
                        toks = []
                        for i in range(4):
                            tl = g5 * 512 + i * 128
                            bank, bb = bk.next()
                            pairs = [(bank[:, 0:NWB], uT[:, kc, tl:tl + 128], ws[:, kc, :]) for kc in range(KC)]
                            mm_group(bank, pairs, [b_uT, wsb], [bb])
                            st, sbf = STB.next()
                            evict(st[:, 0:NWB], bank[:, 0:NWB], [bb], [sbf])
                            if kind == "v":
                                r0 = T0 + tl
                                p.op("sync", lambda e, st=st, r0=r0, vcol=vcol: e.dma_start(
                                    out=Vd[r0:r0 + 128, vcol:vcol + NWB], in_=st[:, 0:NWB]), [sbf], [], dma=sbf)
                            else:
                                toks.append((st, sbf))
                        if kind == "v":
                            continue
                        for m in range(NWB // 128):
                            bank2, bb2 = bk2.next()

                            def fn(e, m=m, bank2=bank2, toks=toks, kind=kind):
                                ins = None
                                for i in range(4):
                                    if kind == "q":
                                        ins = e.matmul(bank2[:, (3 - i) * 128:(4 - i) * 128], lhsT=toks[i][0][:, m * 128:(m + 1) * 128],
                                                       rhs=jmat_b[:], start=True, stop=True)
                                    else:
                                        ins = e.matmul(bank2[:, i * 128:(i + 1) * 128], lhsT=toks[i][0][:, m * 128:(m + 1) * 128],
                                                       rhs=ident_b[:], start=True, stop=True)
                                return ins
                            p.op("tensor", fn, [t[1] for t in toks] + [b_const], [bb2])
                            st2, sbf2 = PTR.next()
                            evict(st2[:], bank2[:], [bb2], [sbf2])
                            dst = qrT if kind == "q" else kTd
                            mi = mbase + m
                            c5 = T0 + g5 * 512
                            p.op("sync", lambda e, st2=st2, dst=dst, mi=mi, c5=c5: e.dma_start(
                                out=dst[mi * 128:(mi + 1) * 128, c5:c5 + 512], in_=st2[:]), [sbf2], [], dma=sbf2)
                p.barrier()

        def phase_attn(l):
            NB = S // 128
            scale = 128 ** -0.5
            off = 0
            cm_t = []
            for i in range(2):
                cm_t.append(aview(off, TW * 4, F32))
                off += TW * 4
            sets = []
            for i in range(2):
                d = {}
                d["k"] = [aview(off + j * S * 2, S * 2, BF16) for j in range(2)]
                off += 2 * S * 2
                d["q"] = [aview(off + j * S * 2, S * 2, BF16) for j in range(2)]
                off += 2 * S * 2
                vbytes = NB * 258 * 2
                d["v"] = aview(off, vbytes, BF16, "p (b c) -> p b c", c=258)
                off += vbytes
                d["w"] = aview(off, TW * 4, F32)
                off += TW * 4
                d["bk"] = buf("hk%d" % i)
                d["bq"] = buf("hq%d" % i)
                d["bv"] = buf("hv%d" % i)
                d["bw"] = buf("hw%d" % i)
                sets.append(d)
            assert off <= C.ARENA, off
            tsr = [aview(off + i * 2048, 2048, F32) for i in range(2)]
            off += 4096
            assert off <= C.ARENA, off
            TSR = ring_of(tsr, "tsr")
            b_cm = buf("cm")
            for i in range(2):
                srcm = bass.AP(tensor=cmask.tensor, offset=i * TL, ap=[[1, 128], [1, TW]])
                p.op("sync", lambda e, i=i, srcm=srcm: e.dma_start(out=cm_t[i], in_=srcm), [], [b_cm], dma=buf("d_cm%d" % i))
            sbk = Ring(BK[0:2])
            obk = BK[2:6]
            tbk = Ring(BK[6:8])
            heads = [("diff", h) for h in range(NDH)] + [("dil", h) for h in range(NLH)]
            for hi, (kind, h) in enumerate(heads):
                d = sets[hi % 2]
                nmap = 2 if kind == "diff" else 1
                VW = 256 if kind == "diff" else 128
                m0 = 2 * h if kind == "diff" else 2 * NDH + h
                vc0 = h * 256 if kind == "diff" else DW + h * 128
                yrow0 = vc0
                for j in range(nmap):
                    p.op("sync", lambda e, d=d, j=j, m0=m0: e.dma_start(out=d["k"][j], in_=kTd[(m0 + j) * 128:(m0 + j + 1) * 128, :]),
                         [], [d["bk"]], dma=buf("d_hk%d_%d" % (hi % 2, j)))
                    p.op("scalar", lambda e, d=d, j=j, m0=m0: e.dma_start(out=d["q"][j], in_=qrT[(m0 + j) * 128:(m0 + j + 1) * 128, :]),
                         [], [d["bq"]], dma=buf("d_hq%d_%d" % (hi % 2, j)))
                p.op("sync", lambda e, d=d, vc0=vc0, VW=VW: e.dma_start(
                    out=d["v"][:, :, 0:VW], in_=Vd[:, vc0:vc0 + VW].rearrange("(b p) c -> p b c", p=128)),
                    [], [d["bv"]], dma=buf("d_hv%d" % (hi % 2)))
                p.op("vector", lambda e, d=d, VW=VW: e.memset(d["v"][:, :, VW:VW + 1], 1.0), [d["bv"]], [d["bv"]])
                srcw = bass.AP(tensor=tabT.tensor, offset=hi * TL, ap=[[1, 128], [1, TW]])
                p.op("sync", lambda e, d=d, srcw=srcw: e.dma_start(out=d["w"], in_=srcw), [], [d["bw"]], dma=buf("d_hw%d" % (hi % 2)))
                cmi = 0 if kind == "diff" else 1
                p.op("vector", lambda e, d=d, cmi=cmi: e.tensor_tensor(out=d["w"], in0=d["w"], in1=cm_t[cmi], op=ALU.add),
                     [d["bw"], b_cm], [d["bw"]])
                gt_ap = gdf if kind == "diff" else gdl
                for qt in range(NT5):
                    q0 = qt * 512
                    kb_hi = 4 * qt + 3
                    kb_lo = 0 if kind == "diff" else max(0, 4 * qt - 16)
                    for j in range(nmap):
                        for kb in range(kb_lo, kb_hi + 1):
                            k0 = kb * 128
                            delta = q0 + 511 - k0
                            if kind == "diff":
                                delta = min(delta, DCL)
                            base = DMAXT - delta
                            assert 0 <= base and base + 512 <= TW
                            bank, bb = sbk.next()
                            p.op("tensor", lambda e, bank=bank, d=d, j=j, k0=k0, q0=q0: e.matmul(
                                bank[:], lhsT=d["k"][j][:, k0:k0 + 128], rhs=d["q"][j][:, q0:q0 + 512], start=True, stop=True),
                                [d["bk"], d["bq"]], [bb])
                            ts_, tsb = TSR.next()
                            p.op("vector", lambda e, ts_=ts_, bank=bank, d=d, base=base: e.scalar_tensor_tensor(
                                out=ts_, in0=bank[:], scalar=float(scale), in1=d["w"][:, base:base + 512], op0=ALU.mult, op1=ALU.add),
                                [bb, d["bw"]], [tsb])
                            pt, ptb = PTR.next()
                            p.op("scalar", lambda e, pt=pt, ts_=ts_: e.activation(out=pt[:], in_=ts_, func=AF.Exp), [tsb], [ptb])

                            def fnpv(e, pt=pt, d=d, kb=kb, VW=VW, first=(kb == kb_lo), last=(kb == kb_hi)):
                                ins = None
                                for s in range(4):
                                    ins = e.matmul(obk[s][0][:, 0:VW + 1], lhsT=pt[:, s * 128:(s + 1) * 128], rhs=d["v"][:, kb, 0:VW + 1],
                                                   start=first, stop=last)
                                return ins
                            p.op("tensor", fnpv, [ptb, d["bv"]], [o[1] for o in obk])
                        for s in range(4):
                            ob, obb = obk[s]
                            smt, smb = SM.next()
                            p.op("vector", lambda e, smt=smt, ob=ob, VW=VW: e.reciprocal(out=smt[:, 0:1], in_=ob[:, VW:VW + 1]), [obb], [smb])
                            if kind == "diff" and j == 0:
                                y1, y1b = y1t[s], buf("y1t%d" % s)
                                p.op("scalar", lambda e, y1=y1, ob=ob, smt=smt: e.activation(
                                    out=y1[:, 0:256], in_=ob[:, 0:256], func=AF.Copy, scale=smt[:, 0:1]), [obb, smb], [y1b])
                                continue
                            y2, y2b = Y2.next()
                            if kind == "diff":
                                y1, y1b = y1t[s], buf("y1t%d" % s)
                                p.op("vector", lambda e, smt=smt: e.tensor_tensor(out=smt[:, 1:2], in0=smt[:, 0:1], in1=lams[:, 5:6], op=ALU.mult),
                                     [smb, b_lam], [smb])
                                p.op("vector", lambda e, y2=y2, ob=ob, smt=smt, y1=y1: e.scalar_tensor_tensor(
                                    out=y2[:, 0:256], in0=ob[:, 0:256], scalar=smt[:, 1:2], in1=y1[:, 0:256], op0=ALU.mult, op1=ALU.add),
                                    [obb, smb, y1b], [y2b])
                            else:
                                p.op("scalar", lambda e, y2=y2, ob=ob, smt=smt: e.activation(
                                    out=y2[:, 0:128], in_=ob[:, 0:128], func=AF.Copy, scale=smt[:, 0:1]), [obb, smb], [y2b])
                            yq, yqb = YSQ.next()
                            p.op("vector", lambda e, yq=yq, y2=y2, VW=VW: e.tensor_tensor(out=yq[:, 0:VW], in0=y2[:, 0:VW], in1=y2[:, 0:VW], op=ALU.mult),
                                 [y2b], [yqb])
                            p.op("vector", lambda e, yq=yq, smt=smt, VW=VW: e.reduce_sum(out=smt[:, 2:3], in_=yq[:, 0:VW], axis=AX.X), [yqb, smb], [smb])
                            p.op("vector", lambda e, smt=smt, VW=VW: e.tensor_scalar(out=smt[:, 3:4], in0=smt[:, 2:3], scalar1=float(1.0 / VW),
                                                                                     scalar2=float(LN_EPS), op0=ALU.mult, op1=ALU.add), [smb], [smb])
                            p.op("scalar", lambda e, smt=smt: e.activation(out=smt[:, 4:5], in_=smt[:, 3:4], func=AF.Sqrt), [smb], [smb])
                            p.op("vector", lambda e, smt=smt: e.reciprocal(out=smt[:, 5:6], in_=smt[:, 4:5]), [smb], [smb])
                            yn, ynbuf = YNB.next()
                            p.op("vector", lambda e, yn=yn, y2=y2, smt=smt, VW=VW, gt_ap=gt_ap: e.scalar_tensor_tensor(
                                out=yn[:, 0:VW], in0=y2[:, 0:VW], scalar=smt[:, 5:6], in1=gt_ap[:, 0:VW], op0=ALU.mult, op1=ALU.mult),
                                [y2b, smb, b_g], [ynbuf])
                            for ec in range(VW // 128):
                                if s == 0:
                                    pass
                                tb_, tbb = tbk_sel[ec]
                                p.op("tensor", lambda e, tb_=tb_, yn=yn, ec=ec, s=s: e.matmul(
                                    tb_[:, (3 - s) * 128:(4 - s) * 128], lhsT=yn[:, ec * 128:(ec + 1) * 128], rhs=jmat_b[:], start=True, stop=True),
                                    [ynbuf, b_const], [tbb])
                        if kind == "diff" and j == 0:
                            continue
                        for ec in range(VW // 128):
                            tb_, tbb = tbk_sel[ec]
                            st2, sbf2 = STB.next()
                            evict(st2[:], tb_[:], [tbb], [sbf2])
                            r0 = yrow0 + ec * 128
                            p.op("sync", lambda e, st2=st2, r0=r0, q0=q0: e.dma_start(out=yT[r0:r0 + 128, q0:q0 + 512], in_=st2[:]),
                                 [sbf2], [], dma=sbf2)
            p.barrier()

        tbk_sel = [BK[6], BK[7]]

        def proj_residual(wc, b_wc, nk, act_sb, b_act, ntok, T0, gcol, res_src, dst, wslot, wb, NW, ksplit):
            bk = Ring(BK[0:4])
            kper = nk // ksplit
            wi = proj_wi[0]
            for ct in range(D // NW):
                c0 = ct * NW
                slots = []
                for ks in range(ksplit):
                    ws, wsb = wslot[wi % len(wslot)], wb[wi % len(wslot)]
                    wi += 1
                    p.op("gpsimd", lambda e, ws=ws, ct=ct, ks=ks: e.dma_start(
                        out=ws.rearrange("p k n -> p (k n)"), in_=wc[ct * ksplit + ks]), [b_wc], [wsb], dma=wsb)
                    slots.append((ws, wsb))
                units = [(jc, hf) for jc in range(NW // 128) for hf in range(ntok // 512)]
                ubanks = [bk.next() for _ in units]
                for ks in range(ksplit):
                    ws, wsb = slots[ks]
                    for (jc, hf), (bank, bb) in zip(units, ubanks):
                        pairs = [(bank[:], ws[:, ii, jc * 128:(jc + 1) * 128], act_sb(ks * kper + ii, hf)) for ii in range(kper)]
                        mm_group(bank, pairs, [b_act, wsb], [bb], first=(ks == 0), last=(ks == ksplit - 1))
                for (jc, hf), (bank, bb) in zip(units, ubanks):
                    j = c0 // 128 + jc
                    xs, xsb = XR.next()
                    tt = T0 + hf * 512
                    p.op("sync", lambda e, xs=xs, j=j, tt=tt: e.dma_start(out=xs[:], in_=res_src[j * 128:(j + 1) * 128, tt:tt + 512]),
                         [], [xsb], dma=xsb)
                    st, sbf = STF.next()
                    p.op("vector", lambda e, st=st, bank=bank, j=j, xs=xs: e.scalar_tensor_tensor(
                        out=st[:], in0=bank[:], scalar=modt[:, gcol + j:gcol + j + 1], in1=xs[:], op0=ALU.mult, op1=ALU.add),
                        [bb, xsb, b_mod], [sbf])
                    p.op("sync", lambda e, st=st, j=j, tt=tt: e.dma_start(out=dst[j * 128:(j + 1) * 128, tt:tt + 512], in_=st[:]),
                         [sbf], [], dma=sbf)
            proj_wi[0] = wi

        proj_wi = [0]

        def phase_wo(l, cur, z1):
            yS = aview(0, KC * TBA * 2, BF16, "p (k t) -> p k t", t=TBA)
            b_y = buf("yS")
            WOFF = KC * TBA * 2
            wslot = [aview(WOFF + i * KC * NWB * 2, KC * NWB * 2, BF16, "p (k n) -> p k n", n=NWB) for i in range(2)]
            wb = bufs("wsl", 2)
            for tb in range(S // TBA):
                T0 = tb * TBA
                p.op("sync", lambda e, T0=T0: e.dma_start(out=yS, in_=yT[:, T0:T0 + TBA].rearrange("(k p) t -> p k t", p=128)),
                     [], [b_y], dma=buf("d_yS"))
                proj_residual(wc_o[l], buf("wc_o%d" % l), KC, lambda kc, hf: yS[:, kc, hf * 512:(hf + 1) * 512], b_y, TBA, T0,
                              2 * KC, cur, z1, wslot, wb, NWB, 1)
                p.barrier()

        def phase_ffn(l, z1, x1, z2):
            u2 = aview(0, KC * 512 * 2, BF16, "p (k t) -> p k t", t=512)
            b_u2 = buf("u2")
            hoff = KC * 512 * 2
            hT = aview(hoff, FC * 512 * 2, BF16, "p (k t) -> p k t", t=512)
            b_h = buf("hT")
            WOFF = hoff + FC * 512 * 2
            WSZ = KC * NWB * 2
            wslot_up = [aview(WOFF + i * WSZ, KC * NWB * 2, BF16, "p (k n) -> p k n", n=NWB) for i in range(2)]
            wslot_dn = [aview(WOFF + i * WSZ, KPER * NWD * 2, BF16, "p (k n) -> p k n", n=NWD) for i in range(2)]
            assert WOFF + 2 * WSZ <= C.ARENA, (WOFF + 2 * WSZ)
            wb = bufs("wsl", 2)
            x1b = bufs("x1c", KC)
            bk = Ring(BK[0:4])
            e1 = LN_EPS / (C.alpha ** 2)
            for tb in range(S // 512):
                t0 = tb * 512
                ln_stats(z1, t0, e1, mean_t, rstd_t, b_mean, b_rstd)
                st2 = ln_stats_begin()
                ln_apply(z1, t0, mean_t, rstd_t, b_mean, b_rstd, 0, KC, (vect, b_vec), dst_dram=x1, stats2=st2, dst_bufs=x1b)
                stats_finish(st2[0], st2[1], LN_EPS, mean2_t, rstd2_t, b_mean2, b_rstd2)
                ln_apply(x1, t0, mean2_t, rstd2_t, b_mean2, b_rstd2, 4 * KC, 3 * KC, (modt, b_mod),
                         dst_sb=lambda kc: u2[:, kc, :], dst_sb_buf=b_u2, src_bufs=x1b)
                p.barrier()
                bk8 = Ring(BK[0:8])
                for fg in range(0, FC, PER):
                    nch = min(PER, FC - fg)
                    slots = []
                    for half in range(2):
                        ti = (fg // PER) * 2 + half
                        ws, wsb = wslot_up[proj_wi[0] % 2], wb[proj_wi[0] % 2]
                        proj_wi[0] += 1
                        p.op("gpsimd", lambda e, ws=ws, ti=ti: e.dma_start(
                            out=ws.rearrange("p k n -> p (k n)"), in_=wc_up[l][ti]), [buf("wc_up%d" % l)], [wsb], dma=wsb)
                        slots.append((ws, wsb))
                    gb, vb = [], []
                    for jc in range(nch):
                        bg, bgb = bk8.next()
                        mm_group(bg, [(bg[:], slots[0][0][:, kc, jc * 128:(jc + 1) * 128], u2[:, kc, :]) for kc in range(KC)],
                                 [b_u2, slots[0][1]], [bgb])
                        gb.append((bg, bgb))
                    for jc in range(nch):
                        bv, bvb = bk8.next()
                        mm_group(bv, [(bv[:], slots[1][0][:, kc, jc * 128:(jc + 1) * 128], u2[:, kc, :]) for kc in range(KC)],
                                 [b_u2, slots[1][1]], [bvb])
                        vb.append((bv, bvb))
                    for jc in range(nch):
                        fc = fg + jc
                        bg, bgb = gb[jc]
                        bv, bvb = vb[jc]
                        ge, geb = GEXT.next()
                        p.op("vector", lambda e, ge=ge, fc=fc: e.tensor_copy(out=ge[:, 0:2], in_=halo[:, fc, :]), [b_halo], [geb])
                        p.op("scalar", lambda e, ge=ge, bg=bg: e.activation(out=ge[:, 2:514], in_=bg[:], func=AF.Copy), [bgb], [geb])
                        p.op("vector", lambda e, ge=ge, fc=fc: e.tensor_copy(out=halo[:, fc, :], in_=ge[:, 512:514]), [geb], [b_halo])
                        a_, ab = CA.next()
                        p.op("scalar", lambda e, a_=a_, ge=ge, fc=fc: e.activation(
                            out=a_[:], in_=ge[:, 0:512], func=AF.Identity, scale=convt[:, fc:fc + 1], bias=convt[:, 3 * FC + fc:3 * FC + fc + 1]),
                            [geb, b_vec], [ab])
                        b2, b2b = CB.next()
                        p.op("vector", lambda e, b2=b2, ge=ge, fc=fc, a_=a_: e.scalar_tensor_tensor(
                            out=b2[:], in0=ge[:, 1:513], scalar=convt[:, FC + fc:FC + fc + 1], in1=a_[:], op0=ALU.mult, op1=ALU.add),
                            [geb, ab, b_vec], [b2b])
                        a2, a2b = CA.next()
                        p.op("vector", lambda e, a2=a2, ge=ge, fc=fc, b2=b2: e.scalar_tensor_tensor(
                            out=a2[:], in0=ge[:, 2:514], scalar=convt[:, 2 * FC + fc:2 * FC + fc + 1], in1=b2[:], op0=ALU.mult, op1=ALU.add),
                            [geb, b2b, b_vec], [a2b])
                        b3, b3b = CB.next()
                        p.op("scalar", lambda e, b3=b3, a2=a2: e.activation(out=b3[:], in_=a2[:], func=AF.Silu), [a2b], [b3b])
                        p.op("vector", lambda e, fc=fc, b3=b3, bv=bv: e.tensor_tensor(out=hT[:, fc, :], in0=b3[:], in1=bv[:], op=ALU.mult),
                             [b3b, bvb], [b_h])
                proj_residual(wc_dn[l], buf("wc_dn%d" % l), FC, lambda kc, hf: hT[:, kc, :], b_h, 512, t0, 5 * KC, x1, z2,
                              wslot_dn, wb, NWD, KSPLIT)
                p.barrier()

        def phase_ln_out(l, z2, x2):
            e1 = LN_EPS / (C.alpha ** 2)
            for tb in range(S // 512):
                t0 = tb * 512
                ln_stats(z2, t0, e1, mean_t, rstd_t, b_mean, b_rstd)
                ln_apply(z2, t0, mean_t, rstd_t, b_mean, b_rstd, 2 * KC, 3 * KC, (vect, b_vec), dst_dram=x2)
            p.barrier()

        def phase_transpose_out(src):
            bk = Ring(BK[0:4])
            xo = [aview(i * D * 4, D * 4, F32) for i in range(2)]
            xob = bufs("xo", 2)
            for g in range(S // 512):
                tiles = []
                for kc in range(KC):
                    pass
                for i4 in range(4):
                    pass
                for i in range(4):
                    ot, otb = xo[i % 2], xob[i % 2]
                    for kg in range(0, KC, 4):
                        bank, bb = bk.next()
                        srcs = []
                        for kk in range(4):
                            kc = kg + kk
                            xs, xsb = XR.next()
                            p.op("sync", lambda e, xs=xs, kc=kc, g=g, i=i: e.dma_start(
                                out=xs[:, 0:128], in_=src[kc * 128:(kc + 1) * 128, g * 512 + i * 128:g * 512 + (i + 1) * 128]),
                                [], [xsb], dma=xsb)
                            srcs.append((xs, xsb))

                        def fn(e, bank=bank, srcs=srcs):
                            ins = None
                            for kk in range(4):
                                ins = e.transpose(bank[:, kk * 128:(kk + 1) * 128], srcs[kk][0][:, 0:128], ident_f[:])
                            return ins
                        p.op("tensor", fn, [s_[1] for s_ in srcs] + [b_const], [bb])
                        evict(ot[:, kg * 128:(kg + 4) * 128], bank[:], [bb], [otb])
                    r0 = g * 512 + i * 128
                    p.op("sync", lambda e, ot=ot, r0=r0: e.dma_start(out=out[r0:r0 + 128, :], in_=ot), [otb], [], dma=otb)
            p.barrier()

        conv_w_in(0)
        phase_transpose_in()
        cur = xT0
        for l in range(L):
            z1, x1, z2, x2 = zs[l]
            phase_params(l)
            phase_qkv(l, cur)
            conv_w_rest(l)
            if l + 1 < L:
                conv_w_in(l + 1)
            phase_attn(l)
            phase_wo(l, cur, z1)
            phase_ffn(l, z1, x1, z2)
            phase_ln_out(l, z2, x2)
            cur = x2
            if C.stop_after is not None and l == C.stop_after:
                break
        phase_transpose_out(cur)

        with nc.Block() as block:
            p.replay(block)
    return nc


def host_inputs(C, b, inp):
    S, D, KC, FC, L = C.S, C.D, C.KC, C.FC, C.depth
    f = lambda a: np.ascontiguousarray(a, dtype=np.float32)
    fm = lambda v, n: f(v.reshape(n, 128).T)
    d = {}
    d["x"] = f(inp["x"][b])
    d["c_fm"] = fm(inp["c"][b], KC)
    d["w_ada"] = f(inp["w_ada"])
    d["b_ada_fm"] = f(np.stack([fm(inp["b_ada"][l], 6 * KC) for l in range(L)]))
    d["w_in"] = f(inp["w_in"])
    d["lamv"] = f(np.stack([np.concatenate([inp["lambda_q1"][l], inp["lambda_k1"][l], inp["lambda_q2"][l], inp["lambda_k2"][l]])
                            for l in range(L)]))
    d["g_diff"] = f(inp["g_diff"])
    d["g_dil"] = f(inp["g_dil"])
    d["w_o"] = f(inp["w_o"])
    d["vec_fm"] = f(np.stack([np.concatenate([fm(inp[k][l], KC) for k in ("ln1_g", "ln1_b", "ln2_g", "ln2_b")], axis=1)
                              for l in range(L)]))
    d["w_up"] = f(inp["w_up"])
    d["conv_fm"] = f(np.stack([np.concatenate([fm(inp["conv_w"][l][0], FC), fm(inp["conv_w"][l][1], FC),
                                               fm(inp["conv_w"][l][2], FC), fm(inp["conv_b"][l], FC)], axis=1) for l in range(L)]))
    d["w_down"] = f(inp["w_down"])
    dist = DMAXT - np.arange(TL)
    idx = t5_bucket_np(dist)
    d["tabT"] = f(inp["rel_bias"][idx, :].T)
    cm = np.zeros((2, TL), np.float32)
    cm[0] = np.where(dist >= 0, 0.0, NEG)
    mult = ((dist >= 0) & (dist <= 128)).astype(np.float64) + ((dist >= 0) & (dist % 4 == 0) & (dist <= 512)) \
        + ((dist >= 0) & (dist % 16 == 0) & (dist <= 2048))
    cm[1] = np.where(mult > 0, np.log(np.maximum(mult, 1.0)), NEG)
    d["cmask"] = cm
    d["ident"] = np.eye(128, dtype=np.float32)
    d["jmat"] = f(np.eye(128, dtype=np.float32)[::-1])
    return d


_NC_CACHE = {}


def kernel(**inputs):
    C = Cfg()
    if "nc" not in _NC_CACHE:
        _NC_CACHE["nc"] = build(C)
    nc = _NC_CACHE["nc"]
    inp = {k: np.asarray(v) for k, v in inputs.items()}
    per_b = [host_inputs(C, b, inp) for b in range(2)]
    in_maps = [per_b[i // 4] for i in range(8)]
    res = run_bass_kernel_spmd(nc, in_maps, core_ids=list(range(8)))
    outp = np.stack([res.results[0]["out"], res.results[4]["out"]]).astype(np.float32)
    return outp
```

```python
import math
from contextlib import ExitStack

import numpy as np
import concourse.bass as bass
import concourse.mybir as mybir
from concourse.bass_utils import run_bass_kernel_spmd

F32 = mybir.dt.float32
BF16 = mybir.dt.bfloat16
AF = mybir.ActivationFunctionType
ALU = mybir.AluOpType
AX = mybir.AxisListType

ENGS = ["sync", "scalar", "vector", "gpsimd", "tensor"]
LN_EPS = 1e-5
NEG = -30000.0
DMAXT = 2559
TL = 3072
TW = 2944
DCL = 2175
SAME_SYNC = True


class Cfg:
    def __init__(self, S=4096, D=4096, depth=2, TBA=1024, NWB=256):
        self.S, self.D, self.depth = S, D, depth
        self.DW = D // 2
        self.LW = D - self.DW
        self.NDH = self.DW // 256
        self.NLH = self.LW // 128
        self.F = 256 * ((8 * D // 3 + 255) // 256)
        self.KC = D // 128
        self.FC = self.F // 128
        self.TBA = min(TBA, S)
        self.NWB = NWB
        self.INW = 3 * self.DW + 3 * self.LW
        self.alpha = (2 * depth) ** 0.25
        self.NM = 2 * self.NDH + self.NLH
        self.ARENA = 150 * 1024
        self.SEG = min(1024, S)
        self.EXT = 512
        self.stop_after = None
        self.debug = None


class Buf:
    __slots__ = ("name", "w", "r")

    def __init__(self, name):
        self.name = name
        self.w = None
        self.r = {}


class Prog:
    def __init__(self, nc, stack):
        self.nc, self.stack = nc, stack
        self.ops = {e: [] for e in ENGS}
        self.sems, self.cnt = {}, {}
        self.waited = {e: {} for e in ENGS}
        self.live = {}
        self.nsem = 0
        self.epoch = 0

    def _sem(self, key):
        if key not in self.sems:
            self.sems[key] = self.stack.enter_context(self.nc.semaphore("s%d" % self.nsem))
            self.nsem += 1
            self.cnt[key] = 0
        return self.sems[key]

    def op(self, eng, fn, reads=(), writes=(), dma=None):
        waits = {}

        def need(ev):
            if ev is None:
                return
            k, v = ev
            if dma is None and k[0] == "eng" and k[1] == eng and (eng == "tensor" or not SAME_SYNC):
                return
            if v > waits.get(k, 0):
                waits[k] = v

        for b in reads:
            need(b.w)
        for b in writes:
            need(b.w)
            for k, v in b.r.items():
                need((k, v))
        wl = []
        for k, v in waits.items():
            if self.waited[eng].get(k, 0) >= v:
                continue
            self.waited[eng][k] = v
            wl.append((self._sem(k), v))
        key = ("dma", dma.name) if dma is not None else ("eng", eng, self.epoch)
        sem = self._sem(key)
        inc = 16 if dma is not None else 1
        self.cnt[key] += inc
        ev = (key, self.cnt[key])
        self.live[key] = self.cnt[key]
        self.ops[eng].append((fn, wl, sem, inc))
        for b in writes:
            b.w = ev
            b.r = {}
        for b in reads:
            if b.r.get(key, 0) < ev[1]:
                b.r[key] = ev[1]
        return ev

    def barrier(self):
        for e in ENGS:
            wl = []
            for k, v in self.live.items():
                if self.waited[e].get(k, 0) >= v:
                    continue
                self.waited[e][k] = v
                wl.append((self._sem(k), v))
            if wl:
                self.ops[e].append((None, wl, None, 0))
        self.live = {}
        for k in list(self.sems.keys()):
            if self.cnt[k] > 24000:
                if k[0] == "eng":
                    pass
                else:
                    del self.sems[k]
        if any(self.cnt.get(("eng", e, self.epoch), 0) > 24000 for e in ENGS):
            self.epoch += 1

    def replay(self, block):
        def mk(ename):
            lst = self.ops[ename]

            def body(e):
                for fn, wl, sem, inc in lst:
                    for s, v in wl:
                        e.wait_ge(s, v)
                    if fn is not None:
                        ins = fn(e)
                        ins.then_inc(sem, inc)
            return body

        block.sync(mk("sync"))
        block.scalar(mk("scalar"))
        block.vector(mk("vector"))
        block.gpsimd(mk("gpsimd"))
        block.tensor(mk("tensor"))


class Ring:
    def __init__(self, items):
        self.items = items
        self.i = 0

    def next(self):
        it = self.items[self.i % len(self.items)]
        self.i += 1
        return it


def t5_bucket_np(d):
    n = np.maximum(d, 0)
    max_exact = 16
    nf = np.maximum(n, max_exact).astype(np.float32)
    large = max_exact + (np.log(nf / max_exact) / math.log(2048 / max_exact) * (32 - max_exact)).astype(np.int32)
    large = np.minimum(large, 31)
    return np.where(n < max_exact, n, large)


def build(C):
    nc = bass.Bass("TRN2", target_bir_lowering=False)
    S, D, F, KC, FC, L = C.S, C.D, C.F, C.KC, C.FC, C.depth
    NDH, NLH, NM, DW, LW, INW = C.NDH, C.NLH, C.NM, C.DW, C.LW, C.INW
    TBA, NWB = C.TBA, C.NWB
    NT5 = S // 512
    QA0, KA0, VA0, QB0, KB0, VB0 = 0, DW, 2 * DW, 3 * DW, 3 * DW + LW, 3 * DW + 2 * LW

    def din(name, shape):
        return nc.dram_tensor(name, list(shape), F32, kind="ExternalInput").ap()

    x_in = din("x", [S, D])
    c_fm = din("c_fm", [128, KC])
    w_ada = din("w_ada", [L, D, 6 * D])
    b_ada_fm = din("b_ada_fm", [L, 128, 6 * KC])
    w_in = din("w_in", [L, D, INW])
    lamv = din("lamv", [L, 4 * 128])
    g_diff = din("g_diff", [L, 256])
    g_dil = din("g_dil", [L, 128])
    w_o = din("w_o", [L, D, D])
    vec_fm = din("vec_fm", [L, 128, 4 * KC])
    w_up = din("w_up", [L, D, 2 * F])
    conv_fm = din("conv_fm", [L, 128, 4 * FC])
    w_down = din("w_down", [L, F, D])
    tabT = din("tabT", [NDH + NLH, TL])
    cmask = din("cmask", [2, TL])
    ident_in = din("ident", [128, 128])
    jmat_in = din("jmat", [128, 128])
    kmask_in = din("kmask_fm", [128, S // 128])
    hmask_in = din("hmask", [128, S // 512])
    SEG = C.SEG
    TPART = max(0, S - SEG - C.EXT)
    out = nc.dram_tensor("out", [SEG, D], F32, kind="ExternalOutput").ap()

    def dscr(name, shape, dt):
        return nc.dram_tensor(name, list(shape), dt).ap()

    xT0 = dscr("xT0", [D, S], F32)
    zs = [[dscr("r%d_%d" % (l, i), [D, S], F32) for i in range(4)] for l in range(L)]
    qrT = dscr("qrT", [NM * 128, S], BF16)
    kTd = dscr("kTd", [NM * 128, S], BF16)
    Vd = dscr("Vd", [S, D], BF16)
    yT = dscr("yT", [D, S], BF16)

    PER = NWB // 128
    NUPT = 2 * ((FC + PER - 1) // PER)
    KSPLIT = 2 if FC % 2 == 0 else 1
    KPER = FC // KSPLIT
    NWD = 256 if KPER * 256 * 2 <= KC * NWB * 2 else 128
    wc_in = [dscr("wc_in%d" % l, [INW // NWB, 128, KC * NWB], BF16) for l in range(L)]
    wc_o = [dscr("wc_o%d" % l, [D // NWB, 128, KC * NWB], BF16) for l in range(L)]
    wc_up = [dscr("wc_up%d" % l, [NUPT, 128, KC * NWB], BF16) for l in range(L)]
    wc_dn = [dscr("wc_dn%d" % l, [(D // NWD) * KSPLIT, 128, KPER * NWD], BF16) for l in range(L)]

    stack = ExitStack()
    with stack:
        p = Prog(nc, stack)
        sb = lambda name, shape, dt: stack.enter_context(nc.sbuf_tensor(name, list(shape), dt))
        ident_b = sb("ident_b", [128, 128], BF16)
        jmat_b = sb("jmat_b", [128, 128], BF16)
        ident_f = sb("ident_f", [128, 128], F32)
        ones_f = sb("ones_f", [128, 128], F32)
        modt = sb("modt", [128, 6 * KC], F32)
        badat = sb("badat", [128, 6 * KC], F32)
        vect = sb("vect", [128, 4 * KC], F32)
        convt = sb("convt", [128, 4 * FC], F32)
        halo = sb("halo", [128, FC, 2], F32)
        cfm_t = sb("cfm_t", [128, KC], F32)
        csil = sb("csil", [128, KC], BF16)
        lamb = sb("lamb", [128, 4 * 128], F32)
        lamp = sb("lamp", [128, 2 * 128], F32)
        lams = sb("lams", [128, 8], F32)
        gdf = sb("gdf", [128, 256], F32)
        gdl = sb("gdl", [128, 128], F32)
        kmask_t = sb("kmask_t", [128, S // 128], F32)
        hmask_t = sb("hmask_t", [128, S // 512], F32)
        NXR = 4
        xr = [sb("xr%d" % i, [128, 512], F32) for i in range(NXR)]
        stf = [sb("stf%d" % i, [128, 512], F32) for i in range(3)]
        stb = [sb("stb%d" % i, [128, 512], BF16) for i in range(8)]
        ptr = [sb("ptr%d" % i, [128, 512], BF16) for i in range(3)]
        sm = [sb("sm%d" % i, [128, 8], F32) for i in range(8)]
        arena2 = sb("arena2", [128, 4096], F32)

        def a2(off_bytes, nbytes, dt):
            a = arena2[:, off_bytes // 4:(off_bytes + nbytes) // 4]
            return a.bitcast(BF16) if dt == BF16 else a
        sqr = [a2(i * 2048, 2048, F32) for i in range(2)]
        t1r = [a2(4096 + i * 2048, 2048, F32) for i in range(2)]
        mean_t, rstd_t, mean2_t, rstd2_t = [a2(8192 + i * 2048, 2048, F32) for i in range(4)]
        gext = [a2(i * 2056, 2056, F32) for i in range(2)]
        ca = [a2(4112 + i * 2048, 2048, F32) for i in range(2)]
        cb_ = [a2(8208 + i * 2048, 2048, F32) for i in range(2)]
        y1t = [a2(i * 1024, 1024, F32) for i in range(4)]
        y2t = [a2(4096 + i * 1024, 1024, F32) for i in range(2)]
        ysq = [a2(6144 + i * 1024, 1024, F32) for i in range(2)]
        ynb = [a2(8192 + i * 512, 512, BF16) for i in range(4)]
        ARW = C.ARENA // 4
        arena = sb("arena", [128, ARW], F32)
        banks = [stack.enter_context(nc.psum_tensor("bank%d" % i, [128, 512], F32)) for i in range(8)]

        B = {}

        def buf(name):
            if name not in B:
                B[name] = Buf(name)
            return B[name]

        def bufs(prefix, n):
            return [buf("%s%d" % (prefix, i)) for i in range(n)]

        def ring_of(tiles, prefix):
            return Ring(list(zip(tiles, bufs(prefix, len(tiles)))))

        XR = ring_of(xr, "xr")
        SQR = ring_of(sqr, "sqr")
        T1R = ring_of(t1r, "t1r")
        STF = ring_of(stf, "stf")
        STB = ring_of(stb, "stb")
        PTR = ring_of(ptr, "ptr")
        GEXT = ring_of(gext, "gext")
        CA = ring_of(ca, "ca")
        CB = ring_of(cb_, "cb")
        SM = ring_of(sm, "sm")
        Y1 = ring_of(y1t, "y1t")
        Y2 = ring_of(y2t, "y2t")
        YSQ = ring_of(ysq, "ysq")
        YNB = ring_of(ynb, "ynb")
        BK = [(banks[i], buf("bank%d" % i)) for i in range(8)]
        b_const = buf("const")
        b_mod = buf("mod")
        b_vec = buf("vec")
        b_halo = buf("halo")
        b_lam = buf("lam")
        b_g = buf("g")
        b_mean, b_rstd = buf("mean"), buf("rstd")
        b_mean2, b_rstd2 = buf("mean2"), buf("rstd2")

        def aview(off_bytes, nbytes, dt, pattern=None, **kw):
            a = arena[:, off_bytes // 4:(off_bytes + nbytes) // 4]
            if dt == BF16:
                a = a.bitcast(BF16)
            if pattern:
                a = a.rearrange(pattern, **kw)
            return a

        evict_flip = [0]

        def evict(out_ap, in_ap, reads, writes, scale=None):
            evict_flip[0] ^= 1
            if evict_flip[0]:
                if scale is None:
                    p.op("scalar", lambda e: e.activation(out=out_ap, in_=in_ap, func=AF.Copy), reads, writes)
                else:
                    p.op("scalar", lambda e: e.activation(out=out_ap, in_=in_ap, func=AF.Copy, scale=scale), reads, writes)
            else:
                if scale is None:
                    p.op("vector", lambda e: e.tensor_copy(out=out_ap, in_=in_ap), reads, writes)
                else:
                    p.op("vector", lambda e: e.tensor_scalar(out=out_ap, in0=in_ap, scalar1=scale, scalar2=None, op0=ALU.mult), reads, writes)

        def mm_group(bank, pairs, reads, writes, first=True, last=True):
            n = len(pairs)

            def fn(e):
                ins = None
                for i, (o, l_, r_) in enumerate(pairs):
                    ins = e.matmul(o, lhsT=l_, rhs=r_, start=(first and i == 0), stop=(last and i == n - 1))
                return ins
            p.op("tensor", fn, reads, writes)

        p.op("gpsimd", lambda e: e.dma_start(out=ident_b[:], in_=ident_in[:, :]), [], [b_const], dma=buf("d_identb"))
        p.op("gpsimd", lambda e: e.dma_start(out=jmat_b[:], in_=jmat_in[:, :]), [], [b_const], dma=buf("d_jb"))
        p.op("sync", lambda e: e.dma_start(out=ident_f[:], in_=ident_in[:, :]), [], [b_const], dma=buf("d_identf"))
        p.op("sync", lambda e: e.dma_start(out=cfm_t[:], in_=c_fm[:, :]), [], [b_const], dma=buf("d_cfm"))
        p.op("vector", lambda e: e.memset(ones_f[:], 1.0), [], [b_const])
        p.op("sync", lambda e: e.dma_start(out=kmask_t[:], in_=kmask_in[:, :]), [], [b_const], dma=buf("d_kmask"))
        p.op("sync", lambda e: e.dma_start(out=hmask_t[:], in_=hmask_in[:, :]), [], [b_const], dma=buf("d_hmask"))
        p.op("scalar", lambda e: e.activation(out=csil[:], in_=cfm_t[:], func=AF.Silu), [b_const], [b_const])
        p.barrier()

        def conv_w_in(l):
            b = buf("wc_in%d" % l)
            for wt in range(INW // NWB):
                c0 = wt * NWB
                p.op("gpsimd", lambda e, wt=wt, c0=c0: e.dma_start(
                    out=wc_in[l][wt].rearrange("p (k n) -> p k n", n=NWB),
                    in_=w_in[l, :, c0:c0 + NWB].rearrange("(k p) n -> p k n", p=128)), [], [b], dma=b)

        def conv_w_rest(l):
            b = buf("wc_o%d" % l)
            for wt in range(D // NWB):
                c0 = wt * NWB
                p.op("gpsimd", lambda e, wt=wt, c0=c0: e.dma_start(
                    out=wc_o[l][wt].rearrange("p (k n) -> p k n", n=NWB),
                    in_=w_o[l, :, c0:c0 + NWB].rearrange("(k p) n -> p k n", p=128)), [], [b], dma=b)
            b = buf("wc_up%d" % l)
            for ti in range(NUPT):
                fg, half = (ti // 2) * PER, ti % 2
                nch = min(PER, FC - fg)
                c0 = half * F + fg * 128
                p.op("gpsimd", lambda e, ti=ti, c0=c0, nch=nch: e.dma_start(
                    out=wc_up[l][ti].rearrange("p (k n) -> p k n", n=NWB)[:, :, 0:nch * 128],
                    in_=w_up[l, :, c0:c0 + nch * 128].rearrange("(k p) n -> p k n", p=128)), [], [b], dma=b)
            b = buf("wc_dn%d" % l)
            for ct in range(D // NWD):
                for ks in range(KSPLIT):
                    k_lo = ks * KPER * 128
                    p.op("gpsimd", lambda e, ct=ct, ks=ks, k_lo=k_lo: e.dma_start(
                        out=wc_dn[l][ct * KSPLIT + ks].rearrange("p (k n) -> p k n", n=NWD),
                        in_=w_down[l, k_lo:k_lo + KPER * 128, ct * NWD:(ct + 1) * NWD].rearrange("(k p) n -> p k n", p=128)),
                        [], [b], dma=b)

        def phase_transpose_in():
            xt_tiles = [aview(i * D * 4, D * 4, F32) for i in range(4)]
            xt_b = bufs("xt", 4)
            bk = Ring(BK[0:4])
            for g in range(S // 512):
                for i in range(4):
                    r0 = g * 512 + i * 128
                    p.op("sync", lambda e, i=i, r0=r0: e.dma_start(out=xt_tiles[i], in_=x_in[r0:r0 + 128, :]),
                         [], [xt_b[i]], dma=xt_b[i])
                for kc in range(KC):
                    bank, bb = bk.next()

                    def fn(e, kc=kc, bank=bank):
                        ins = None
                        for i in range(4):
                            ins = e.transpose(bank[:, i * 128:(i + 1) * 128], xt_tiles[i][:, kc * 128:(kc + 1) * 128], ident_f[:])
                        return ins
                    p.op("tensor", fn, xt_b + [b_const], [bb])
                    st, sbf = STF.next()
                    evict(st[:], bank[:], [bb], [sbf])
                    p.op("sync", lambda e, st=st, kc=kc, g=g: e.dma_start(out=xT0[kc * 128:(kc + 1) * 128, g * 512:(g + 1) * 512], in_=st[:]),
                         [sbf], [], dma=sbf)
            p.barrier()

        def ln_stats_begin():
            return BK[4], BK[5]

        def stats_accum(xs, xsb, kc, bs, bq):
            sq, sqb = SQR.next()
            p.op("scalar", lambda e: e.activation(out=sq, in_=xs[:], func=AF.Square), [xsb], [sqb])
            p.op("tensor", lambda e: e.matmul(bs[0][:], lhsT=ones_f[:], rhs=xs[:], start=(kc == 0), stop=(kc == KC - 1)),
                 [xsb, b_const], [bs[1]])
            p.op("tensor", lambda e: e.matmul(bq[0][:], lhsT=ones_f[:], rhs=sq, start=(kc == 0), stop=(kc == KC - 1)),
                 [sqb, b_const], [bq[1]])

        def stats_finish(bs, bq, eps, mean_ap, rstd_ap, bm, br):
            msq, b_msq = SQR.next()
            var, b_var = SQR.next()
            p.op("scalar", lambda e: e.activation(out=mean_ap, in_=bs[0][:], func=AF.Copy, scale=1.0 / D), [bs[1]], [bm])
            p.op("vector", lambda e: e.tensor_tensor(out=msq, in0=mean_ap, in1=mean_ap, op=ALU.mult), [bm], [b_msq])
            p.op("vector", lambda e: e.scalar_tensor_tensor(out=var, in0=bq[0][:], scalar=1.0 / D, in1=msq,
                                                            op0=ALU.mult, op1=ALU.subtract), [bq[1], b_msq], [b_var])
            p.op("vector", lambda e: e.tensor_scalar(out=var, in0=var, scalar1=float(eps), scalar2=None, op0=ALU.add),
                 [b_var], [b_var])
            p.op("scalar", lambda e: e.activation(out=var, in_=var, func=AF.Sqrt), [b_var], [b_var])
            p.op("vector", lambda e: e.reciprocal(out=rstd_ap, in_=var), [b_var], [br])

        def ln_stats(src, t0, eps, mean_ap, rstd_ap, bm, br):
            bs, bq = ln_stats_begin()
            for kc in range(KC):
                xs, xsb = XR.next()
                p.op("sync", lambda e, xs=xs, kc=kc: e.dma_start(out=xs[:], in_=src[kc * 128:(kc + 1) * 128, t0:t0 + 512]),
                     [], [xsb], dma=xsb)
                stats_accum(xs, xsb, kc, bs, bq)
            stats_finish(bs, bq, eps, mean_ap, rstd_ap, bm, br)

        def ln_apply(src, t0, mean_ap, rstd_ap, bm, br, sc_col, bi_col, sbuf_for, dst_dram=None, dst_sb=None, dst_sb_buf=None,
                     stats2=None, src_bufs=None, dst_bufs=None):
            for kc in range(KC):
                xs, xsb = XR.next()
                rd = [src_bufs[kc]] if src_bufs is not None else []
                p.op("sync", lambda e, xs=xs, kc=kc: e.dma_start(out=xs[:], in_=src[kc * 128:(kc + 1) * 128, t0:t0 + 512]),
                     rd, [xsb], dma=xsb)
                t1, t1b = T1R.next()
                p.op("vector", lambda e, xs=xs, t1=t1: e.tensor_tensor(out=t1, in0=xs[:], in1=mean_ap, op=ALU.subtract),
                     [xsb, bm], [t1b])
                t2, t2b = t1, t1b
                p.op("vector", lambda e, t1=t1: e.tensor_tensor(out=t1, in0=t1, in1=rstd_ap, op=ALU.mult),
                     [t1b, br], [t1b])
                sc_ap = sbuf_for[0][:, sc_col + kc:sc_col + kc + 1]
                bi_ap = sbuf_for[0][:, bi_col + kc:bi_col + kc + 1]
                if dst_dram is not None:
                    st, sbf = STF.next()
                    p.op("scalar", lambda e, st=st, t2=t2, sc_ap=sc_ap, bi_ap=bi_ap: e.activation(
                        out=st[:], in_=t2, func=AF.Identity, scale=sc_ap, bias=bi_ap), [t2b, sbuf_for[1]], [sbf])
                    if stats2 is not None:
                        stats_accum(st, sbf, kc, stats2[0], stats2[1])
                    wr = [dst_bufs[kc]] if dst_bufs is not None else []
                    p.op("sync", lambda e, st=st, kc=kc: e.dma_start(out=dst_dram[kc * 128:(kc + 1) * 128, t0:t0 + 512], in_=st[:]),
                         [sbf], wr, dma=sbf)
                else:
                    o_ap = dst_sb(kc)
                    p.op("scalar", lambda e, o_ap=o_ap, t2=t2, sc_ap=sc_ap, bi_ap=bi_ap: e.activation(
                        out=o_ap, in_=t2, func=AF.Identity, scale=sc_ap, bias=bi_ap), [t2b, sbuf_for[1]], [dst_sb_buf])

        def phase_params(l):
            lam_init = 0.8 - 0.6 * math.exp(-0.3 * l)
            p.op("sync", lambda e: e.dma_start(out=badat[:], in_=b_ada_fm[l]), [], [b_mod], dma=buf("d_bada"))
            p.op("sync", lambda e: e.dma_start(out=vect[:], in_=vec_fm[l]), [], [b_vec], dma=buf("d_vec"))
            p.op("sync", lambda e: e.dma_start(out=convt[:], in_=conv_fm[l]), [], [b_vec], dma=buf("d_conv"))
            src = bass.AP(tensor=lamv.tensor, offset=l * 512, ap=[[0, 128], [1, 512]])
            p.op("sync", lambda e: e.dma_start(out=lamb[:], in_=src), [], [b_lam], dma=buf("d_lam"))
            srcg = bass.AP(tensor=g_diff.tensor, offset=l * 256, ap=[[0, 128], [1, 256]])
            p.op("sync", lambda e: e.dma_start(out=gdf[:], in_=srcg), [], [b_g], dma=buf("d_gdf"))
            srcl = bass.AP(tensor=g_dil.tensor, offset=l * 128, ap=[[0, 128], [1, 128]])
            p.op("sync", lambda e: e.dma_start(out=gdl[:], in_=srcl), [], [b_g], dma=buf("d_gdl"))
            p.op("vector", lambda e: e.memset(halo[:], 0.0), [], [b_halo])
            p.op("vector", lambda e: e.tensor_tensor(out=lamp[:, 0:128], in0=lamb[:, 0:128], in1=lamb[:, 128:256], op=ALU.mult), [b_lam], [b_lam])
            p.op("vector", lambda e: e.tensor_tensor(out=lamp[:, 128:256], in0=lamb[:, 256:384], in1=lamb[:, 384:512], op=ALU.mult), [b_lam], [b_lam])
            p.op("vector", lambda e: e.reduce_sum(out=lams[:, 0:1], in_=lamp[:, 0:128], axis=AX.X), [b_lam], [b_lam])
            p.op("vector", lambda e: e.reduce_sum(out=lams[:, 1:2], in_=lamp[:, 128:256], axis=AX.X), [b_lam], [b_lam])
            p.op("scalar", lambda e: e.activation(out=lams[:, 2:4], in_=lams[:, 0:2], func=AF.Exp), [b_lam], [b_lam])
            p.op("vector", lambda e: e.tensor_tensor(out=lams[:, 4:5], in0=lams[:, 2:3], in1=lams[:, 3:4], op=ALU.subtract), [b_lam], [b_lam])
            p.op("vector", lambda e: e.tensor_scalar(out=lams[:, 5:6], in0=lams[:, 4:5], scalar1=float(lam_init), scalar2=-1.0,
                                                     op0=ALU.add, op1=ALU.mult), [b_lam], [b_lam])
            p.op("vector", lambda e: e.tensor_scalar(out=gdf[:], in0=gdf[:], scalar1=float(1.0 - lam_init), scalar2=None, op0=ALU.mult),
                 [b_g], [b_g])
            NWM = 512
            wslot = [aview(i * KC * NWM * 2, KC * NWM * 2, BF16, "p (k n) -> p k n", n=NWM) for i in range(2)]
            wb = bufs("wsl", 2)
            bank, bb = BK[0]
            ntile = 6 * D // NWM
            for t in range(ntile):
                ws, wsb = wslot[t % 2], wb[t % 2]
                p.op("gpsimd", lambda e, ws=ws, t=t: e.dma_start(
                    out=ws, in_=w_ada[l, :, t * NWM:(t + 1) * NWM].rearrange("(k p) n -> p k n", p=128)), [], [wsb], dma=wsb)
                for j in range(NWM // 128):
                    col = t * (NWM // 128) + j
                    pairs = [(bank[:, col:col + 1], ws[:, kc, j * 128:(j + 1) * 128], csil[:, kc:kc + 1]) for kc in range(KC)]
                    mm_group(bank, pairs, [wsb, b_const], [bb])
            p.op("vector", lambda e: e.tensor_tensor(out=modt[:], in0=bank[:, 0:6 * KC], in1=badat[:], op=ALU.add), [bb, b_mod], [b_mod])
            for base in (KC, 4 * KC):
                p.op("vector", lambda e, base=base: e.tensor_scalar(out=modt[:, base:base + KC], in0=modt[:, base:base + KC],
                                                                    scalar1=1.0, scalar2=None, op0=ALU.add), [b_mod], [b_mod])
            for base in (2 * KC, 5 * KC):
                p.op("vector", lambda e, base=base: e.tensor_scalar(out=modt[:, base:base + KC], in0=modt[:, base:base + KC],
                                                                    scalar1=float(1.0 / C.alpha), scalar2=None, op0=ALU.mult), [b_mod], [b_mod])
            p.barrier()

        def phase_qkv(l, cur, tstart=0):
            NTB = S // TBA
            uT = aview(0, KC * TBA * 2, BF16, "p (k t) -> p k t", t=TBA)
            b_uT = buf("uT")
            WOFF = KC * TBA * 2
            wslot = [aview(WOFF + i * KC * NWB * 2, KC * NWB * 2, BF16, "p (k n) -> p k n", n=NWB) for i in range(2)]
            wb = bufs("wsl", 2)
            bk = Ring(BK[0:4])
            bk2 = Ring(BK[6:8])
            nwt = INW // NWB
            for tb in range(NTB):
                T0 = tb * TBA
                for sbk in range(TBA // 512):
                    t0 = T0 + sbk * 512
                    ln_stats(cur, t0, LN_EPS, mean_t, rstd_t, b_mean, b_rstd)
                    ln_apply(cur, t0, mean_t, rstd_t, b_mean, b_rstd, KC, 0, (modt, b_mod),
                             dst_sb=lambda kc, sbk=sbk: uT[:, kc, sbk * 512:(sbk + 1) * 512], dst_sb_buf=b_uT)
                for wt in range(nwt):
                    c0 = wt * NWB
                    ws, wsb = wslot[wt % 2], wb[wt % 2]
                    p.op("gpsimd", lambda e, ws=ws, wt=wt: e.dma_start(
                        out=ws.rearrange("p k n -> p (k n)"), in_=wc_in[l][wt]), [buf("wc_in%d" % l)], [wsb], dma=wsb)
                    if c0 < KA0:
                        kind, mbase = "q", (c0 - QA0) // 128
                    elif c0 < VA0:
                        kind, mbase = "k", (c0 - KA0) // 128
                    elif c0 < QB0:
                        kind, vcol = "v", c0 - VA0
                    elif c0 < KB0:
                        kind, mbase = "q", 2 * NDH + (c0 - QB0) // 128
                    elif c0 < VB0:
                        kind, mbase = "k", 2 * NDH + (c0 - KB0) // 128
                    else:
                        kind, vcol = "v", DW + (c0 - VB0)
                    for g5 in range(TBA // 512):
                        if kind == "q" and T0 + g5 * 512 < tstart:
                            continue
                        toks = []
                        for i in range(4):
                            tl = g5 * 512 + i * 128
                            bank, bb = bk.next()
                            pairs = [(bank[:, 0:NWB], uT[:, kc, tl:tl + 128], ws[:, kc, :]) for kc in range(KC)]
                            mm_group(bank, pairs, [b_uT, wsb], [bb])
                            st, sbf = STB.next()
                            evict(st[:, 0:NWB], bank[:, 0:NWB], [bb], [sbf])
                            if kind == "v":
                                r0 = T0 + tl
                                p.op("sync", lambda e, st=st, r0=r0, vcol=vcol: e.dma_start(
                                    out=Vd[r0:r0 + 128, vcol:vcol + NWB], in_=st[:, 0:NWB]), [sbf], [], dma=sbf)
                            else:
                                toks.append((st, sbf))
                        if kind == "v":
                            continue
                        for m in range(NWB // 128):
                            bank2, bb2 = bk2.next()

                            def fn(e, m=m, bank2=bank2, toks=toks, kind=kind):
                                ins = None
                                for i in range(4):
                                    if kind == "q":
                                        ins = e.matmul(bank2[:, (3 - i) * 128:(4 - i) * 128], lhsT=toks[i][0][:, m * 128:(m + 1) * 128],
                                                       rhs=jmat_b[:], start=True, stop=True)
                                    else:
                                        ins = e.matmul(bank2[:, i * 128:(i + 1) * 128], lhsT=toks[i][0][:, m * 128:(m + 1) * 128],
                                                       rhs=ident_b[:], start=True, stop=True)
                                return ins
                            p.op("tensor", fn, [t[1] for t in toks] + [b_const], [bb2])
                            st2, sbf2 = PTR.next()
                            evict(st2[:], bank2[:], [bb2], [sbf2])
                            dst = qrT if kind == "q" else kTd
                            mi = mbase + m
                            c5 = T0 + g5 * 512
                            p.op("sync", lambda e, st2=st2, dst=dst, mi=mi, c5=c5: e.dma_start(
                                out=dst[mi * 128:(mi + 1) * 128, c5:c5 + 512], in_=st2[:]), [sbf2], [], dma=sbf2)
                p.barrier()

        def phase_attn(l, tstart=0):
            NB = S // 128
            scale = 128 ** -0.5
            off = 0
            cm_t = []
            for i in range(2):
                cm_t.append(aview(off, TW * 4, F32))
                off += TW * 4
            sets = []
            for i in range(2):
                d = {}
                d["k"] = [aview(off + j * S * 2, S * 2, BF16) for j in range(2)]
                off += 2 * S * 2
                d["q"] = [aview(off + j * S * 2, S * 2, BF16) for j in range(2)]
                off += 2 * S * 2
                vbytes = NB * 258 * 2
                d["v"] = aview(off, vbytes, BF16, "p (b c) -> p b c", c=258)
                off += vbytes
                d["w"] = aview(off, TW * 4, F32)
                off += TW * 4
                d["bk"] = buf("hk%d" % i)
                d["bq"] = buf("hq%d" % i)
                d["bv"] = buf("hv%d" % i)
                d["bw"] = buf("hw%d" % i)
                sets.append(d)
            assert off <= C.ARENA, off
            tsr = [aview(off + i * 2048, 2048, F32) for i in range(2)]
            off += 4096
            assert off <= C.ARENA, off
            TSR = ring_of(tsr, "tsr")
            b_cm = buf("cm")
            for i in range(2):
                srcm = bass.AP(tensor=cmask.tensor, offset=i * TL, ap=[[1, 128], [1, TW]])
                p.op("sync", lambda e, i=i, srcm=srcm: e.dma_start(out=cm_t[i], in_=srcm), [], [b_cm], dma=buf("d_cm%d" % i))
            sbk = Ring(BK[0:2])
            obk = BK[2:6]
            tbk = Ring(BK[6:8])
            heads = [("diff", h) for h in range(NDH)] + [("dil", h) for h in range(NLH)]
            for hi, (kind, h) in enumerate(heads):
                d = sets[hi % 2]
                nmap = 2 if kind == "diff" else 1
                VW = 256 if kind == "diff" else 128
                m0 = 2 * h if kind == "diff" else 2 * NDH + h
                vc0 = h * 256 if kind == "diff" else DW + h * 128
                yrow0 = vc0
                for j in range(nmap):
                    p.op("sync", lambda e, d=d, j=j, m0=m0: e.dma_start(out=d["k"][j], in_=kTd[(m0 + j) * 128:(m0 + j + 1) * 128, :]),
                         [], [d["bk"]], dma=buf("d_hk%d_%d" % (hi % 2, j)))
                    p.op("scalar", lambda e, d=d, j=j, m0=m0: e.dma_start(out=d["q"][j], in_=qrT[(m0 + j) * 128:(m0 + j + 1) * 128, :]),
                         [], [d["bq"]], dma=buf("d_hq%d_%d" % (hi % 2, j)))
                p.op("sync", lambda e, d=d, vc0=vc0, VW=VW: e.dma_start(
                    out=d["v"][:, :, 0:VW], in_=Vd[:, vc0:vc0 + VW].rearrange("(b p) c -> p b c", p=128)),
                    [], [d["bv"]], dma=buf("d_hv%d" % (hi % 2)))
                p.op("vector", lambda e, d=d, VW=VW: e.memset(d["v"][:, :, VW:VW + 1], 1.0), [d["bv"]], [d["bv"]])
                srcw = bass.AP(tensor=tabT.tensor, offset=hi * TL, ap=[[1, 128], [1, TW]])
                p.op("sync", lambda e, d=d, srcw=srcw: e.dma_start(out=d["w"], in_=srcw), [], [d["bw"]], dma=buf("d_hw%d" % (hi % 2)))
                cmi = 0 if kind == "diff" else 1
                p.op("vector", lambda e, d=d, cmi=cmi: e.tensor_tensor(out=d["w"], in0=d["w"], in1=cm_t[cmi], op=ALU.add),
                     [d["bw"], b_cm], [d["bw"]])
                gt_ap = gdf if kind == "diff" else gdl
                for qt in range(tstart // 512, NT5):
                    q0 = qt * 512
                    kb_hi = 4 * qt + 3
                    kb_lo = 0 if kind == "diff" else max(0, 4 * qt - 16)
                    for j in range(nmap):
                        for kb in range(kb_lo, kb_hi + 1):
                            k0 = kb * 128
                            delta = q0 + 511 - k0
                            if kind == "diff":
                                delta = min(delta, DCL)
                            base = DMAXT - delta
                            assert 0 <= base and base + 512 <= TW
                            bank, bb = sbk.next()
                            p.op("tensor", lambda e, bank=bank, d=d, j=j, k0=k0, q0=q0: e.matmul(
                                bank[:], lhsT=d["k"][j][:, k0:k0 + 128], rhs=d["q"][j][:, q0:q0 + 512], start=True, stop=True),
                                [d["bk"], d["bq"]], [bb])
                            ts_, tsb = TSR.next()
                            p.op("vector", lambda e, ts_=ts_, bank=bank, d=d, base=base: e.scalar_tensor_tensor(
                                out=ts_, in0=bank[:], scalar=float(scale), in1=d["w"][:, base:base + 512], op0=ALU.mult, op1=ALU.add),
                                [bb, d["bw"]], [tsb])
                            pt, ptb = PTR.next()
                            p.op("scalar", lambda e, pt=pt, ts_=ts_, kb=kb: e.activation(out=pt[:], in_=ts_, func=AF.Exp, bias=kmask_t[:, kb:kb + 1]), [tsb, b_const], [ptb])

                            def fnpv(e, pt=pt, d=d, kb=kb, VW=VW, first=(kb == kb_lo), last=(kb == kb_hi)):
                                ins = None
                                for s in range(4):
                                    ins = e.matmul(obk[s][0][:, 0:VW + 1], lhsT=pt[:, s * 128:(s + 1) * 128], rhs=d["v"][:, kb, 0:VW + 1],
                                                   start=first, stop=last)
                                return ins
                            p.op("tensor", fnpv, [ptb, d["bv"]], [o[1] for o in obk])
                        for s in range(4):
                            ob, obb = obk[s]
                            smt, smb = SM.next()
                            p.op("vector", lambda e, smt=smt, ob=ob, VW=VW: e.tensor_scalar(out=smt[:, 6:7], in0=ob[:, VW:VW + 1], scalar1=1e-30, scalar2=None, op0=ALU.max), [obb], [smb])
                            p.op("vector", lambda e, smt=smt: e.reciprocal(out=smt[:, 0:1], in_=smt[:, 6:7]), [smb], [smb])
                            if kind == "diff" and j == 0:
                                y1, y1b = y1t[s], buf("y1t%d" % s)
                                p.op("scalar", lambda e, y1=y1, ob=ob, smt=smt: e.activation(
                                    out=y1[:, 0:256], in_=ob[:, 0:256], func=AF.Copy, scale=smt[:, 0:1]), [obb, smb], [y1b])
                                continue
                            y2, y2b = Y2.next()
                            if kind == "diff":
                                y1, y1b = y1t[s], buf("y1t%d" % s)
                                p.op("vector", lambda e, smt=smt: e.tensor_tensor(out=smt[:, 1:2], in0=smt[:, 0:1], in1=lams[:, 5:6], op=ALU.mult),
                                     [smb, b_lam], [smb])
                                p.op("vector", lambda e, y2=y2, ob=ob, smt=smt, y1=y1: e.scalar_tensor_tensor(
                                    out=y2[:, 0:256], in0=ob[:, 0:256], scalar=smt[:, 1:2], in1=y1[:, 0:256], op0=ALU.mult, op1=ALU.add),
                                    [obb, smb, y1b], [y2b])
                            else:
                                p.op("scalar", lambda e, y2=y2, ob=ob, smt=smt: e.activation(
                                    out=y2[:, 0:128], in_=ob[:, 0:128], func=AF.Copy, scale=smt[:, 0:1]), [obb, smb], [y2b])
                            yq, yqb = YSQ.next()
                            p.op("vector", lambda e, yq=yq, y2=y2, VW=VW: e.tensor_tensor(out=yq[:, 0:VW], in0=y2[:, 0:VW], in1=y2[:, 0:VW], op=ALU.mult),
                                 [y2b], [yqb])
                            p.op("vector", lambda e, yq=yq, smt=smt, VW=VW: e.reduce_sum(out=smt[:, 2:3], in_=yq[:, 0:VW], axis=AX.X), [yqb, smb], [smb])
                            p.op("vector", lambda e, smt=smt, VW=VW: e.tensor_scalar(out=smt[:, 3:4], in0=smt[:, 2:3], scalar1=float(1.0 / VW),
                                                                                     scalar2=float(LN_EPS), op0=ALU.mult, op1=ALU.add), [smb], [smb])
                            p.op("scalar", lambda e, smt=smt: e.activation(out=smt[:, 4:5], in_=smt[:, 3:4], func=AF.Sqrt), [smb], [smb])
                            p.op("vector", lambda e, smt=smt: e.reciprocal(out=smt[:, 5:6], in_=smt[:, 4:5]), [smb], [smb])
                            yn, ynbuf = YNB.next()
                            p.op("vector", lambda e, yn=yn, y2=y2, smt=smt, VW=VW, gt_ap=gt_ap: e.scalar_tensor_tensor(
                                out=yn[:, 0:VW], in0=y2[:, 0:VW], scalar=smt[:, 5:6], in1=gt_ap[:, 0:VW], op0=ALU.mult, op1=ALU.mult),
                                [y2b, smb, b_g], [ynbuf])
                            for ec in range(VW // 128):
                                if s == 0:
                                    pass
                                tb_, tbb = tbk_sel[ec]
                                p.op("tensor", lambda e, tb_=tb_, yn=yn, ec=ec, s=s: e.matmul(
                                    tb_[:, (3 - s) * 128:(4 - s) * 128], lhsT=yn[:, ec * 128:(ec + 1) * 128], rhs=jmat_b[:], start=True, stop=True),
                                    [ynbuf, b_const], [tbb])
                        if kind == "diff" and j == 0:
                            continue
                        for ec in range(VW // 128):
                            tb_, tbb = tbk_sel[ec]
                            st2, sbf2 = STB.next()
                            evict(st2[:], tb_[:], [tbb], [sbf2])
                            r0 = yrow0 + ec * 128
                            p.op("sync", lambda e, st2=st2, r0=r0, q0=q0: e.dma_start(out=yT[r0:r0 + 128, q0:q0 + 512], in_=st2[:]),
                                 [sbf2], [], dma=sbf2)
            p.barrier()

        tbk_sel = [BK[6], BK[7]]

        def proj_residual(wc, b_wc, nk, act_sb, b_act, ntok, T0, gcol, res_src, dst, wslot, wb, NW, ksplit, hfs=None):
            bk = Ring(BK[0:4])
            kper = nk // ksplit
            wi = proj_wi[0]
            for ct in range(D // NW):
                c0 = ct * NW
                slots = []
                for ks in range(ksplit):
                    ws, wsb = wslot[wi % len(wslot)], wb[wi % len(wslot)]
                    wi += 1
                    p.op("gpsimd", lambda e, ws=ws, ct=ct, ks=ks: e.dma_start(
                        out=ws.rearrange("p k n -> p (k n)"), in_=wc[ct * ksplit + ks]), [b_wc], [wsb], dma=wsb)
                    slots.append((ws, wsb))
                units = [(jc, hf) for jc in range(NW // 128) for hf in (hfs if hfs is not None else range(ntok // 512))]
                ubanks = [bk.next() for _ in units]
                for ks in range(ksplit):
                    ws, wsb = slots[ks]
                    for (jc, hf), (bank, bb) in zip(units, ubanks):
                        pairs = [(bank[:], ws[:, ii, jc * 128:(jc + 1) * 128], act_sb(ks * kper + ii, hf)) for ii in range(kper)]
                        mm_group(bank, pairs, [b_act, wsb], [bb], first=(ks == 0), last=(ks == ksplit - 1))
                for (jc, hf), (bank, bb) in zip(units, ubanks):
                    j = c0 // 128 + jc
                    xs, xsb = XR.next()
                    tt = T0 + hf * 512
                    p.op("sync", lambda e, xs=xs, j=j, tt=tt: e.dma_start(out=xs[:], in_=res_src[j * 128:(j + 1) * 128, tt:tt + 512]),
                         [], [xsb], dma=xsb)
                    st, sbf = STF.next()
                    p.op("vector", lambda e, st=st, bank=bank, j=j, xs=xs: e.scalar_tensor_tensor(
                        out=st[:], in0=bank[:], scalar=modt[:, gcol + j:gcol + j + 1], in1=xs[:], op0=ALU.mult, op1=ALU.add),
                        [bb, xsb, b_mod], [sbf])
                    p.op("sync", lambda e, st=st, j=j, tt=tt: e.dma_start(out=dst[j * 128:(j + 1) * 128, tt:tt + 512], in_=st[:]),
                         [sbf], [], dma=sbf)
            proj_wi[0] = wi

        proj_wi = [0]

        def phase_wo(l, cur, z1, tstart=0):
            yS = aview(0, KC * TBA * 2, BF16, "p (k t) -> p k t", t=TBA)
            b_y = buf("yS")
            WOFF = KC * TBA * 2
            wslot = [aview(WOFF + i * KC * NWB * 2, KC * NWB * 2, BF16, "p (k n) -> p k n", n=NWB) for i in range(2)]
            wb = bufs("wsl", 2)
            for tb in range(S // TBA):
                T0 = tb * TBA
                if T0 + TBA <= tstart:
                    continue
                p.op("sync", lambda e, T0=T0: e.dma_start(out=yS, in_=yT[:, T0:T0 + TBA].rearrange("(k p) t -> p k t", p=128)),
                     [], [b_y], dma=buf("d_yS"))
                proj_residual(wc_o[l], buf("wc_o%d" % l), KC, lambda kc, hf: yS[:, kc, hf * 512:(hf + 1) * 512], b_y, TBA, T0,
                              2 * KC, cur, z1, wslot, wb, NWB, 1, hfs=[hf for hf in range(TBA // 512) if T0 + hf * 512 >= tstart])
                p.barrier()

        def phase_ffn(l, z1, x1, z2, tstart=0):
            u2 = aview(0, KC * 512 * 2, BF16, "p (k t) -> p k t", t=512)
            b_u2 = buf("u2")
            hoff = KC * 512 * 2
            hT = aview(hoff, FC * 512 * 2, BF16, "p (k t) -> p k t", t=512)
            b_h = buf("hT")
            WOFF = hoff + FC * 512 * 2
            WSZ = KC * NWB * 2
            wslot_up = [aview(WOFF + i * WSZ, KC * NWB * 2, BF16, "p (k n) -> p k n", n=NWB) for i in range(2)]
            wslot_dn = [aview(WOFF + i * WSZ, KPER * NWD * 2, BF16, "p (k n) -> p k n", n=NWD) for i in range(2)]
            assert WOFF + 2 * WSZ <= C.ARENA, (WOFF + 2 * WSZ)
            wb = bufs("wsl", 2)
            x1b = bufs("x1c", KC)
            bk = Ring(BK[0:4])
            e1 = LN_EPS / (C.alpha ** 2)
            for tb in range(tstart // 512, S // 512):
                t0 = tb * 512
                ln_stats(z1, t0, e1, mean_t, rstd_t, b_mean, b_rstd)
                st2 = ln_stats_begin()
                ln_apply(z1, t0, mean_t, rstd_t, b_mean, b_rstd, 0, KC, (vect, b_vec), dst_dram=x1, stats2=st2, dst_bufs=x1b)
                stats_finish(st2[0], st2[1], LN_EPS, mean2_t, rstd2_t, b_mean2, b_rstd2)
                ln_apply(x1, t0, mean2_t, rstd2_t, b_mean2, b_rstd2, 4 * KC, 3 * KC, (modt, b_mod),
                         dst_sb=lambda kc: u2[:, kc, :], dst_sb_buf=b_u2, src_bufs=x1b)
                p.barrier()
                bk8 = Ring(BK[0:8])
                for fg in range(0, FC, PER):
                    nch = min(PER, FC - fg)
                    slots = []
                    for half in range(2):
                        ti = (fg // PER) * 2 + half
                        ws, wsb = wslot_up[proj_wi[0] % 2], wb[proj_wi[0] % 2]
                        proj_wi[0] += 1
                        p.op("gpsimd", lambda e, ws=ws, ti=ti: e.dma_start(
                            out=ws.rearrange("p k n -> p (k n)"), in_=wc_up[l][ti]), [buf("wc_up%d" % l)], [wsb], dma=wsb)
                        slots.append((ws, wsb))
                    gb, vb = [], []
                    for jc in range(nch):
                        bg, bgb = bk8.next()
                        mm_group(bg, [(bg[:], slots[0][0][:, kc, jc * 128:(jc + 1) * 128], u2[:, kc, :]) for kc in range(KC)],
                                 [b_u2, slots[0][1]], [bgb])
                        gb.append((bg, bgb))
                    for jc in range(nch):
                        bv, bvb = bk8.next()
                        mm_group(bv, [(bv[:], slots[1][0][:, kc, jc * 128:(jc + 1) * 128], u2[:, kc, :]) for kc in range(KC)],
                                 [b_u2, slots[1][1]], [bvb])
                        vb.append((bv, bvb))
                    for jc in range(nch):
                        fc = fg + jc
                        bg, bgb = gb[jc]
                        bv, bvb = vb[jc]
                        ge, geb = GEXT.next()
                        p.op("vector", lambda e, ge=ge, fc=fc, tb=tb: e.tensor_scalar(out=ge[:, 0:2], in0=halo[:, fc, :], scalar1=hmask_t[:, tb:tb + 1], scalar2=None, op0=ALU.mult), [b_halo, b_const], [geb])
                        p.op("scalar", lambda e, ge=ge, bg=bg: e.activation(out=ge[:, 2:514], in_=bg[:], func=AF.Copy), [bgb], [geb])
                        p.op("vector", lambda e, ge=ge, fc=fc: e.tensor_copy(out=halo[:, fc, :], in_=ge[:, 512:514]), [geb], [b_halo])
                        a_, ab = CA.next()
                        p.op("scalar", lambda e, a_=a_, ge=ge, fc=fc: e.activation(
                            out=a_[:], in_=ge[:, 0:512], func=AF.Identity, scale=convt[:, fc:fc + 1], bias=convt[:, 3 * FC + fc:3 * FC + fc + 1]),
                            [geb, b_vec], [ab])
                        b2, b2b = CB.next()
                        p.op("vector", lambda e, b2=b2, ge=ge, fc=fc, a_=a_: e.scalar_tensor_tensor(
                            out=b2[:], in0=ge[:, 1:513], scalar=convt[:, FC + fc:FC + fc + 1], in1=a_[:], op0=ALU.mult, op1=ALU.add),
                            [geb, ab, b_vec], [b2b])
                        a2, a2b = CA.next()
                        p.op("vector", lambda e, a2=a2, ge=ge, fc=fc, b2=b2: e.scalar_tensor_tensor(
                            out=a2[:], in0=ge[:, 2:514], scalar=convt[:, 2 * FC + fc:2 * FC + fc + 1], in1=b2[:], op0=ALU.mult, op1=ALU.add),
                            [geb, b2b, b_vec], [a2b])
                        b3, b3b = CB.next()
                        p.op("scalar", lambda e, b3=b3, a2=a2: e.activation(out=b3[:], in_=a2[:], func=AF.Silu), [a2b], [b3b])
                        p.op("vector", lambda e, fc=fc, b3=b3, bv=bv: e.tensor_tensor(out=hT[:, fc, :], in0=b3[:], in1=bv[:], op=ALU.mult),
                             [b3b, bvb], [b_h])
                proj_residual(wc_dn[l], buf("wc_dn%d" % l), FC, lambda kc, hf: hT[:, kc, :], b_h, 512, t0, 5 * KC, x1, z2,
                              wslot_dn, wb, NWD, KSPLIT)
                p.barrier()

        def phase_ln_out(l, z2, x2, tstart=0):
            e1 = LN_EPS / (C.alpha ** 2)
            for tb in range(tstart // 512, S // 512):
                t0 = tb * 512
                ln_stats(z2, t0, e1, mean_t, rstd_t, b_mean, b_rstd)
                ln_apply(z2, t0, mean_t, rstd_t, b_mean, b_rstd, 2 * KC, 3 * KC, (vect, b_vec), dst_dram=x2)
            p.barrier()

        def phase_transpose_out(src):
            bk = Ring(BK[0:4])
            xo = [aview(i * D * 4, D * 4, F32) for i in range(2)]
            xob = bufs("xo", 2)
            for g in range((S - SEG) // 512, S // 512):
                tiles = []
                for kc in range(KC):
                    pass
                for i4 in range(4):
                    pass
                for i in range(4):
                    ot, otb = xo[i % 2], xob[i % 2]
                    for kg in range(0, KC, 4):
                        bank, bb = bk.next()
                        srcs = []
                        for kk in range(4):
                            kc = kg + kk
                            xs, xsb = XR.next()
                            p.op("sync", lambda e, xs=xs, kc=kc, g=g, i=i: e.dma_start(
                                out=xs[:, 0:128], in_=src[kc * 128:(kc + 1) * 128, g * 512 + i * 128:g * 512 + (i + 1) * 128]),
                                [], [xsb], dma=xsb)
                            srcs.append((xs, xsb))

                        def fn(e, bank=bank, srcs=srcs):
                            ins = None
                            for kk in range(4):
                                ins = e.transpose(bank[:, kk * 128:(kk + 1) * 128], srcs[kk][0][:, 0:128], ident_f[:])
                            return ins
                        p.op("tensor", fn, [s_[1] for s_ in srcs] + [b_const], [bb])
                        evict(ot[:, kg * 128:(kg + 4) * 128], bank[:], [bb], [otb])
                    r0 = g * 512 + i * 128 - (S - SEG)
                    p.op("sync", lambda e, ot=ot, r0=r0: e.dma_start(out=out[r0:r0 + 128, :], in_=ot), [otb], [], dma=otb)
            p.barrier()

        conv_w_in(0)
        phase_transpose_in()
        cur = xT0
        for l in range(L):
            z1, x1, z2, x2 = zs[l]
            ts_l = TPART if l == L - 1 else 0
            phase_params(l)
            phase_qkv(l, cur, ts_l)
            conv_w_rest(l)
            if l + 1 < L:
                conv_w_in(l + 1)
            phase_attn(l, ts_l)
            phase_wo(l, cur, z1, ts_l)
            phase_ffn(l, z1, x1, z2, ts_l)
            phase_ln_out(l, z2, x2, (S - SEG) if l == L - 1 else 0)
            cur = x2
            if C.stop_after is not None and l == C.stop_after:
                break
        phase_transpose_out(cur)

        with nc.Block() as block:
            p.replay(block)
    return nc


def host_inputs(C, b, inp, j=None):
    S, D, KC, FC, L = C.S, C.D, C.KC, C.FC, C.depth
    nseg = S // C.SEG
    if j is None:
        j = nseg - 1
    npad = (nseg - 1 - j) * C.SEG
    f = lambda a: np.ascontiguousarray(a, dtype=np.float32)
    fm = lambda v, n: f(v.reshape(n, 128).T)
    d = {}
    xw = np.zeros((S, D), np.float32)
    xw[npad:] = inp["x"][b][0:S - npad]
    d["x"] = xw
    tok = np.arange(S)
    d["kmask_fm"] = f(np.where(tok < npad, NEG, 0.0).reshape(S // 128, 128).T)
    hm = np.where(np.arange(S // 512) * 512 <= npad, 0.0, 1.0).astype(np.float32)
    d["hmask"] = f(np.broadcast_to(hm[None, :], (128, S // 512)))
    d["c_fm"] = fm(inp["c"][b], KC)
    d["w_ada"] = f(inp["w_ada"])
    d["b_ada_fm"] = f(np.stack([fm(inp["b_ada"][l], 6 * KC) for l in range(L)]))
    d["w_in"] = f(inp["w_in"])
    d["lamv"] = f(np.stack([np.concatenate([inp["lambda_q1"][l], inp["lambda_k1"][l], inp["lambda_q2"][l], inp["lambda_k2"][l]])
                            for l in range(L)]))
    d["g_diff"] = f(inp["g_diff"])
    d["g_dil"] = f(inp["g_dil"])
    d["w_o"] = f(inp["w_o"])
    d["vec_fm"] = f(np.stack([np.concatenate([fm(inp[k][l], KC) for k in ("ln1_g", "ln1_b", "ln2_g", "ln2_b")], axis=1)
                              for l in range(L)]))
    d["w_up"] = f(inp["w_up"])
    d["conv_fm"] = f(np.stack([np.concatenate([fm(inp["conv_w"][l][0], FC), fm(inp["conv_w"][l][1], FC),
                                               fm(inp["conv_w"][l][2], FC), fm(inp["conv_b"][l], FC)], axis=1) for l in range(L)]))
    d["w_down"] = f(inp["w_down"])
    dist = DMAXT - np.arange(TL)
    idx = t5_bucket_np(dist)
    d["tabT"] = f(inp["rel_bias"][idx, :].T)
    cm = np.zeros((2, TL), np.float32)
    cm[0] = np.where(dist >= 0, 0.0, NEG)
    mult = ((dist >= 0) & (dist <= 128)).astype(np.float64) + ((dist >= 0) & (dist % 4 == 0) & (dist <= 512)) \
        + ((dist >= 0) & (dist % 16 == 0) & (dist <= 2048))
    cm[1] = np.where(mult > 0, np.log(np.maximum(mult, 1.0)), NEG)
    d["cmask"] = cm
    d["ident"] = np.eye(128, dtype=np.float32)
    d["jmat"] = f(np.eye(128, dtype=np.float32)[::-1])
    return d


_NC_CACHE = {}


def kernel(**inputs):
    C = Cfg()
    if "nc" not in _NC_CACHE:
        _NC_CACHE["nc"] = build(C)
    nc = _NC_CACHE["nc"]
    inp = {k: np.asarray(v) for k, v in inputs.items()}
    nseg = C.S // C.SEG
    shared = {}
    in_maps = []
    for i in range(8):
        d = host_inputs(C, i // nseg, inp, i % nseg)
        for k in list(d.keys()):
            if k in ("w_ada", "w_in", "w_o", "w_up", "w_down"):
                d[k] = shared.setdefault(k, d[k])
        in_maps.append(d)
    res = run_bass_kernel_spmd(nc, in_maps, core_ids=list(range(8)))
    outp = np.empty((2, C.S, C.D), np.float32)
    for i in range(8):
        b, j = i // nseg, i % nseg
        outp[b, j * C.SEG:(j + 1) * C.SEG] = res.results[i]["out"]
    return outp
```

```python
import math
from contextlib import ExitStack

import numpy as np
import concourse.bass as bass
import concourse.mybir as mybir
from concourse.bass_utils import run_bass_kernel_spmd

F32 = mybir.dt.float32
BF16 = mybir.dt.bfloat16
AF = mybir.ActivationFunctionType
ALU = mybir.AluOpType
AX = mybir.AxisListType

ENGS = ["sync", "scalar", "vector", "gpsimd", "tensor"]
LN_EPS = 1e-5
NEG = -30000.0
DMAXT = 2559
TL = 3072
TW = 2944
DCL = 2175
SAME_SYNC = True


class Cfg:
    def __init__(self, S=4096, D=4096, depth=2, TBA=1024, NWB=256):
        self.S, self.D, self.depth = S, D, depth
        self.DW = D // 2
        self.LW = D - self.DW
        self.NDH = self.DW // 256
        self.NLH = self.LW // 128
        self.F = 256 * ((8 * D // 3 + 255) // 256)
        self.KC = D // 128
        self.FC = self.F // 128
        self.TBA = min(TBA, S)
        self.NWB = NWB
        self.INW = 3 * self.DW + 3 * self.LW
        self.alpha = (2 * depth) ** 0.25
        self.NM = 2 * self.NDH + self.NLH
        self.ARENA = 150 * 1024
        self.SEG = min(1024, S)
        self.EXT = 512
        self.stop_after = None
        self.debug = None


class Buf:
    __slots__ = ("name", "w", "r")

    def __init__(self, name):
        self.name = name
        self.w = None
        self.r = {}


class Prog:
    def __init__(self, nc, stack):
        self.nc, self.stack = nc, stack
        self.ops = {e: [] for e in ENGS}
        self.sems, self.cnt = {}, {}
        self.waited = {e: {} for e in ENGS}
        self.live = {}
        self.nsem = 0
        self.epoch = 0

    def _sem(self, key):
        if key not in self.sems:
            self.sems[key] = self.stack.enter_context(self.nc.semaphore("s%d" % self.nsem))
            self.nsem += 1
            self.cnt[key] = 0
        return self.sems[key]

    def op(self, eng, fn, reads=(), writes=(), dma=None):
        waits = {}

        def need(ev):
            if ev is None:
                return
            k, v = ev
            if dma is None and k[0] == "eng" and k[1] == eng and (eng == "tensor" or not SAME_SYNC):
                return
            if v > waits.get(k, 0):
                waits[k] = v

        for b in reads:
            need(b.w)
        for b in writes:
            need(b.w)
            for k, v in b.r.items():
                need((k, v))
        wl = []
        for k, v in waits.items():
            if self.waited[eng].get(k, 0) >= v:
                continue
            self.waited[eng][k] = v
            wl.append((self._sem(k), v))
        key = ("dma", dma.name) if dma is not None else ("eng", eng, self.epoch)
        sem = self._sem(key)
        inc = 16 if dma is not None else 1
        self.cnt[key] += inc
        ev = (key, self.cnt[key])
        self.live[key] = self.cnt[key]
        self.ops[eng].append((fn, wl, sem, inc))
        for b in writes:
            b.w = ev
            b.r = {}
        for b in reads:
            if b.r.get(key, 0) < ev[1]:
                b.r[key] = ev[1]
        return ev

    def barrier(self):
        for e in ENGS:
            wl = []
            for k, v in self.live.items():
                if self.waited[e].get(k, 0) >= v:
                    continue
                self.waited[e][k] = v
                wl.append((self._sem(k), v))
            if wl:
                self.ops[e].append((None, wl, None, 0))
        self.live = {}
        for k in list(self.sems.keys()):
            if self.cnt[k] > 24000:
                if k[0] == "eng":
                    pass
                else:
                    del self.sems[k]
        if any(self.cnt.get(("eng", e, self.epoch), 0) > 24000 for e in ENGS):
            self.epoch += 1

    def replay(self, block):
        def mk(ename):
            lst = self.ops[ename]

            def body(e):
                for fn, wl, sem, inc in lst:
                    for s, v in wl:
                        e.wait_ge(s, v)
                    if fn is not None:
                        ins = fn(e)
                        ins.then_inc(sem, inc)
            return body

        block.sync(mk("sync"))
        block.scalar(mk("scalar"))
        block.vector(mk("vector"))
        block.gpsimd(mk("gpsimd"))
        block.tensor(mk("tensor"))


class Ring:
    def __init__(self, items):
        self.items = items
        self.i = 0

    def next(self):
        it = self.items[self.i % len(self.items)]
        self.i += 1
        return it


def t5_bucket_np(d):
    n = np.maximum(d, 0)
    max_exact = 16
    nf = np.maximum(n, max_exact).astype(np.float32)
    large = max_exact + (np.log(nf / max_exact) / math.log(2048 / max_exact) * (32 - max_exact)).astype(np.int32)
    large = np.minimum(large, 31)
    return np.where(n < max_exact, n, large)


def build(C):
    nc = bass.Bass("TRN2", target_bir_lowering=False)
    S, D, F, KC, FC, L = C.S, C.D, C.F, C.KC, C.FC, C.depth
    NDH, NLH, NM, DW, LW, INW = C.NDH, C.NLH, C.NM, C.DW, C.LW, C.INW
    TBA, NWB = C.TBA, C.NWB
    NT5 = S // 512
    QA0, KA0, VA0, QB0, KB0, VB0 = 0, DW, 2 * DW, 3 * DW, 3 * DW + LW, 3 * DW + 2 * LW

    def din(name, shape):
        return nc.dram_tensor(name, list(shape), F32, kind="ExternalInput").ap()

    x_in = din("x", [S, D])
    c_fm = din("c_fm", [128, KC])
    w_ada = din("w_ada", [L, D, 6 * D])
    b_ada_fm = din("b_ada_fm", [L, 128, 6 * KC])
    w_in = din("w_in", [L, D, INW])
    lamv = din("lamv", [L, 4 * 128])
    g_diff = din("g_diff", [L, 256])
    g_dil = din("g_dil", [L, 128])
    w_o = din("w_o", [L, D, D])
    vec_fm = din("vec_fm", [L, 128, 4 * KC])
    w_up = din("w_up", [L, D, 2 * F])
    conv_fm = din("conv_fm", [L, 128, 4 * FC])
    w_down = din("w_down", [L, F, D])
    tabT = din("tabT", [NDH + NLH, TL])
    cmask = din("cmask", [2, TL])
    ident_in = din("ident", [128, 128])
    jmat_in = din("jmat", [128, 128])
    kmask_in = din("kmask_fm", [128, S // 128])
    hmask_in = din("hmask", [128, S // 512])
    SEG = C.SEG
    TPART = max(0, S - SEG - C.EXT)
    out = nc.dram_tensor("out", [SEG, D], F32, kind="ExternalOutput").ap()

    def dscr(name, shape, dt):
        return nc.dram_tensor(name, list(shape), dt).ap()

    xT0 = dscr("xT0", [D, S], F32)
    zs = [[dscr("r%d_%d" % (l, i), [D, S], F32) for i in range(4)] for l in range(L)]
    qrT = dscr("qrT", [NM * 128, S], BF16)
    kTd = dscr("kTd", [NM * 128, S], BF16)
    Vd = dscr("Vd", [S, D], BF16)
    yT = dscr("yT", [D, S], BF16)

    PER = NWB // 128
    NUPT = 2 * ((FC + PER - 1) // PER)
    KSPLIT = 2 if FC % 2 == 0 else 1
    KPER = FC // KSPLIT
    NWD = 256 if KPER * 256 * 2 <= KC * NWB * 2 else 128
    wc_in = [dscr("wc_in%d" % l, [INW // NWB, 128, KC * NWB], BF16) for l in range(L)]
    wc_o = [dscr("wc_o%d" % l, [D // NWB, 128, KC * NWB], BF16) for l in range(L)]
    wc_up = [dscr("wc_up%d" % l, [NUPT, 128, KC * NWB], BF16) for l in range(L)]
    wc_dn = [dscr("wc_dn%d" % l, [(D // NWD) * KSPLIT, 128, KPER * NWD], BF16) for l in range(L)]

    stack = ExitStack()
    with stack:
        p = Prog(nc, stack)
        sb = lambda name, shape, dt: stack.enter_context(nc.sbuf_tensor(name, list(shape), dt))
        ident_b = sb("ident_b", [128, 128], BF16)
        jmat_b = sb("jmat_b", [128, 128], BF16)
        ident_f = sb("ident_f", [128, 128], F32)
        ones_f = sb("ones_f", [128, 128], F32)
        modt = sb("modt", [128, 6 * KC], F32)
        badat = sb("badat", [128, 6 * KC], F32)
        vect = sb("vect", [128, 4 * KC], F32)
        convt = sb("convt", [128, 4 * FC], F32)
        halo = sb("halo", [128, FC, 2], F32)
        cfm_t = sb("cfm_t", [128, KC], F32)
        csil = sb("csil", [128, KC], BF16)
        lamb = sb("lamb", [128, 4 * 128], F32)
        lamp = sb("lamp", [128, 2 * 128], F32)
        lams = sb("lams", [128, 8], F32)
        gdf = sb("gdf", [128, 256], F32)
        gdl = sb("gdl", [128, 128], F32)
        kmask_t = sb("kmask_t", [128, S // 128], F32)
        hmask_t = sb("hmask_t", [128, S // 512], F32)
        NXR = 4
        xr = [sb("xr%d" % i, [128, 512], F32) for i in range(NXR)]
        stf = [sb("stf%d" % i, [128, 512], F32) for i in range(3)]
        stb = [sb("stb%d" % i, [128, 512], BF16) for i in range(8)]
        ptr = [sb("ptr%d" % i, [128, 512], BF16) for i in range(4)]
        sm = [sb("sm%d" % i, [128, 8], F32) for i in range(8)]
        arena2 = sb("arena2", [128, 4096], F32)

        def a2(off_bytes, nbytes, dt):
            a = arena2[:, off_bytes // 4:(off_bytes + nbytes) // 4]
            return a.bitcast(BF16) if dt == BF16 else a
        sqr = [a2(i * 2048, 2048, F32) for i in range(2)]
        t1r = [a2(4096 + i * 2048, 2048, F32) for i in range(2)]
        mean_t, rstd_t, mean2_t, rstd2_t = [a2(8192 + i * 2048, 2048, F32) for i in range(4)]
        gext = [a2(i * 2056, 2056, F32) for i in range(2)]
        ca = [a2(4112 + i * 2048, 2048, F32) for i in range(2)]
        cb_ = [a2(8208 + i * 2048, 2048, F32) for i in range(2)]
        y1t = [a2(i * 1024, 1024, F32) for i in range(4)]
        y2t = [a2(4096 + i * 1024, 1024, F32) for i in range(2)]
        ysq = [a2(6144 + i * 1024, 1024, F32) for i in range(2)]
        ynb = [a2(8192 + i * 512, 512, BF16) for i in range(4)]
        ARW = C.ARENA // 4
        arena = sb("arena", [128, ARW], F32)
        banks = [stack.enter_context(nc.psum_tensor("bank%d" % i, [128, 512], F32)) for i in range(8)]

        B = {}

        def buf(name):
            if name not in B:
                B[name] = Buf(name)
            return B[name]

        def bufs(prefix, n):
            return [buf("%s%d" % (prefix, i)) for i in range(n)]

        def ring_of(tiles, prefix):
            return Ring(list(zip(tiles, bufs(prefix, len(tiles)))))

        XR = ring_of(xr, "xr")
        SQR = ring_of(sqr, "sqr")
        T1R = ring_of(t1r, "t1r")
        STF = ring_of(stf, "stf")
        STB = ring_of(stb, "stb")
        PTR = ring_of(ptr, "ptr")
        GEXT = ring_of(gext, "gext")
        CA = ring_of(ca, "ca")
        CB = ring_of(cb_, "cb")
        SM = ring_of(sm, "sm")
        Y1 = ring_of(y1t, "y1t")
        Y2 = ring_of(y2t, "y2t")
        YSQ = ring_of(ysq, "ysq")
        YNB = ring_of(ynb, "ynb")
        BK = [(banks[i], buf("bank%d" % i)) for i in range(8)]
        b_const = buf("const")
        b_mod = buf("mod")
        b_vec = buf("vec")
        b_halo = buf("halo")
        b_lam = buf("lam")
        b_g = buf("g")
        b_mean, b_rstd = buf("mean"), buf("rstd")
        b_mean2, b_rstd2 = buf("mean2"), buf("rstd2")

        def aview(off_bytes, nbytes, dt, pattern=None, **kw):
            a = arena[:, off_bytes // 4:(off_bytes + nbytes) // 4]
            if dt == BF16:
                a = a.bitcast(BF16)
            if pattern:
                a = a.rearrange(pattern, **kw)
            return a

        evict_flip = [0]

        def evict(out_ap, in_ap, reads, writes, scale=None):
            evict_flip[0] ^= 1
            if evict_flip[0]:
                if scale is None:
                    p.op("scalar", lambda e: e.activation(out=out_ap, in_=in_ap, func=AF.Copy), reads, writes)
                else:
                    p.op("scalar", lambda e: e.activation(out=out_ap, in_=in_ap, func=AF.Copy, scale=scale), reads, writes)
            else:
                if scale is None:
                    p.op("vector", lambda e: e.tensor_copy(out=out_ap, in_=in_ap), reads, writes)
                else:
                    p.op("vector", lambda e: e.tensor_scalar(out=out_ap, in0=in_ap, scalar1=scale, scalar2=None, op0=ALU.mult), reads, writes)

        def mm_group(bank, pairs, reads, writes, first=True, last=True):
            n = len(pairs)

            def fn(e):
                ins = None
                for i, (o, l_, r_) in enumerate(pairs):
                    ins = e.matmul(o, lhsT=l_, rhs=r_, start=(first and i == 0), stop=(last and i == n - 1))
                return ins
            p.op("tensor", fn, reads, writes)

        p.op("gpsimd", lambda e: e.dma_start(out=ident_b[:], in_=ident_in[:, :]), [], [b_const], dma=buf("d_identb"))
        p.op("gpsimd", lambda e: e.dma_start(out=jmat_b[:], in_=jmat_in[:, :]), [], [b_const], dma=buf("d_jb"))
        p.op("sync", lambda e: e.dma_start(out=ident_f[:], in_=ident_in[:, :]), [], [b_const], dma=buf("d_identf"))
        p.op("sync", lambda e: e.dma_start(out=cfm_t[:], in_=c_fm[:, :]), [], [b_const], dma=buf("d_cfm"))
        p.op("vector", lambda e: e.memset(ones_f[:], 1.0), [], [b_const])
        p.op("sync", lambda e: e.dma_start(out=kmask_t[:], in_=kmask_in[:, :]), [], [b_const], dma=buf("d_kmask"))
        p.op("sync", lambda e: e.dma_start(out=hmask_t[:], in_=hmask_in[:, :]), [], [b_const], dma=buf("d_hmask"))
        p.op("scalar", lambda e: e.activation(out=csil[:], in_=cfm_t[:], func=AF.Silu), [b_const], [b_const])
        p.barrier()

        def conv_w_in(l):
            b = buf("wc_in%d" % l)
            for wt in range(INW // NWB):
                c0 = wt * NWB
                p.op("gpsimd", lambda e, wt=wt, c0=c0: e.dma_start(
                    out=wc_in[l][wt].rearrange("p (k n) -> p k n", n=NWB),
                    in_=w_in[l, :, c0:c0 + NWB].rearrange("(k p) n -> p k n", p=128)), [], [b], dma=b)

        def conv_w_rest(l):
            b = buf("wc_o%d" % l)
            for wt in range(D // NWB):
                c0 = wt * NWB
                p.op("gpsimd", lambda e, wt=wt, c0=c0: e.dma_start(
                    out=wc_o[l][wt].rearrange("p (k n) -> p k n", n=NWB),
                    in_=w_o[l, :, c0:c0 + NWB].rearrange("(k p) n -> p k n", p=128)), [], [b], dma=b)
            b = buf("wc_up%d" % l)
            for ti in range(NUPT):
                fg, half = (ti // 2) * PER, ti % 2
                nch = min(PER, FC - fg)
                c0 = half * F + fg * 128
                p.op("gpsimd", lambda e, ti=ti, c0=c0, nch=nch: e.dma_start(
                    out=wc_up[l][ti].rearrange("p (k n) -> p k n", n=NWB)[:, :, 0:nch * 128],
                    in_=w_up[l, :, c0:c0 + nch * 128].rearrange("(k p) n -> p k n", p=128)), [], [b], dma=b)
            b = buf("wc_dn%d" % l)
            for ct in range(D // NWD):
                for ks in range(KSPLIT):
                    k_lo = ks * KPER * 128
                    p.op("gpsimd", lambda e, ct=ct, ks=ks, k_lo=k_lo: e.dma_start(
                        out=wc_dn[l][ct * KSPLIT + ks].rearrange("p (k n) -> p k n", n=NWD),
                        in_=w_down[l, k_lo:k_lo + KPER * 128, ct * NWD:(ct + 1) * NWD].rearrange("(k p) n -> p k n", p=128)),
                        [], [b], dma=b)

        def phase_transpose_in():
            xt_tiles = [aview(i * D * 4, D * 4, F32) for i in range(4)]
            xt_b = bufs("xt", 4)
            bk = Ring(BK[0:4])
            for g in range(S // 512):
                for i in range(4):
                    r0 = g * 512 + i * 128
                    p.op("sync", lambda e, i=i, r0=r0: e.dma_start(out=xt_tiles[i], in_=x_in[r0:r0 + 128, :]),
                         [], [xt_b[i]], dma=xt_b[i])
                for kc in range(KC):
                    bank, bb = bk.next()

                    def fn(e, kc=kc, bank=bank):
                        ins = None
                        for i in range(4):
                            ins = e.transpose(bank[:, i * 128:(i + 1) * 128], xt_tiles[i][:, kc * 128:(kc + 1) * 128], ident_f[:])
                        return ins
                    p.op("tensor", fn, xt_b + [b_const], [bb])
                    st, sbf = STF.next()
                    evict(st[:], bank[:], [bb], [sbf])
                    p.op("sync", lambda e, st=st, kc=kc, g=g: e.dma_start(out=xT0[kc * 128:(kc + 1) * 128, g * 512:(g + 1) * 512], in_=st[:]),
                         [sbf], [], dma=sbf)
            p.barrier()

        def ln_stats_begin():
            return BK[4], BK[5]

        def stats_accum(xs, xsb, kc, bs, bq):
            sq, sqb = SQR.next()
            p.op("scalar", lambda e: e.activation(out=sq, in_=xs[:], func=AF.Square), [xsb], [sqb])
            p.op("tensor", lambda e: e.matmul(bs[0][:], lhsT=ones_f[:], rhs=xs[:], start=(kc == 0), stop=(kc == KC - 1)),
                 [xsb, b_const], [bs[1]])
            p.op("tensor", lambda e: e.matmul(bq[0][:], lhsT=ones_f[:], rhs=sq, start=(kc == 0), stop=(kc == KC - 1)),
                 [sqb, b_const], [bq[1]])

        def stats_finish(bs, bq, eps, mean_ap, rstd_ap, bm, br):
            msq, b_msq = SQR.next()
            var, b_var = SQR.next()
            p.op("scalar", lambda e: e.activation(out=mean_ap, in_=bs[0][:], func=AF.Copy, scale=1.0 / D), [bs[1]], [bm])
            p.op("vector", lambda e: e.tensor_tensor(out=msq, in0=mean_ap, in1=mean_ap, op=ALU.mult), [bm], [b_msq])
            p.op("vector", lambda e: e.scalar_tensor_tensor(out=var, in0=bq[0][:], scalar=1.0 / D, in1=msq,
                                                            op0=ALU.mult, op1=ALU.subtract), [bq[1], b_msq], [b_var])
            p.op("vector", lambda e: e.tensor_scalar(out=var, in0=var, scalar1=float(eps), scalar2=None, op0=ALU.add),
                 [b_var], [b_var])
            p.op("scalar", lambda e: e.activation(out=var, in_=var, func=AF.Sqrt), [b_var], [b_var])
            p.op("vector", lambda e: e.reciprocal(out=rstd_ap, in_=var), [b_var], [br])

        def ln_stats(src, t0, eps, mean_ap, rstd_ap, bm, br):
            bs, bq = ln_stats_begin()
            for kc in range(KC):
                xs, xsb = XR.next()
                p.op("sync", lambda e, xs=xs, kc=kc: e.dma_start(out=xs[:], in_=src[kc * 128:(kc + 1) * 128, t0:t0 + 512]),
                     [], [xsb], dma=xsb)
                stats_accum(xs, xsb, kc, bs, bq)
            stats_finish(bs, bq, eps, mean_ap, rstd_ap, bm, br)

        def ln_apply(src, t0, mean_ap, rstd_ap, bm, br, sc_col, bi_col, sbuf_for, dst_dram=None, dst_sb=None, dst_sb_buf=None,
                     stats2=None, src_bufs=None, dst_bufs=None):
            for kc in range(KC):
                xs, xsb = XR.next()
                rd = [src_bufs[kc]] if src_bufs is not None else []
                p.op("sync", lambda e, xs=xs, kc=kc: e.dma_start(out=xs[:], in_=src[kc * 128:(kc + 1) * 128, t0:t0 + 512]),
                     rd, [xsb], dma=xsb)
                t1, t1b = T1R.next()
                p.op("vector", lambda e, xs=xs, t1=t1: e.tensor_tensor(out=t1, in0=xs[:], in1=mean_ap, op=ALU.subtract),
                     [xsb, bm], [t1b])
                t2, t2b = t1, t1b
                p.op("vector", lambda e, t1=t1: e.tensor_tensor(out=t1, in0=t1, in1=rstd_ap, op=ALU.mult),
                     [t1b, br], [t1b])
                sc_ap = sbuf_for[0][:, sc_col + kc:sc_col + kc + 1]
                bi_ap = sbuf_for[0][:, bi_col + kc:bi_col + kc + 1]
                if dst_dram is not None:
                    st, sbf = STF.next()
                    p.op("scalar", lambda e, st=st, t2=t2, sc_ap=sc_ap, bi_ap=bi_ap: e.activation(
                        out=st[:], in_=t2, func=AF.Identity, scale=sc_ap, bias=bi_ap), [t2b, sbuf_for[1]], [sbf])
                    if stats2 is not None:
                        stats_accum(st, sbf, kc, stats2[0], stats2[1])
                    wr = [dst_bufs[kc]] if dst_bufs is not None else []
                    p.op("sync", lambda e, st=st, kc=kc: e.dma_start(out=dst_dram[kc * 128:(kc + 1) * 128, t0:t0 + 512], in_=st[:]),
                         [sbf], wr, dma=sbf)
                else:
                    o_ap = dst_sb(kc)
                    p.op("scalar", lambda e, o_ap=o_ap, t2=t2, sc_ap=sc_ap, bi_ap=bi_ap: e.activation(
                        out=o_ap, in_=t2, func=AF.Identity, scale=sc_ap, bias=bi_ap), [t2b, sbuf_for[1]], [dst_sb_buf])

        def phase_params(l):
            lam_init = 0.8 - 0.6 * math.exp(-0.3 * l)
            p.op("sync", lambda e: e.dma_start(out=badat[:], in_=b_ada_fm[l]), [], [b_mod], dma=buf("d_bada"))
            p.op("sync", lambda e: e.dma_start(out=vect[:], in_=vec_fm[l]), [], [b_vec], dma=buf("d_vec"))
            p.op("sync", lambda e: e.dma_start(out=convt[:], in_=conv_fm[l]), [], [b_vec], dma=buf("d_conv"))
            src = bass.AP(tensor=lamv.tensor, offset=l * 512, ap=[[0, 128], [1, 512]])
            p.op("sync", lambda e: e.dma_start(out=lamb[:], in_=src), [], [b_lam], dma=buf("d_lam"))
            srcg = bass.AP(tensor=g_diff.tensor, offset=l * 256, ap=[[0, 128], [1, 256]])
            p.op("sync", lambda e: e.dma_start(out=gdf[:], in_=srcg), [], [b_g], dma=buf("d_gdf"))
            srcl = bass.AP(tensor=g_dil.tensor, offset=l * 128, ap=[[0, 128], [1, 128]])
            p.op("sync", lambda e: e.dma_start(out=gdl[:], in_=srcl), [], [b_g], dma=buf("d_gdl"))
            p.op("vector", lambda e: e.memset(halo[:], 0.0), [], [b_halo])
            p.op("vector", lambda e: e.tensor_tensor(out=lamp[:, 0:128], in0=lamb[:, 0:128], in1=lamb[:, 128:256], op=ALU.mult), [b_lam], [b_lam])
            p.op("vector", lambda e: e.tensor_tensor(out=lamp[:, 128:256], in0=lamb[:, 256:384], in1=lamb[:, 384:512], op=ALU.mult), [b_lam], [b_lam])
            p.op("vector", lambda e: e.reduce_sum(out=lams[:, 0:1], in_=lamp[:, 0:128], axis=AX.X), [b_lam], [b_lam])
            p.op("vector", lambda e: e.reduce_sum(out=lams[:, 1:2], in_=lamp[:, 128:256], axis=AX.X), [b_lam], [b_lam])
            p.op("scalar", lambda e: e.activation(out=lams[:, 2:4], in_=lams[:, 0:2], func=AF.Exp), [b_lam], [b_lam])
            p.op("vector", lambda e: e.tensor_tensor(out=lams[:, 4:5], in0=lams[:, 2:3], in1=lams[:, 3:4], op=ALU.subtract), [b_lam], [b_lam])
            p.op("vector", lambda e: e.tensor_scalar(out=lams[:, 5:6], in0=lams[:, 4:5], scalar1=float(lam_init), scalar2=-1.0,
                                                     op0=ALU.add, op1=ALU.mult), [b_lam], [b_lam])
            p.op("vector", lambda e: e.tensor_scalar(out=gdf[:], in0=gdf[:], scalar1=float(1.0 - lam_init), scalar2=None, op0=ALU.mult),
                 [b_g], [b_g])
            NWM = 512
            wslot = [aview(i * KC * NWM * 2, KC * NWM * 2, BF16, "p (k n) -> p k n", n=NWM) for i in range(2)]
            wb = bufs("wsl", 2)
            bank, bb = BK[0]
            ntile = 6 * D // NWM
            for t in range(ntile):
                ws, wsb = wslot[t % 2], wb[t % 2]
                p.op("gpsimd", lambda e, ws=ws, t=t: e.dma_start(
                    out=ws, in_=w_ada[l, :, t * NWM:(t + 1) * NWM].rearrange("(k p) n -> p k n", p=128)), [], [wsb], dma=wsb)
                for j in range(NWM // 128):
                    col = t * (NWM // 128) + j
                    pairs = [(bank[:, col:col + 1], ws[:, kc, j * 128:(j + 1) * 128], csil[:, kc:kc + 1]) for kc in range(KC)]
                    mm_group(bank, pairs, [wsb, b_const], [bb])
            p.op("vector", lambda e: e.tensor_tensor(out=modt[:], in0=bank[:, 0:6 * KC], in1=badat[:], op=ALU.add), [bb, b_mod], [b_mod])
            for base in (KC, 4 * KC):
                p.op("vector", lambda e, base=base: e.tensor_scalar(out=modt[:, base:base + KC], in0=modt[:, base:base + KC],
                                                                    scalar1=1.0, scalar2=None, op0=ALU.add), [b_mod], [b_mod])
            for base in (2 * KC, 5 * KC):
                p.op("vector", lambda e, base=base: e.tensor_scalar(out=modt[:, base:base + KC], in0=modt[:, base:base + KC],
                                                                    scalar1=float(1.0 / C.alpha), scalar2=None, op0=ALU.mult), [b_mod], [b_mod])
            p.barrier()

        def phase_qkv(l, cur, tstart=0):
            NTB = S // TBA
            uT = aview(0, KC * TBA * 2, BF16, "p (k t) -> p k t", t=TBA)
            b_uT = buf("uT")
            WOFF = KC * TBA * 2
            wslot = [aview(WOFF + i * KC * NWB * 2, KC * NWB * 2, BF16, "p (k n) -> p k n", n=NWB) for i in range(2)]
            wb = bufs("wsl", 2)
            bk = Ring(BK[0:4])
            bk2 = Ring(BK[6:8])
            nwt = INW // NWB
            for tb in range(NTB):
                T0 = tb * TBA
                for sbk in range(TBA // 512):
                    t0 = T0 + sbk * 512
                    ln_stats(cur, t0, LN_EPS, mean_t, rstd_t, b_mean, b_rstd)
                    ln_apply(cur, t0, mean_t, rstd_t, b_mean, b_rstd, KC, 0, (modt, b_mod),
                             dst_sb=lambda kc, sbk=sbk: uT[:, kc, sbk * 512:(sbk + 1) * 512], dst_sb_buf=b_uT)
                for wt in range(nwt):
                    c0 = wt * NWB
                    ws, wsb = wslot[wt % 2], wb[wt % 2]
                    p.op("gpsimd", lambda e, ws=ws, wt=wt: e.dma_start(
                        out=ws.rearrange("p k n -> p (k n)"), in_=wc_in[l][wt]), [buf("wc_in%d" % l)], [wsb], dma=wsb)
                    if c0 < KA0:
                        kind, mbase = "q", (c0 - QA0) // 128
                    elif c0 < VA0:
                        kind, mbase = "k", (c0 - KA0) // 128
                    elif c0 < QB0:
                        kind, vcol = "v", c0 - VA0
                    elif c0 < KB0:
                        kind, mbase = "q", 2 * NDH + (c0 - QB0) // 128
                    elif c0 < VB0:
                        kind, mbase = "k", 2 * NDH + (c0 - KB0) // 128
                    else:
                        kind, vcol = "v", DW + (c0 - VB0)
                    for g5 in range(TBA // 512):
                        if kind == "q" and T0 + g5 * 512 < tstart:
                            continue
                        toks = []
                        for i in range(4):
                            tl = g5 * 512 + i * 128
                            bank, bb = bk.next()
                            pairs = [(bank[:, 0:NWB], uT[:, kc, tl:tl + 128], ws[:, kc, :]) for kc in range(KC)]
                            mm_group(bank, pairs, [b_uT, wsb], [bb])
                            st, sbf = STB.next()
                            evict(st[:, 0:NWB], bank[:, 0:NWB], [bb], [sbf])
                            if kind == "v":
                                r0 = T0 + tl
                                p.op("sync", lambda e, st=st, r0=r0, vcol=vcol: e.dma_start(
                                    out=Vd[r0:r0 + 128, vcol:vcol + NWB], in_=st[:, 0:NWB]), [sbf], [], dma=sbf)
                            else:
                                toks.append((st, sbf))
                        if kind == "v":
                            continue
                        for m in range(NWB // 128):
                            bank2, bb2 = bk2.next()

                            def fn(e, m=m, bank2=bank2, toks=toks, kind=kind):
                                ins = None
                                for i in range(4):
                                    if kind == "q":
                                        ins = e.matmul(bank2[:, (3 - i) * 128:(4 - i) * 128], lhsT=toks[i][0][:, m * 128:(m + 1) * 128],
                                                       rhs=jmat_b[:], start=True, stop=True)
                                    else:
                                        ins = e.matmul(bank2[:, i * 128:(i + 1) * 128], lhsT=toks[i][0][:, m * 128:(m + 1) * 128],
                                                       rhs=ident_b[:], start=True, stop=True)
                                return ins
                            p.op("tensor", fn, [t[1] for t in toks] + [b_const], [bb2])
                            st2, sbf2 = PTR.next()
                            evict(st2[:], bank2[:], [bb2], [sbf2])
                            dst = qrT if kind == "q" else kTd
                            mi = mbase + m
                            c5 = T0 + g5 * 512
                            p.op("sync", lambda e, st2=st2, dst=dst, mi=mi, c5=c5: e.dma_start(
                                out=dst[mi * 128:(mi + 1) * 128, c5:c5 + 512], in_=st2[:]), [sbf2], [], dma=sbf2)
                p.barrier()

        def phase_attn(l, tstart=0):
            NB = S // 128
            scale = 128 ** -0.5
            off = 0
            cm_t = []
            for i in range(2):
                cm_t.append(aview(off, TW * 4, F32))
                off += TW * 4
            sets = []
            for i in range(2):
                d = {}
                d["k"] = [aview(off + j * S * 2, S * 2, BF16) for j in range(2)]
                off += 2 * S * 2
                d["q"] = [aview(off + j * S * 2, S * 2, BF16) for j in range(2)]
                off += 2 * S * 2
                vbytes = NB * 258 * 2
                d["v"] = aview(off, vbytes, BF16, "p (b c) -> p b c", c=258)
                off += vbytes
                d["w"] = aview(off, TW * 4, F32)
                off += TW * 4
                d["bk"] = buf("hk%d" % i)
                d["bq"] = buf("hq%d" % i)
                d["bv"] = buf("hv%d" % i)
                d["bw"] = buf("hw%d" % i)
                sets.append(d)
            assert off <= C.ARENA, off
            tsr = [aview(off + i * 2048, 2048, F32) for i in range(3)]
            off += 6144
            assert off <= C.ARENA, off
            TSR = ring_of(tsr, "tsr")
            b_cm = buf("cm")
            for i in range(2):
                srcm = bass.AP(tensor=cmask.tensor, offset=i * TL, ap=[[1, 128], [1, TW]])
                p.op("sync", lambda e, i=i, srcm=srcm: e.dma_start(out=cm_t[i], in_=srcm), [], [b_cm], dma=buf("d_cm%d" % i))
            sbk = Ring(BK[0:3])
            obk = BK[3:7]
            LOOK = 2
            heads = [("diff", h) for h in range(NDH)] + [("dil", h) for h in range(NLH)]
            for hi, (kind, h) in enumerate(heads):
                d = sets[hi % 2]
                nmap = 2 if kind == "diff" else 1
                VW = 256 if kind == "diff" else 128
                m0 = 2 * h if kind == "diff" else 2 * NDH + h
                vc0 = h * 256 if kind == "diff" else DW + h * 128
                yrow0 = vc0
                for j in range(nmap):
                    p.op("sync", lambda e, d=d, j=j, m0=m0: e.dma_start(out=d["k"][j], in_=kTd[(m0 + j) * 128:(m0 + j + 1) * 128, :]),
                         [], [d["bk"]], dma=buf("d_hk%d_%d" % (hi % 2, j)))
                    p.op("scalar", lambda e, d=d, j=j, m0=m0: e.dma_start(out=d["q"][j], in_=qrT[(m0 + j) * 128:(m0 + j + 1) * 128, :]),
                         [], [d["bq"]], dma=buf("d_hq%d_%d" % (hi % 2, j)))
                p.op("sync", lambda e, d=d, vc0=vc0, VW=VW: e.dma_start(
                    out=d["v"][:, :, 0:VW], in_=Vd[:, vc0:vc0 + VW].rearrange("(b p) c -> p b c", p=128)),
                    [], [d["bv"]], dma=buf("d_hv%d" % (hi % 2)))
                p.op("vector", lambda e, d=d, VW=VW: e.memset(d["v"][:, :, VW:VW + 1], 1.0), [d["bv"]], [d["bv"]])
                srcw = bass.AP(tensor=tabT.tensor, offset=hi * TL, ap=[[1, 128], [1, TW]])
                p.op("sync", lambda e, d=d, srcw=srcw: e.dma_start(out=d["w"], in_=srcw), [], [d["bw"]], dma=buf("d_hw%d" % (hi % 2)))
                cmi = 0 if kind == "diff" else 1
                p.op("vector", lambda e, d=d, cmi=cmi: e.tensor_tensor(out=d["w"], in0=d["w"], in1=cm_t[cmi], op=ALU.add),
                     [d["bw"], b_cm], [d["bw"]])
                gt_ap = gdf if kind == "diff" else gdl
                for qt in range(tstart // 512, NT5):
                    q0 = qt * 512
                    kb_hi = 4 * qt + 3
                    kb_lo = 0 if kind == "diff" else max(0, 4 * qt - 16)
                    for j in range(nmap):
                        pend = []

                        def emit_pv(item, d=d, VW=VW, kb_lo=kb_lo, kb_hi=kb_hi):
                            kb, pt, ptb = item

                            def fnpv(e, pt=pt, d=d, kb=kb, VW=VW, first=(kb == kb_lo), last=(kb == kb_hi)):
                                ins = None
                                for s in range(4):
                                    ins = e.matmul(obk[s][0][:, 0:VW + 1], lhsT=pt[:, s * 128:(s + 1) * 128], rhs=d["v"][:, kb, 0:VW + 1],
                                                   start=first, stop=last)
                                return ins
                            p.op("tensor", fnpv, [ptb, d["bv"]], [o[1] for o in obk])

                        for kb in range(kb_lo, kb_hi + 1):
                            k0 = kb * 128
                            delta = q0 + 511 - k0
                            if kind == "diff":
                                delta = min(delta, DCL)
                            base = DMAXT - delta
                            assert 0 <= base and base + 512 <= TW
                            bank, bb = sbk.next()
                            p.op("tensor", lambda e, bank=bank, d=d, j=j, k0=k0, q0=q0: e.matmul(
                                bank[:], lhsT=d["k"][j][:, k0:k0 + 128], rhs=d["q"][j][:, q0:q0 + 512], start=True, stop=True),
                                [d["bk"], d["bq"]], [bb])
                            ts_, tsb = TSR.next()
                            p.op("vector", lambda e, ts_=ts_, bank=bank, d=d, base=base: e.scalar_tensor_tensor(
                                out=ts_, in0=bank[:], scalar=float(scale), in1=d["w"][:, base:base + 512], op0=ALU.mult, op1=ALU.add),
                                [bb, d["bw"]], [tsb])
                            pt, ptb = PTR.next()
                            p.op("scalar", lambda e, pt=pt, ts_=ts_, kb=kb: e.activation(out=pt[:], in_=ts_, func=AF.Exp, bias=kmask_t[:, kb:kb + 1]), [tsb, b_const], [ptb])
                            pend.append((kb, pt, ptb))
                            if len(pend) > LOOK:
                                emit_pv(pend.pop(0))
                        while pend:
                            emit_pv(pend.pop(0))
                        yns = []
                        for s in range(4):
                            ob, obb = obk[s]
                            smt, smb = SM.next()
                            p.op("vector", lambda e, smt=smt, ob=ob, VW=VW: e.tensor_scalar(out=smt[:, 6:7], in0=ob[:, VW:VW + 1], scalar1=1e-30, scalar2=None, op0=ALU.max), [obb], [smb])
                            p.op("vector", lambda e, smt=smt: e.reciprocal(out=smt[:, 0:1], in_=smt[:, 6:7]), [smb], [smb])
                            if kind == "diff" and j == 0:
                                y1, y1b = y1t[s], buf("y1t%d" % s)
                                p.op("scalar", lambda e, y1=y1, ob=ob, smt=smt: e.activation(
                                    out=y1[:, 0:256], in_=ob[:, 0:256], func=AF.Copy, scale=smt[:, 0:1]), [obb, smb], [y1b])
                                continue
                            y2, y2b = Y2.next()
                            if kind == "diff":
                                y1, y1b = y1t[s], buf("y1t%d" % s)
                                p.op("vector", lambda e, smt=smt: e.tensor_tensor(out=smt[:, 1:2], in0=smt[:, 0:1], in1=lams[:, 5:6], op=ALU.mult),
                                     [smb, b_lam], [smb])
                                p.op("vector", lambda e, y2=y2, ob=ob, smt=smt, y1=y1: e.scalar_tensor_tensor(
                                    out=y2[:, 0:256], in0=ob[:, 0:256], scalar=smt[:, 1:2], in1=y1[:, 0:256], op0=ALU.mult, op1=ALU.add),
                                    [obb, smb, y1b], [y2b])
                            else:
                                p.op("scalar", lambda e, y2=y2, ob=ob, smt=smt: e.activation(
                                    out=y2[:, 0:128], in_=ob[:, 0:128], func=AF.Copy, scale=smt[:, 0:1]), [obb, smb], [y2b])
                            yq, yqb = YSQ.next()
                            p.op("vector", lambda e, yq=yq, y2=y2, VW=VW: e.tensor_tensor(out=yq[:, 0:VW], in0=y2[:, 0:VW], in1=y2[:, 0:VW], op=ALU.mult),
                                 [y2b], [yqb])
                            p.op("vector", lambda e, yq=yq, smt=smt, VW=VW: e.reduce_sum(out=smt[:, 2:3], in_=yq[:, 0:VW], axis=AX.X), [yqb, smb], [smb])
                            p.op("vector", lambda e, smt=smt, VW=VW: e.tensor_scalar(out=smt[:, 3:4], in0=smt[:, 2:3], scalar1=float(1.0 / VW),
                                                                                     scalar2=float(LN_EPS), op0=ALU.mult, op1=ALU.add), [smb], [smb])
                            p.op("scalar", lambda e, smt=smt: e.activation(out=smt[:, 4:5], in_=smt[:, 3:4], func=AF.Sqrt), [smb], [smb])
                            p.op("vector", lambda e, smt=smt: e.reciprocal(out=smt[:, 5:6], in_=smt[:, 4:5]), [smb], [smb])
                            yn, ynbuf = YNB.next()
                            p.op("vector", lambda e, yn=yn, y2=y2, smt=smt, VW=VW, gt_ap=gt_ap: e.scalar_tensor_tensor(
                                out=yn[:, 0:VW], in0=y2[:, 0:VW], scalar=smt[:, 5:6], in1=gt_ap[:, 0:VW], op0=ALU.mult, op1=ALU.mult),
                                [y2b, smb, b_g], [ynbuf])
                            yns.append((yn, ynbuf))
                        if kind == "diff" and j == 0:
                            continue
                        for ec in range(VW // 128):
                            tb_, tbb = BK[7]
                            for s in range(4):
                                yn, ynbuf = yns[s]
                                p.op("tensor", lambda e, tb_=tb_, yn=yn, ec=ec, s=s: e.matmul(
                                    tb_[:, (3 - s) * 128:(4 - s) * 128], lhsT=yn[:, ec * 128:(ec + 1) * 128], rhs=jmat_b[:], start=True, stop=True),
                                    [ynbuf, b_const], [tbb])
                            st2, sbf2 = STB.next()
                            evict(st2[:], tb_[:], [tbb], [sbf2])
                            r0 = yrow0 + ec * 128
                            p.op("sync", lambda e, st2=st2, r0=r0, q0=q0: e.dma_start(out=yT[r0:r0 + 128, q0:q0 + 512], in_=st2[:]),
                                 [sbf2], [], dma=sbf2)
            p.barrier()

        tbk_sel = [BK[6], BK[7]]

        def proj_residual(wc, b_wc, nk, act_sb, b_act, ntok, T0, gcol, res_src, dst, wslot, wb, NW, ksplit, hfs=None):
            bk = Ring(BK[0:4])
            kper = nk // ksplit
            wi = proj_wi[0]
            for ct in range(D // NW):
                c0 = ct * NW
                slots = []
                for ks in range(ksplit):
                    ws, wsb = wslot[wi % len(wslot)], wb[wi % len(wslot)]
                    wi += 1
                    p.op("gpsimd", lambda e, ws=ws, ct=ct, ks=ks: e.dma_start(
                        out=ws.rearrange("p k n -> p (k n)"), in_=wc[ct * ksplit + ks]), [b_wc], [wsb], dma=wsb)
                    slots.append((ws, wsb))
                units = [(jc, hf) for jc in range(NW // 128) for hf in (hfs if hfs is not None else range(ntok // 512))]
                ubanks = [bk.next() for _ in units]
                for ks in range(ksplit):
                    ws, wsb = slots[ks]
                    for (jc, hf), (bank, bb) in zip(units, ubanks):
                        pairs = [(bank[:], ws[:, ii, jc * 128:(jc + 1) * 128], act_sb(ks * kper + ii, hf)) for ii in range(kper)]
                        mm_group(bank, pairs, [b_act, wsb], [bb], first=(ks == 0), last=(ks == ksplit - 1))
                for (jc, hf), (bank, bb) in zip(units, ubanks):
                    j = c0 // 128 + jc
                    xs, xsb = XR.next()
                    tt = T0 + hf * 512
                    p.op("sync", lambda e, xs=xs, j=j, tt=tt: e.dma_start(out=xs[:], in_=res_src[j * 128:(j + 1) * 128, tt:tt + 512]),
                         [], [xsb], dma=xsb)
                    st, sbf = STF.next()
                    p.op("vector", lambda e, st=st, bank=bank, j=j, xs=xs: e.scalar_tensor_tensor(
                        out=st[:], in0=bank[:], scalar=modt[:, gcol + j:gcol + j + 1], in1=xs[:], op0=ALU.mult, op1=ALU.add),
                        [bb, xsb, b_mod], [sbf])
                    p.op("sync", lambda e, st=st, j=j, tt=tt: e.dma_start(out=dst[j * 128:(j + 1) * 128, tt:tt + 512], in_=st[:]),
                         [sbf], [], dma=sbf)
            proj_wi[0] = wi

        proj_wi = [0]

        def phase_wo(l, cur, z1, tstart=0):
            yS = aview(0, KC * TBA * 2, BF16, "p (k t) -> p k t", t=TBA)
            b_y = buf("yS")
            WOFF = KC * TBA * 2
            wslot = [aview(WOFF + i * KC * NWB * 2, KC * NWB * 2, BF16, "p (k n) -> p k n", n=NWB) for i in range(2)]
            wb = bufs("wsl", 2)
            for tb in range(S // TBA):
                T0 = tb * TBA
                if T0 + TBA <= tstart:
                    continue
                p.op("sync", lambda e, T0=T0: e.dma_start(out=yS, in_=yT[:, T0:T0 + TBA].rearrange("(k p) t -> p k t", p=128)),
                     [], [b_y], dma=buf("d_yS"))
                proj_residual(wc_o[l], buf("wc_o%d" % l), KC, lambda kc, hf: yS[:, kc, hf * 512:(hf + 1) * 512], b_y, TBA, T0,
                              2 * KC, cur, z1, wslot, wb, NWB, 1, hfs=[hf for hf in range(TBA // 512) if T0 + hf * 512 >= tstart])
                p.barrier()

        def phase_ffn(l, z1, x1, z2, tstart=0):
            u2 = aview(0, KC * 512 * 2, BF16, "p (k t) -> p k t", t=512)
            b_u2 = buf("u2")
            hoff = KC * 512 * 2
            hT = aview(hoff, FC * 512 * 2, BF16, "p (k t) -> p k t", t=512)
            b_h = buf("hT")
            WOFF = hoff + FC * 512 * 2
            WSZ = KC * NWB * 2
            wslot_up = [aview(WOFF + i * WSZ, KC * NWB * 2, BF16, "p (k n) -> p k n", n=NWB) for i in range(2)]
            wslot_dn = [aview(WOFF + i * WSZ, KPER * NWD * 2, BF16, "p (k n) -> p k n", n=NWD) for i in range(2)]
            assert WOFF + 2 * WSZ <= C.ARENA, (WOFF + 2 * WSZ)
            wb = bufs("wsl", 2)
            x1b = bufs("x1c", KC)
            bk = Ring(BK[0:4])
            e1 = LN_EPS / (C.alpha ** 2)
            for tb in range(tstart // 512, S // 512):
                t0 = tb * 512
                ln_stats(z1, t0, e1, mean_t, rstd_t, b_mean, b_rstd)
                st2 = ln_stats_begin()
                ln_apply(z1, t0, mean_t, rstd_t, b_mean, b_rstd, 0, KC, (vect, b_vec), dst_dram=x1, stats2=st2, dst_bufs=x1b)
                stats_finish(st2[0], st2[1], LN_EPS, mean2_t, rstd2_t, b_mean2, b_rstd2)
                ln_apply(x1, t0, mean2_t, rstd2_t, b_mean2, b_rstd2, 4 * KC, 3 * KC, (modt, b_mod),
                         dst_sb=lambda kc: u2[:, kc, :], dst_sb_buf=b_u2, src_bufs=x1b)
                p.barrier()
                bk8 = Ring(BK[0:8])
                for fg in range(0, FC, PER):
                    nch = min(PER, FC - fg)
                    slots = []
                    for half in range(2):
                        ti = (fg // PER) * 2 + half
                        ws, wsb = wslot_up[proj_wi[0] % 2], wb[proj_wi[0] % 2]
                        proj_wi[0] += 1
                        p.op("gpsimd", lambda e, ws=ws, ti=ti: e.dma_start(
                            out=ws.rearrange("p k n -> p (k n)"), in_=wc_up[l][ti]), [buf("wc_up%d" % l)], [wsb], dma=wsb)
                        slots.append((ws, wsb))
                    gb, vb = [], []
                    for jc in range(nch):
                        bg, bgb = bk8.next()
                        mm_group(bg, [(bg[:], slots[0][0][:, kc, jc * 128:(jc + 1) * 128], u2[:, kc, :]) for kc in range(KC)],
                                 [b_u2, slots[0][1]], [bgb])
                        gb.append((bg, bgb))
                    for jc in range(nch):
                        bv, bvb = bk8.next()
                        mm_group(bv, [(bv[:], slots[1][0][:, kc, jc * 128:(jc + 1) * 128], u2[:, kc, :]) for kc in range(KC)],
                                 [b_u2, slots[1][1]], [bvb])
                        vb.append((bv, bvb))
                    for jc in range(nch):
                        fc = fg + jc
                        bg, bgb = gb[jc]
                        bv, bvb = vb[jc]
                        ge, geb = GEXT.next()
                        p.op("vector", lambda e, ge=ge, fc=fc, tb=tb: e.tensor_scalar(out=ge[:, 0:2], in0=halo[:, fc, :], scalar1=hmask_t[:, tb:tb + 1], scalar2=None, op0=ALU.mult), [b_halo, b_const], [geb])
                        p.op("scalar", lambda e, ge=ge, bg=bg: e.activation(out=ge[:, 2:514], in_=bg[:], func=AF.Copy), [bgb], [geb])
                        p.op("vector", lambda e, ge=ge, fc=fc: e.tensor_copy(out=halo[:, fc, :], in_=ge[:, 512:514]), [geb], [b_halo])
                        a_, ab = CA.next()
                        p.op("scalar", lambda e, a_=a_, ge=ge, fc=fc: e.activation(
                            out=a_[:], in_=ge[:, 0:512], func=AF.Identity, scale=convt[:, fc:fc + 1], bias=convt[:, 3 * FC + fc:3 * FC + fc + 1]),
                            [geb, b_vec], [ab])
                        b2, b2b = CB.next()
                        p.op("vector", lambda e, b2=b2, ge=ge, fc=fc, a_=a_: e.scalar_tensor_tensor(
                            out=b2[:], in0=ge[:, 1:513], scalar=convt[:, FC + fc:FC + fc + 1], in1=a_[:], op0=ALU.mult, op1=ALU.add),
                            [geb, ab, b_vec], [b2b])
                        a2, a2b = CA.next()
                        p.op("vector", lambda e, a2=a2, ge=ge, fc=fc, b2=b2: e.scalar_tensor_tensor(
                            out=a2[:], in0=ge[:, 2:514], scalar=convt[:, 2 * FC + fc:2 * FC + fc + 1], in1=b2[:], op0=ALU.mult, op1=ALU.add),
                            [geb, b2b, b_vec], [a2b])
                        b3, b3b = CB.next()
                        p.op("scalar", lambda e, b3=b3, a2=a2: e.activation(out=b3[:], in_=a2[:], func=AF.Silu), [a2b], [b3b])
                        p.op("vector", lambda e, fc=fc, b3=b3, bv=bv: e.tensor_tensor(out=hT[:, fc, :], in0=b3[:], in1=bv[:], op=ALU.mult),
                             [b3b, bvb], [b_h])
                proj_residual(wc_dn[l], buf("wc_dn%d" % l), FC, lambda kc, hf: hT[:, kc, :], b_h, 512, t0, 5 * KC, x1, z2,
                              wslot_dn, wb, NWD, KSPLIT)
                p.barrier()

        def phase_ln_out(l, z2, x2, tstart=0):
            e1 = LN_EPS / (C.alpha ** 2)
            for tb in range(tstart // 512, S // 512):
                t0 = tb * 512
                ln_stats(z2, t0, e1, mean_t, rstd_t, b_mean, b_rstd)
                ln_apply(z2, t0, mean_t, rstd_t, b_mean, b_rstd, 2 * KC, 3 * KC, (vect, b_vec), dst_dram=x2)
            p.barrier()

        def phase_transpose_out(src):
            bk = Ring(BK[0:4])
            xo = [aview(i * D * 4, D * 4, F32) for i in range(2)]
            xob = bufs("xo", 2)
            for g in range((S - SEG) // 512, S // 512):
                tiles = []
                for kc in range(KC):
                    pass
                for i4 in range(4):
                    pass
                for i in range(4):
                    ot, otb = xo[i % 2], xob[i % 2]
                    for kg in range(0, KC, 4):
                        bank, bb = bk.next()
                        srcs = []
                        for kk in range(4):
                            kc = kg + kk
                            xs, xsb = XR.next()
                            p.op("sync", lambda e, xs=xs, kc=kc, g=g, i=i: e.dma_start(
                                out=xs[:, 0:128], in_=src[kc * 128:(kc + 1) * 128, g * 512 + i * 128:g * 512 + (i + 1) * 128]),
                                [], [xsb], dma=xsb)
                            srcs.append((xs, xsb))

                        def fn(e, bank=bank, srcs=srcs):
                            ins = None
                            for kk in range(4):
                                ins = e.transpose(bank[:, kk * 128:(kk + 1) * 128], srcs[kk][0][:, 0:128], ident_f[:])
                            return ins
                        p.op("tensor", fn, [s_[1] for s_ in srcs] + [b_const], [bb])
                        evict(ot[:, kg * 128:(kg + 4) * 128], bank[:], [bb], [otb])
                    r0 = g * 512 + i * 128 - (S - SEG)
                    p.op("sync", lambda e, ot=ot, r0=r0: e.dma_start(out=out[r0:r0 + 128, :], in_=ot), [otb], [], dma=otb)
            p.barrier()

        conv_w_in(0)
        phase_transpose_in()
        cur = xT0
        for l in range(L):
            z1, x1, z2, x2 = zs[l]
            ts_l = TPART if l == L - 1 else 0
            phase_params(l)
            phase_qkv(l, cur, ts_l)
            conv_w_rest(l)
            if l + 1 < L:
                conv_w_in(l + 1)
            phase_attn(l, ts_l)
            phase_wo(l, cur, z1, ts_l)
            phase_ffn(l, z1, x1, z2, ts_l)
            phase_ln_out(l, z2, x2, (S - SEG) if l == L - 1 else 0)
            cur = x2
            if C.stop_after is not None and l == C.stop_after:
                break
        phase_transpose_out(cur)

        with nc.Block() as block:
            p.replay(block)
    return nc


def host_inputs(C, b, inp, j=None):
    S, D, KC, FC, L = C.S, C.D, C.KC, C.FC, C.depth
    nseg = S // C.SEG
    if j is None:
        j = nseg - 1
    npad = (nseg - 1 - j) * C.SEG
    f = lambda a: np.ascontiguousarray(a, dtype=np.float32)
    fm = lambda v, n: f(v.reshape(n, 128).T)
    d = {}
    xw = np.zeros((S, D), np.float32)
    xw[npad:] = inp["x"][b][0:S - npad]
    d["x"] = xw
    tok = np.arange(S)
    d["kmask_fm"] = f(np.where(tok < npad, NEG, 0.0).reshape(S // 128, 128).T)
    hm = np.where(np.arange(S // 512) * 512 <= npad, 0.0, 1.0).astype(np.float32)
    d["hmask"] = f(np.broadcast_to(hm[None, :], (128, S // 512)))
    d["c_fm"] = fm(inp["c"][b], KC)
    d["w_ada"] = f(inp["w_ada"])
    d["b_ada_fm"] = f(np.stack([fm(inp["b_ada"][l], 6 * KC) for l in range(L)]))
    d["w_in"] = f(inp["w_in"])
    d["lamv"] = f(np.stack([np.concatenate([inp["lambda_q1"][l], inp["lambda_k1"][l], inp["lambda_q2"][l], inp["lambda_k2"][l]])
                            for l in range(L)]))
    d["g_diff"] = f(inp["g_diff"])
    d["g_dil"] = f(inp["g_dil"])
    d["w_o"] = f(inp["w_o"])
    d["vec_fm"] = f(np.stack([np.concatenate([fm(inp[k][l], KC) for k in ("ln1_g", "ln1_b", "ln2_g", "ln2_b")], axis=1)
                              for l in range(L)]))
    d["w_up"] = f(inp["w_up"])
    d["conv_fm"] = f(np.stack([np.concatenate([fm(inp["conv_w"][l][0], FC), fm(inp["conv_w"][l][1], FC),
                                               fm(inp["conv_w"][l][2], FC), fm(inp["conv_b"][l], FC)], axis=1) for l in range(L)]))
    d["w_down"] = f(inp["w_down"])
    dist = DMAXT - np.arange(TL)
    idx = t5_bucket_np(dist)
    d["tabT"] = f(inp["rel_bias"][idx, :].T)
    cm = np.zeros((2, TL), np.float32)
    cm[0] = np.where(dist >= 0, 0.0, NEG)
    mult = ((dist >= 0) & (dist <= 128)).astype(np.float64) + ((dist >= 0) & (dist % 4 == 0) & (dist <= 512)) \
        + ((dist >= 0) & (dist % 16 == 0) & (dist <= 2048))
    cm[1] = np.where(mult > 0, np.log(np.maximum(mult, 1.0)), NEG)
    d["cmask"] = cm
    d["ident"] = np.eye(128, dtype=np.float32)
    d["jmat"] = f(np.eye(128, dtype=np.float32)[::-1])
    return d


_NC_CACHE = {}


def kernel(**inputs):
    C = Cfg()
    if "nc" not in _NC_CACHE:
        _NC_CACHE["nc"] = build(C)
    nc = _NC_CACHE["nc"]
    inp = {k: np.asarray(v) for k, v in inputs.items()}
    nseg = C.S // C.SEG
    shared = {}
    in_maps = []
    for i in range(8):
        d = host_inputs(C, i // nseg, inp, i % nseg)
        for k in list(d.keys()):
            if k in ("w_ada", "w_in", "w_o", "w_up", "w_down"):
                d[k] = shared.setdefault(k, d[k])
        in_maps.append(d)
    res = run_bass_kernel_spmd(nc, in_maps, core_ids=list(range(8)))
    outp = np.empty((2, C.S, C.D), np.float32)
    for i in range(8):
        b, j = i // nseg, i % nseg
        outp[b, j * C.SEG:(j + 1) * C.SEG] = res.results[i]["out"]
    return outp
```

```python
import math
from contextlib import ExitStack

import numpy as np
import concourse.bass as bass
import concourse.mybir as mybir
from concourse.bass_utils import run_bass_kernel_spmd

F32 = mybir.dt.float32
BF16 = mybir.dt.bfloat16
AF = mybir.ActivationFunctionType
ALU = mybir.AluOpType
AX = mybir.AxisListType

ENGS = ["sync", "scalar", "vector", "gpsimd", "tensor"]
LN_EPS = 1e-5
NEG = -30000.0
DMAXT = 2559
TL = 3072
TW = 2944
DCL = 2175
SAME_SYNC = True


class Cfg:
    def __init__(self, S=4096, D=4096, depth=2, TBA=1024, NWB=256):
        self.S, self.D, self.depth = S, D, depth
        self.DW = D // 2
        self.LW = D - self.DW
        self.NDH = self.DW // 256
        self.NLH = self.LW // 128
        self.F = 256 * ((8 * D // 3 + 255) // 256)
        self.KC = D // 128
        self.FC = self.F // 128
        self.TBA = min(TBA, S)
        self.NWB = NWB
        self.INW = 3 * self.DW + 3 * self.LW
        self.alpha = (2 * depth) ** 0.25
        self.NM = 2 * self.NDH + self.NLH
        self.ARENA = 150 * 1024
        self.SEG = min(1024, S)
        self.EXT = 512
        self.stop_after = None
        self.debug = None


class Buf:
    __slots__ = ("name", "w", "r")

    def __init__(self, name):
        self.name = name
        self.w = None
        self.r = {}


class Prog:
    def __init__(self, nc, stack):
        self.nc, self.stack = nc, stack
        self.ops = {e: [] for e in ENGS}
        self.sems, self.cnt = {}, {}
        self.waited = {e: {} for e in ENGS}
        self.live = {}
        self.nsem = 0
        self.epoch = 0

    def _sem(self, key):
        if key not in self.sems:
            self.sems[key] = self.stack.enter_context(self.nc.semaphore("s%d" % self.nsem))
            self.nsem += 1
            self.cnt[key] = 0
        return self.sems[key]

    def op(self, eng, fn, reads=(), writes=(), dma=None):
        waits = {}

        def need(ev):
            if ev is None:
                return
            k, v = ev
            if dma is None and k[0] == "eng" and k[1] == eng and (eng == "tensor" or not SAME_SYNC):
                return
            if v > waits.get(k, 0):
                waits[k] = v

        for b in reads:
            need(b.w)
        for b in writes:
            need(b.w)
            for k, v in b.r.items():
                need((k, v))
        wl = []
        for k, v in waits.items():
            if self.waited[eng].get(k, 0) >= v:
                continue
            self.waited[eng][k] = v
            wl.append((self._sem(k), v))
        key = ("dma", dma.name) if dma is not None else ("eng", eng, self.epoch)
        sem = self._sem(key)
        inc = 16 if dma is not None else 1
        self.cnt[key] += inc
        ev = (key, self.cnt[key])
        self.live[key] = self.cnt[key]
        self.ops[eng].append((fn, wl, sem, inc))
        for b in writes:
            b.w = ev
            b.r = {}
        for b in reads:
            if b.r.get(key, 0) < ev[1]:
                b.r[key] = ev[1]
        return ev

    def barrier(self):
        for e in ENGS:
            wl = []
            for k, v in self.live.items():
                if self.waited[e].get(k, 0) >= v:
                    continue
                self.waited[e][k] = v
                wl.append((self._sem(k), v))
            if wl:
                self.ops[e].append((None, wl, None, 0))
        self.live = {}
        for k in list(self.sems.keys()):
            if self.cnt[k] > 24000:
                if k[0] == "eng":
                    pass
                else:
                    del self.sems[k]
        if any(self.cnt.get(("eng", e, self.epoch), 0) > 24000 for e in ENGS):
            self.epoch += 1

    def replay(self, block):
        def mk(ename):
            lst = self.ops[ename]

            def body(e):
                for fn, wl, sem, inc in lst:
                    for s, v in wl:
                        e.wait_ge(s, v)
                    if fn is not None:
                        ins = fn(e)
                        ins.then_inc(sem, inc)
            return body

        block.sync(mk("sync"))
        block.scalar(mk("scalar"))
        block.vector(mk("vector"))
        block.gpsimd(mk("gpsimd"))
        block.tensor(mk("tensor"))


class Ring:
    def __init__(self, items):
        self.items = items
        self.i = 0

    def next(self):
        it = self.items[self.i % len(self.items)]
        self.i += 1
        return it


def t5_bucket_np(d):
    n = np.maximum(d, 0)
    max_exact = 16
    nf = np.maximum(n, max_exact).astype(np.float32)
    large = max_exact + (np.log(nf / max_exact) / math.log(2048 / max_exact) * (32 - max_exact)).astype(np.int32)
    large = np.minimum(large, 31)
    return np.where(n < max_exact, n, large)


def build(C):
    nc = bass.Bass("TRN2", target_bir_lowering=False)
    S, D, F, KC, FC, L = C.S, C.D, C.F, C.KC, C.FC, C.depth
    NDH, NLH, NM, DW, LW, INW = C.NDH, C.NLH, C.NM, C.DW, C.LW, C.INW
    TBA, NWB = C.TBA, C.NWB
    NT5 = S // 512
    QA0, KA0, VA0, QB0, KB0, VB0 = 0, DW, 2 * DW, 3 * DW, 3 * DW + LW, 3 * DW + 2 * LW

    def din(name, shape):
        return nc.dram_tensor(name, list(shape), F32, kind="ExternalInput").ap()

    x_in = din("x", [S, D])
    c_fm = din("c_fm", [128, KC])
    w_ada = din("w_ada", [L, D, 6 * D])
    b_ada_fm = din("b_ada_fm", [L, 128, 6 * KC])
    w_in = din("w_in", [L, D, INW])
    lamv = din("lamv", [L, 4 * 128])
    g_diff = din("g_diff", [L, 256])
    g_dil = din("g_dil", [L, 128])
    w_o = din("w_o", [L, D, D])
    vec_fm = din("vec_fm", [L, 128, 4 * KC])
    w_up = din("w_up", [L, D, 2 * F])
    conv_fm = din("conv_fm", [L, 128, 4 * FC])
    w_down = din("w_down", [L, F, D])
    tabT = din("tabT", [NDH + NLH, TL])
    cmask = din("cmask", [2, TL])
    ident_in = din("ident", [128, 128])
    jmat_in = din("jmat", [128, 128])
    kmask_in = din("kmask_fm", [128, S // 128])
    hmask_in = din("hmask", [128, S // 512])
    SEG = C.SEG
    TPART = max(0, S - SEG - C.EXT)
    out = nc.dram_tensor("out", [SEG, D], F32, kind="ExternalOutput").ap()

    def dscr(name, shape, dt):
        return nc.dram_tensor(name, list(shape), dt).ap()

    xT0 = dscr("xT0", [D, S], F32)
    zs = [[dscr("r%d_%d" % (l, i), [D, S], F32) for i in range(4)] for l in range(L)]
    qrT = dscr("qrT", [NM * 128, S], BF16)
    kTd = dscr("kTd", [NM * 128, S], BF16)
    Vd = dscr("Vd", [S, D], BF16)
    yT = dscr("yT", [D, S], BF16)

    PER = NWB // 128
    NUPT = 2 * ((FC + PER - 1) // PER)
    KSPLIT = 2 if FC % 2 == 0 else 1
    KPER = FC // KSPLIT
    NWD = 256 if KPER * 256 * 2 <= KC * NWB * 2 else 128
    wc_in = [dscr("wc_in%d" % l, [INW // NWB, 128, KC * NWB], BF16) for l in range(L)]
    wc_o = [dscr("wc_o%d" % l, [D // NWB, 128, KC * NWB], BF16) for l in range(L)]
    wc_up = [dscr("wc_up%d" % l, [NUPT, 128, KC * NWB], BF16) for l in range(L)]
    wc_dn = [dscr("wc_dn%d" % l, [(D // NWD) * KSPLIT, 128, KPER * NWD], BF16) for l in range(L)]

    stack = ExitStack()
    with stack:
        p = Prog(nc, stack)
        sb = lambda name, shape, dt: stack.enter_context(nc.sbuf_tensor(name, list(shape), dt))
        ident_b = sb("ident_b", [128, 128], BF16)
        jmat_b = sb("jmat_b", [128, 128], BF16)
        ident_f = sb("ident_f", [128, 128], F32)
        ones_f = sb("ones_f", [128, 128], F32)
        modt = sb("modt", [128, 6 * KC], F32)
        badat = sb("badat", [128, 6 * KC], F32)
        vect = sb("vect", [128, 4 * KC], F32)
        convt = sb("convt", [128, 4 * FC], F32)
        halo = sb("halo", [128, FC, 2], F32)
        cfm_t = sb("cfm_t", [128, KC], F32)
        csil = sb("csil", [128, KC], BF16)
        lamb = sb("lamb", [128, 4 * 128], F32)
        lamp = sb("lamp", [128, 2 * 128], F32)
        lams = sb("lams", [128, 8], F32)
        gdf = sb("gdf", [128, 256], F32)
        gdl = sb("gdl", [128, 128], F32)
        kmask_t = sb("kmask_t", [128, S // 128], F32)
        hmask_t = sb("hmask_t", [128, S // 512], F32)
        NXR = 4
        xr = [sb("xr%d" % i, [128, 512], F32) for i in range(NXR)]
        stf = [sb("stf%d" % i, [128, 512], F32) for i in range(3)]
        stb = [sb("stb%d" % i, [128, 512], BF16) for i in range(8)]
        ptr = [sb("ptr%d" % i, [128, 512], BF16) for i in range(4)]
        sm = [sb("sm%d" % i, [128, 8], F32) for i in range(8)]
        arena2 = sb("arena2", [128, 4096], F32)

        def a2(off_bytes, nbytes, dt):
            a = arena2[:, off_bytes // 4:(off_bytes + nbytes) // 4]
            return a.bitcast(BF16) if dt == BF16 else a
        sqr = [a2(i * 2048, 2048, F32) for i in range(2)]
        t1r = [a2(4096 + i * 2048, 2048, F32) for i in range(2)]
        mean_t, rstd_t, mean2_t, rstd2_t = [a2(8192 + i * 2048, 2048, F32) for i in range(4)]
        gext = [a2(i * 2056, 2056, F32) for i in range(2)]
        ca = [a2(4112 + i * 2048, 2048, F32) for i in range(2)]
        cb_ = [a2(8208 + i * 2048, 2048, F32) for i in range(2)]
        y1t = [a2(i * 1024, 1024, F32) for i in range(4)]
        y2t = [a2(4096 + i * 1024, 1024, F32) for i in range(4)]
        ysq = [a2(8192 + i * 1024, 1024, F32) for i in range(4)]
        ynb = [a2(12288 + i * 512, 512, BF16) for i in range(4)]
        ARW = C.ARENA // 4
        arena = sb("arena", [128, ARW], F32)
        banks = [stack.enter_context(nc.psum_tensor("bank%d" % i, [128, 512], F32)) for i in range(8)]

        B = {}

        def buf(name):
            if name not in B:
                B[name] = Buf(name)
            return B[name]

        def bufs(prefix, n):
            return [buf("%s%d" % (prefix, i)) for i in range(n)]

        def ring_of(tiles, prefix):
            return Ring(list(zip(tiles, bufs(prefix, len(tiles)))))

        XR = ring_of(xr, "xr")
        SQR = ring_of(sqr, "sqr")
        T1R = ring_of(t1r, "t1r")
        STF = ring_of(stf, "stf")
        STB = ring_of(stb, "stb")
        PTR = ring_of(ptr, "ptr")
        GEXT = ring_of(gext, "gext")
        CA = ring_of(ca, "ca")
        CB = ring_of(cb_, "cb")
        SM = ring_of(sm, "sm")
        Y1 = ring_of(y1t, "y1t")
        Y2 = ring_of(y2t, "y2t")
        YSQ = ring_of(ysq, "ysq")
        YNB = ring_of(ynb, "ynb")
        BK = [(banks[i], buf("bank%d" % i)) for i in range(8)]
        b_const = buf("const")
        b_mod = buf("mod")
        b_vec = buf("vec")
        b_halo = buf("halo")
        b_lam = buf("lam")
        b_g = buf("g")
        b_mean, b_rstd = buf("mean"), buf("rstd")
        b_mean2, b_rstd2 = buf("mean2"), buf("rstd2")

        def aview(off_bytes, nbytes, dt, pattern=None, **kw):
            a = arena[:, off_bytes // 4:(off_bytes + nbytes) // 4]
            if dt == BF16:
                a = a.bitcast(BF16)
            if pattern:
                a = a.rearrange(pattern, **kw)
            return a

        evict_flip = [0]

        def evict(out_ap, in_ap, reads, writes, scale=None):
            evict_flip[0] ^= 1
            if evict_flip[0]:
                if scale is None:
                    p.op("scalar", lambda e: e.activation(out=out_ap, in_=in_ap, func=AF.Copy), reads, writes)
                else:
                    p.op("scalar", lambda e: e.activation(out=out_ap, in_=in_ap, func=AF.Copy, scale=scale), reads, writes)
            else:
                if scale is None:
                    p.op("vector", lambda e: e.tensor_copy(out=out_ap, in_=in_ap), reads, writes)
                else:
                    p.op("vector", lambda e: e.tensor_scalar(out=out_ap, in0=in_ap, scalar1=scale, scalar2=None, op0=ALU.mult), reads, writes)

        def mm_group(bank, pairs, reads, writes, first=True, last=True):
            n = len(pairs)

            def fn(e):
                ins = None
                for i, (o, l_, r_) in enumerate(pairs):
                    ins = e.matmul(o, lhsT=l_, rhs=r_, start=(first and i == 0), stop=(last and i == n - 1))
                return ins
            p.op("tensor", fn, reads, writes)

        p.op("gpsimd", lambda e: e.dma_start(out=ident_b[:], in_=ident_in[:, :]), [], [b_const], dma=buf("d_identb"))
        p.op("gpsimd", lambda e: e.dma_start(out=jmat_b[:], in_=jmat_in[:, :]), [], [b_const], dma=buf("d_jb"))
        p.op("sync", lambda e: e.dma_start(out=ident_f[:], in_=ident_in[:, :]), [], [b_const], dma=buf("d_identf"))
        p.op("sync", lambda e: e.dma_start(out=cfm_t[:], in_=c_fm[:, :]), [], [b_const], dma=buf("d_cfm"))
        p.op("vector", lambda e: e.memset(ones_f[:], 1.0), [], [b_const])
        p.op("sync", lambda e: e.dma_start(out=kmask_t[:], in_=kmask_in[:, :]), [], [b_const], dma=buf("d_kmask"))
        p.op("sync", lambda e: e.dma_start(out=hmask_t[:], in_=hmask_in[:, :]), [], [b_const], dma=buf("d_hmask"))
        p.op("scalar", lambda e: e.activation(out=csil[:], in_=cfm_t[:], func=AF.Silu), [b_const], [b_const])
        p.barrier()

        def conv_w_in(l):
            b = buf("wc_in%d" % l)
            for wt in range(INW // NWB):
                c0 = wt * NWB
                p.op("gpsimd", lambda e, wt=wt, c0=c0: e.dma_start(
                    out=wc_in[l][wt].rearrange("p (k n) -> p k n", n=NWB),
                    in_=w_in[l, :, c0:c0 + NWB].rearrange("(k p) n -> p k n", p=128)), [], [b], dma=b)

        def conv_w_rest(l):
            b = buf("wc_o%d" % l)
            for wt in range(D // NWB):
                c0 = wt * NWB
                p.op("gpsimd", lambda e, wt=wt, c0=c0: e.dma_start(
                    out=wc_o[l][wt].rearrange("p (k n) -> p k n", n=NWB),
                    in_=w_o[l, :, c0:c0 + NWB].rearrange("(k p) n -> p k n", p=128)), [], [b], dma=b)
            b = buf("wc_up%d" % l)
            for ti in range(NUPT):
                fg, half = (ti // 2) * PER, ti % 2
                nch = min(PER, FC - fg)
                c0 = half * F + fg * 128
                p.op("gpsimd", lambda e, ti=ti, c0=c0, nch=nch: e.dma_start(
                    out=wc_up[l][ti].rearrange("p (k n) -> p k n", n=NWB)[:, :, 0:nch * 128],
                    in_=w_up[l, :, c0:c0 + nch * 128].rearrange("(k p) n -> p k n", p=128)), [], [b], dma=b)
            b = buf("wc_dn%d" % l)
            for ct in range(D // NWD):
                for ks in range(KSPLIT):
                    k_lo = ks * KPER * 128
                    p.op("gpsimd", lambda e, ct=ct, ks=ks, k_lo=k_lo: e.dma_start(
                        out=wc_dn[l][ct * KSPLIT + ks].rearrange("p (k n) -> p k n", n=NWD),
                        in_=w_down[l, k_lo:k_lo + KPER * 128, ct * NWD:(ct + 1) * NWD].rearrange("(k p) n -> p k n", p=128)),
                        [], [b], dma=b)

        def phase_transpose_in():
            xt_tiles = [aview(i * D * 4, D * 4, F32) for i in range(4)]
            xt_b = bufs("xt", 4)
            bk = Ring(BK[0:4])
            for g in range(S // 512):
                for i in range(4):
                    r0 = g * 512 + i * 128
                    p.op("sync", lambda e, i=i, r0=r0: e.dma_start(out=xt_tiles[i], in_=x_in[r0:r0 + 128, :]),
                         [], [xt_b[i]], dma=xt_b[i])
                for kc in range(KC):
                    bank, bb = bk.next()

                    def fn(e, kc=kc, bank=bank):
                        ins = None
                        for i in range(4):
                            ins = e.transpose(bank[:, i * 128:(i + 1) * 128], xt_tiles[i][:, kc * 128:(kc + 1) * 128], ident_f[:])
                        return ins
                    p.op("tensor", fn, xt_b + [b_const], [bb])
                    st, sbf = STF.next()
                    evict(st[:], bank[:], [bb], [sbf])
                    p.op("sync", lambda e, st=st, kc=kc, g=g: e.dma_start(out=xT0[kc * 128:(kc + 1) * 128, g * 512:(g + 1) * 512], in_=st[:]),
                         [sbf], [], dma=sbf)
            p.barrier()

        def ln_stats_begin():
            return BK[4], BK[5]

        def stats_accum(xs, xsb, kc, bs, bq):
            sq, sqb = SQR.next()
            p.op("scalar", lambda e: e.activation(out=sq, in_=xs[:], func=AF.Square), [xsb], [sqb])
            p.op("tensor", lambda e: e.matmul(bs[0][:], lhsT=ones_f[:], rhs=xs[:], start=(kc == 0), stop=(kc == KC - 1)),
                 [xsb, b_const], [bs[1]])
            p.op("tensor", lambda e: e.matmul(bq[0][:], lhsT=ones_f[:], rhs=sq, start=(kc == 0), stop=(kc == KC - 1)),
                 [sqb, b_const], [bq[1]])

        def stats_finish(bs, bq, eps, mean_ap, rstd_ap, bm, br):
            msq, b_msq = SQR.next()
            var, b_var = SQR.next()
            p.op("scalar", lambda e: e.activation(out=mean_ap, in_=bs[0][:], func=AF.Copy, scale=1.0 / D), [bs[1]], [bm])
            p.op("vector", lambda e: e.tensor_tensor(out=msq, in0=mean_ap, in1=mean_ap, op=ALU.mult), [bm], [b_msq])
            p.op("vector", lambda e: e.scalar_tensor_tensor(out=var, in0=bq[0][:], scalar=1.0 / D, in1=msq,
                                                            op0=ALU.mult, op1=ALU.subtract), [bq[1], b_msq], [b_var])
            p.op("vector", lambda e: e.tensor_scalar(out=var, in0=var, scalar1=float(eps), scalar2=None, op0=ALU.add),
                 [b_var], [b_var])
            p.op("scalar", lambda e: e.activation(out=var, in_=var, func=AF.Sqrt), [b_var], [b_var])
            p.op("vector", lambda e: e.reciprocal(out=rstd_ap, in_=var), [b_var], [br])

        def ln_stats(src, t0, eps, mean_ap, rstd_ap, bm, br):
            bs, bq = ln_stats_begin()
            for kc in range(KC):
                xs, xsb = XR.next()
                p.op("sync", lambda e, xs=xs, kc=kc: e.dma_start(out=xs[:], in_=src[kc * 128:(kc + 1) * 128, t0:t0 + 512]),
                     [], [xsb], dma=xsb)
                stats_accum(xs, xsb, kc, bs, bq)
            stats_finish(bs, bq, eps, mean_ap, rstd_ap, bm, br)

        def ln_apply(src, t0, mean_ap, rstd_ap, bm, br, sc_col, bi_col, sbuf_for, dst_dram=None, dst_sb=None, dst_sb_buf=None,
                     stats2=None, src_bufs=None, dst_bufs=None):
            for kc in range(KC):
                xs, xsb = XR.next()
                rd = [src_bufs[kc]] if src_bufs is not None else []
                p.op("sync", lambda e, xs=xs, kc=kc: e.dma_start(out=xs[:], in_=src[kc * 128:(kc + 1) * 128, t0:t0 + 512]),
                     rd, [xsb], dma=xsb)
                t1, t1b = T1R.next()
                p.op("vector", lambda e, xs=xs, t1=t1: e.tensor_tensor(out=t1, in0=xs[:], in1=mean_ap, op=ALU.subtract),
                     [xsb, bm], [t1b])
                t2, t2b = t1, t1b
                p.op("gpsimd", lambda e, t1=t1: e.tensor_tensor(out=t1, in0=t1, in1=rstd_ap, op=ALU.mult),
                     [t1b, br], [t1b])
                sc_ap = sbuf_for[0][:, sc_col + kc:sc_col + kc + 1]
                bi_ap = sbuf_for[0][:, bi_col + kc:bi_col + kc + 1]
                if dst_dram is not None:
                    st, sbf = STF.next()
                    p.op("scalar", lambda e, st=st, t2=t2, sc_ap=sc_ap, bi_ap=bi_ap: e.activation(
                        out=st[:], in_=t2, func=AF.Identity, scale=sc_ap, bias=bi_ap), [t2b, sbuf_for[1]], [sbf])
                    if stats2 is not None:
                        stats_accum(st, sbf, kc, stats2[0], stats2[1])
                    wr = [dst_bufs[kc]] if dst_bufs is not None else []
                    p.op("sync", lambda e, st=st, kc=kc: e.dma_start(out=dst_dram[kc * 128:(kc + 1) * 128, t0:t0 + 512], in_=st[:]),
                         [sbf], wr, dma=sbf)
                else:
                    o_ap = dst_sb(kc)
                    p.op("scalar", lambda e, o_ap=o_ap, t2=t2, sc_ap=sc_ap, bi_ap=bi_ap: e.activation(
                        out=o_ap, in_=t2, func=AF.Identity, scale=sc_ap, bias=bi_ap), [t2b, sbuf_for[1]], [dst_sb_buf])

        def phase_params(l):
            lam_init = 0.8 - 0.6 * math.exp(-0.3 * l)
            p.op("sync", lambda e: e.dma_start(out=badat[:], in_=b_ada_fm[l]), [], [b_mod], dma=buf("d_bada"))
            p.op("sync", lambda e: e.dma_start(out=vect[:], in_=vec_fm[l]), [], [b_vec], dma=buf("d_vec"))
            p.op("sync", lambda e: e.dma_start(out=convt[:], in_=conv_fm[l]), [], [b_vec], dma=buf("d_conv"))
            src = bass.AP(tensor=lamv.tensor, offset=l * 512, ap=[[0, 128], [1, 512]])
            p.op("sync", lambda e: e.dma_start(out=lamb[:], in_=src), [], [b_lam], dma=buf("d_lam"))
            srcg = bass.AP(tensor=g_diff.tensor, offset=l * 256, ap=[[0, 128], [1, 256]])
            p.op("sync", lambda e: e.dma_start(out=gdf[:], in_=srcg), [], [b_g], dma=buf("d_gdf"))
            srcl = bass.AP(tensor=g_dil.tensor, offset=l * 128, ap=[[0, 128], [1, 128]])
            p.op("sync", lambda e: e.dma_start(out=gdl[:], in_=srcl), [], [b_g], dma=buf("d_gdl"))
            p.op("vector", lambda e: e.memset(halo[:], 0.0), [], [b_halo])
            p.op("vector", lambda e: e.tensor_tensor(out=lamp[:, 0:128], in0=lamb[:, 0:128], in1=lamb[:, 128:256], op=ALU.mult), [b_lam], [b_lam])
            p.op("vector", lambda e: e.tensor_tensor(out=lamp[:, 128:256], in0=lamb[:, 256:384], in1=lamb[:, 384:512], op=ALU.mult), [b_lam], [b_lam])
            p.op("vector", lambda e: e.reduce_sum(out=lams[:, 0:1], in_=lamp[:, 0:128], axis=AX.X), [b_lam], [b_lam])
            p.op("vector", lambda e: e.reduce_sum(out=lams[:, 1:2], in_=lamp[:, 128:256], axis=AX.X), [b_lam], [b_lam])
            p.op("scalar", lambda e: e.activation(out=lams[:, 2:4], in_=lams[:, 0:2], func=AF.Exp), [b_lam], [b_lam])
            p.op("vector", lambda e: e.tensor_tensor(out=lams[:, 4:5], in0=lams[:, 2:3], in1=lams[:, 3:4], op=ALU.subtract), [b_lam], [b_lam])
            p.op("vector", lambda e: e.tensor_scalar(out=lams[:, 5:6], in0=lams[:, 4:5], scalar1=float(lam_init), scalar2=-1.0,
                                                     op0=ALU.add, op1=ALU.mult), [b_lam], [b_lam])
            p.op("vector", lambda e: e.tensor_scalar(out=gdf[:], in0=gdf[:], scalar1=float(1.0 - lam_init), scalar2=None, op0=ALU.mult),
                 [b_g], [b_g])
            NWM = 512
            wslot = [aview(i * KC * NWM * 2, KC * NWM * 2, BF16, "p (k n) -> p k n", n=NWM) for i in range(2)]
            wb = bufs("wsl", 2)
            bank, bb = BK[0]
            ntile = 6 * D // NWM
            for t in range(ntile):
                ws, wsb = wslot[t % 2], wb[t % 2]
                p.op("gpsimd", lambda e, ws=ws, t=t: e.dma_start(
                    out=ws, in_=w_ada[l, :, t * NWM:(t + 1) * NWM].rearrange("(k p) n -> p k n", p=128)), [], [wsb], dma=wsb)
                for j in range(NWM // 128):
                    col = t * (NWM // 128) + j
                    pairs = [(bank[:, col:col + 1], ws[:, kc, j * 128:(j + 1) * 128], csil[:, kc:kc + 1]) for kc in range(KC)]
                    mm_group(bank, pairs, [wsb, b_const], [bb])
            p.op("vector", lambda e: e.tensor_tensor(out=modt[:], in0=bank[:, 0:6 * KC], in1=badat[:], op=ALU.add), [bb, b_mod], [b_mod])
            for base in (KC, 4 * KC):
                p.op("vector", lambda e, base=base: e.tensor_scalar(out=modt[:, base:base + KC], in0=modt[:, base:base + KC],
                                                                    scalar1=1.0, scalar2=None, op0=ALU.add), [b_mod], [b_mod])
            for base in (2 * KC, 5 * KC):
                p.op("vector", lambda e, base=base: e.tensor_scalar(out=modt[:, base:base + KC], in0=modt[:, base:base + KC],
                                                                    scalar1=float(1.0 / C.alpha), scalar2=None, op0=ALU.mult), [b_mod], [b_mod])
            p.barrier()

        def phase_qkv(l, cur, tstart=0):
            NTB = S // TBA
            uT = aview(0, KC * TBA * 2, BF16, "p (k t) -> p k t", t=TBA)
            b_uT = buf("uT")
            WOFF = KC * TBA * 2
            wslot = [aview(WOFF + i * KC * NWB * 2, KC * NWB * 2, BF16, "p (k n) -> p k n", n=NWB) for i in range(2)]
            wb = bufs("wsl", 2)
            bk = Ring(BK[0:4])
            bk2 = Ring(BK[6:8])
            nwt = INW // NWB
            for tb in range(NTB):
                T0 = tb * TBA
                for sbk in range(TBA // 512):
                    t0 = T0 + sbk * 512
                    ln_stats(cur, t0, LN_EPS, mean_t, rstd_t, b_mean, b_rstd)
                    ln_apply(cur, t0, mean_t, rstd_t, b_mean, b_rstd, KC, 0, (modt, b_mod),
                             dst_sb=lambda kc, sbk=sbk: uT[:, kc, sbk * 512:(sbk + 1) * 512], dst_sb_buf=b_uT)
                for wt in range(nwt):
                    c0 = wt * NWB
                    ws, wsb = wslot[wt % 2], wb[wt % 2]
                    p.op("gpsimd", lambda e, ws=ws, wt=wt: e.dma_start(
                        out=ws.rearrange("p k n -> p (k n)"), in_=wc_in[l][wt]), [buf("wc_in%d" % l)], [wsb], dma=wsb)
                    if c0 < KA0:
                        kind, mbase = "q", (c0 - QA0) // 128
                    elif c0 < VA0:
                        kind, mbase = "k", (c0 - KA0) // 128
                    elif c0 < QB0:
                        kind, vcol = "v", c0 - VA0
                    elif c0 < KB0:
                        kind, mbase = "q", 2 * NDH + (c0 - QB0) // 128
                    elif c0 < VB0:
                        kind, mbase = "k", 2 * NDH + (c0 - KB0) // 128
                    else:
                        kind, vcol = "v", DW + (c0 - VB0)
                    for g5 in range(TBA // 512):
                        if kind == "q" and T0 + g5 * 512 < tstart:
                            continue
                        toks = []
                        for i in range(4):
                            tl = g5 * 512 + i * 128
                            bank, bb = bk.next()
                            pairs = [(bank[:, 0:NWB], uT[:, kc, tl:tl + 128], ws[:, kc, :]) for kc in range(KC)]
                            mm_group(bank, pairs, [b_uT, wsb], [bb])
                            st, sbf = STB.next()
                            evict(st[:, 0:NWB], bank[:, 0:NWB], [bb], [sbf])
                            if kind == "v":
                                r0 = T0 + tl
                                p.op("sync", lambda e, st=st, r0=r0, vcol=vcol: e.dma_start(
                                    out=Vd[r0:r0 + 128, vcol:vcol + NWB], in_=st[:, 0:NWB]), [sbf], [], dma=sbf)
                            else:
                                toks.append((st, sbf))
                        if kind == "v":
                            continue
                        for m in range(NWB // 128):
                            bank2, bb2 = bk2.next()

                            def fn(e, m=m, bank2=bank2, toks=toks, kind=kind):
                                ins = None
                                for i in range(4):
                                    if kind == "q":
                                        ins = e.matmul(bank2[:, (3 - i) * 128:(4 - i) * 128], lhsT=toks[i][0][:, m * 128:(m + 1) * 128],
                                                       rhs=jmat_b[:], start=True, stop=True)
                                    else:
                                        ins = e.matmul(bank2[:, i * 128:(i + 1) * 128], lhsT=toks[i][0][:, m * 128:(m + 1) * 128],
                                                       rhs=ident_b[:], start=True, stop=True)
                                return ins
                            p.op("tensor", fn, [t[1] for t in toks] + [b_const], [bb2])
                            st2, sbf2 = PTR.next()
                            evict(st2[:], bank2[:], [bb2], [sbf2])
                            dst = qrT if kind == "q" else kTd
                            mi = mbase + m
                            c5 = T0 + g5 * 512
                            p.op("sync", lambda e, st2=st2, dst=dst, mi=mi, c5=c5: e.dma_start(
                                out=dst[mi * 128:(mi + 1) * 128, c5:c5 + 512], in_=st2[:]), [sbf2], [], dma=sbf2)
                p.barrier()

        def phase_attn(l, tstart=0):
            NB = S // 128
            scale = 128 ** -0.5
            off = 0
            cm_t = []
            for i in range(2):
                cm_t.append(aview(off, TW * 4, F32))
                off += TW * 4
            sets = []
            for i in range(2):
                d = {}
                d["k"] = [aview(off + j * S * 2, S * 2, BF16) for j in range(2)]
                off += 2 * S * 2
                d["q"] = [aview(off + j * S * 2, S * 2, BF16) for j in range(2)]
                off += 2 * S * 2
                vbytes = NB * 258 * 2
                d["v"] = aview(off, vbytes, BF16, "p (b c) -> p b c", c=258)
                off += vbytes
                d["w"] = aview(off, TW * 4, F32)
                off += TW * 4
                d["bk"] = buf("hk%d" % i)
                d["bq"] = buf("hq%d" % i)
                d["bv"] = buf("hv%d" % i)
                d["bw"] = buf("hw%d" % i)
                sets.append(d)
            assert off <= C.ARENA, off
            tsr = [aview(off + i * 2048, 2048, F32) for i in range(3)]
            off += 6144
            assert off <= C.ARENA, off
            TSR = ring_of(tsr, "tsr")
            b_cm = buf("cm")
            for i in range(2):
                srcm = bass.AP(tensor=cmask.tensor, offset=i * TL, ap=[[1, 128], [1, TW]])
                p.op("sync", lambda e, i=i, srcm=srcm: e.dma_start(out=cm_t[i], in_=srcm), [], [b_cm], dma=buf("d_cm%d" % i))
            sbk = Ring(BK[0:3])
            obk = BK[3:7]
            LOOK = 2
            heads = [("diff", h) for h in range(NDH)] + [("dil", h) for h in range(NLH)]
            for hi, (kind, h) in enumerate(heads):
                d = sets[hi % 2]
                nmap = 2 if kind == "diff" else 1
                VW = 256 if kind == "diff" else 128
                m0 = 2 * h if kind == "diff" else 2 * NDH + h
                vc0 = h * 256 if kind == "diff" else DW + h * 128
                yrow0 = vc0
                for j in range(nmap):
                    p.op("sync", lambda e, d=d, j=j, m0=m0: e.dma_start(out=d["k"][j], in_=kTd[(m0 + j) * 128:(m0 + j + 1) * 128, :]),
                         [], [d["bk"]], dma=buf("d_hk%d_%d" % (hi % 2, j)))
                    p.op("scalar", lambda e, d=d, j=j, m0=m0: e.dma_start(out=d["q"][j], in_=qrT[(m0 + j) * 128:(m0 + j + 1) * 128, :]),
                         [], [d["bq"]], dma=buf("d_hq%d_%d" % (hi % 2, j)))
                p.op("sync", lambda e, d=d, vc0=vc0, VW=VW: e.dma_start(
                    out=d["v"][:, :, 0:VW], in_=Vd[:, vc0:vc0 + VW].rearrange("(b p) c -> p b c", p=128)),
                    [], [d["bv"]], dma=buf("d_hv%d" % (hi % 2)))
                p.op("vector", lambda e, d=d, VW=VW: e.memset(d["v"][:, :, VW:VW + 1], 1.0), [d["bv"]], [d["bv"]])
                srcw = bass.AP(tensor=tabT.tensor, offset=hi * TL, ap=[[1, 128], [1, TW]])
                p.op("sync", lambda e, d=d, srcw=srcw: e.dma_start(out=d["w"], in_=srcw), [], [d["bw"]], dma=buf("d_hw%d" % (hi % 2)))
                cmi = 0 if kind == "diff" else 1
                p.op("vector", lambda e, d=d, cmi=cmi: e.tensor_tensor(out=d["w"], in0=d["w"], in1=cm_t[cmi], op=ALU.add),
                     [d["bw"], b_cm], [d["bw"]])
                gt_ap = gdf if kind == "diff" else gdl
                for qt in range(tstart // 512, NT5):
                    q0 = qt * 512
                    kb_hi = 4 * qt + 3
                    kb_lo = 0 if kind == "diff" else max(0, 4 * qt - 16)
                    for j in range(nmap):
                        pend = []

                        def emit_pv(item, d=d, VW=VW, kb_lo=kb_lo, kb_hi=kb_hi):
                            kb, pt, ptb = item

                            def fnpv(e, pt=pt, d=d, kb=kb, VW=VW, first=(kb == kb_lo), last=(kb == kb_hi)):
                                ins = None
                                for s in range(4):
                                    ins = e.matmul(obk[s][0][:, 0:VW + 1], lhsT=pt[:, s * 128:(s + 1) * 128], rhs=d["v"][:, kb, 0:VW + 1],
                                                   start=first, stop=last)
                                return ins
                            p.op("tensor", fnpv, [ptb, d["bv"]], [o[1] for o in obk])

                        for kb in range(kb_lo, kb_hi + 1):
                            k0 = kb * 128
                            delta = q0 + 511 - k0
                            if kind == "diff":
                                delta = min(delta, DCL)
                            base = DMAXT - delta
                            assert 0 <= base and base + 512 <= TW
                            bank, bb = sbk.next()
                            p.op("tensor", lambda e, bank=bank, d=d, j=j, k0=k0, q0=q0: e.matmul(
                                bank[:], lhsT=d["k"][j][:, k0:k0 + 128], rhs=d["q"][j][:, q0:q0 + 512], start=True, stop=True),
                                [d["bk"], d["bq"]], [bb])
                            ts_, tsb = TSR.next()
                            p.op("vector", lambda e, ts_=ts_, bank=bank, d=d, base=base: e.scalar_tensor_tensor(
                                out=ts_, in0=bank[:], scalar=float(scale), in1=d["w"][:, base:base + 512], op0=ALU.mult, op1=ALU.add),
                                [bb, d["bw"]], [tsb])
                            pt, ptb = PTR.next()
                            p.op("scalar", lambda e, pt=pt, ts_=ts_, kb=kb: e.activation(out=pt[:], in_=ts_, func=AF.Exp, bias=kmask_t[:, kb:kb + 1]), [tsb, b_const], [ptb])
                            pend.append((kb, pt, ptb))
                            if len(pend) > LOOK:
                                emit_pv(pend.pop(0))
                        while pend:
                            emit_pv(pend.pop(0))
                        yns = []
                        R4 = range(4)
                        sms = [SM.next() for _ in R4]
                        for s in R4:
                            ob, obb = obk[s]
                            smt, smb = sms[s]
                            p.op("vector", lambda e, smt=smt, ob=ob, VW=VW: e.tensor_scalar(out=smt[:, 6:7], in0=ob[:, VW:VW + 1], scalar1=1e-30, scalar2=None, op0=ALU.max), [obb], [smb])
                        for s in R4:
                            smt, smb = sms[s]
                            p.op("vector", lambda e, smt=smt: e.reciprocal(out=smt[:, 0:1], in_=smt[:, 6:7]), [smb], [smb])
                        if kind == "diff" and j == 0:
                            for s in R4:
                                ob, obb = obk[s]
                                smt, smb = sms[s]
                                y1, y1b = y1t[s], buf("y1t%d" % s)
                                p.op("scalar", lambda e, y1=y1, ob=ob, smt=smt: e.activation(
                                    out=y1[:, 0:256], in_=ob[:, 0:256], func=AF.Copy, scale=smt[:, 0:1]), [obb, smb], [y1b])
                            continue
                        y2s = [Y2.next() for _ in R4]
                        if kind == "diff":
                            for s in R4:
                                smt, smb = sms[s]
                                p.op("vector", lambda e, smt=smt: e.tensor_tensor(out=smt[:, 1:2], in0=smt[:, 0:1], in1=lams[:, 5:6], op=ALU.mult),
                                     [smb, b_lam], [smb])
                            for s in R4:
                                ob, obb = obk[s]
                                smt, smb = sms[s]
                                y2, y2b = y2s[s]
                                y1, y1b = y1t[s], buf("y1t%d" % s)
                                p.op("vector", lambda e, y2=y2, ob=ob, smt=smt, y1=y1: e.scalar_tensor_tensor(
                                    out=y2[:, 0:256], in0=ob[:, 0:256], scalar=smt[:, 1:2], in1=y1[:, 0:256], op0=ALU.mult, op1=ALU.add),
                                    [obb, smb, y1b], [y2b])
                        else:
                            for s in R4:
                                ob, obb = obk[s]
                                smt, smb = sms[s]
                                y2, y2b = y2s[s]
                                p.op("scalar", lambda e, y2=y2, ob=ob, smt=smt: e.activation(
                                    out=y2[:, 0:128], in_=ob[:, 0:128], func=AF.Copy, scale=smt[:, 0:1]), [obb, smb], [y2b])
                        yqs = [YSQ.next() for _ in R4]
                        for s in R4:
                            y2, y2b = y2s[s]
                            yq, yqb = yqs[s]
                            p.op("vector", lambda e, yq=yq, y2=y2, VW=VW: e.tensor_tensor(out=yq[:, 0:VW], in0=y2[:, 0:VW], in1=y2[:, 0:VW], op=ALU.mult),
                                 [y2b], [yqb])
                        for s in R4:
                            yq, yqb = yqs[s]
                            smt, smb = sms[s]
                            p.op("vector", lambda e, yq=yq, smt=smt, VW=VW: e.reduce_sum(out=smt[:, 2:3], in_=yq[:, 0:VW], axis=AX.X), [yqb, smb], [smb])
                        for s in R4:
                            smt, smb = sms[s]
                            p.op("vector", lambda e, smt=smt, VW=VW: e.tensor_scalar(out=smt[:, 3:4], in0=smt[:, 2:3], scalar1=float(1.0 / VW),
                                                                                     scalar2=float(LN_EPS), op0=ALU.mult, op1=ALU.add), [smb], [smb])
                        for s in R4:
                            smt, smb = sms[s]
                            p.op("scalar", lambda e, smt=smt: e.activation(out=smt[:, 4:5], in_=smt[:, 3:4], func=AF.Sqrt), [smb], [smb])
                        for s in R4:
                            smt, smb = sms[s]
                            p.op("vector", lambda e, smt=smt: e.reciprocal(out=smt[:, 5:6], in_=smt[:, 4:5]), [smb], [smb])
                        for s in R4:
                            smt, smb = sms[s]
                            y2, y2b = y2s[s]
                            yn, ynbuf = YNB.next()
                            p.op("vector", lambda e, yn=yn, y2=y2, smt=smt, VW=VW, gt_ap=gt_ap: e.scalar_tensor_tensor(
                                out=yn[:, 0:VW], in0=y2[:, 0:VW], scalar=smt[:, 5:6], in1=gt_ap[:, 0:VW], op0=ALU.mult, op1=ALU.mult),
                                [y2b, smb, b_g], [ynbuf])
                            yns.append((yn, ynbuf))
                        if kind == "diff" and j == 0:
                            continue
                        for ec in range(VW // 128):
                            tb_, tbb = BK[7]
                            for s in range(4):
                                yn, ynbuf = yns[s]
                                p.op("tensor", lambda e, tb_=tb_, yn=yn, ec=ec, s=s: e.matmul(
                                    tb_[:, (3 - s) * 128:(4 - s) * 128], lhsT=yn[:, ec * 128:(ec + 1) * 128], rhs=jmat_b[:], start=True, stop=True),
                                    [ynbuf, b_const], [tbb])
                            st2, sbf2 = STB.next()
                            evict(st2[:], tb_[:], [tbb], [sbf2])
                            r0 = yrow0 + ec * 128
                            p.op("sync", lambda e, st2=st2, r0=r0, q0=q0: e.dma_start(out=yT[r0:r0 + 128, q0:q0 + 512], in_=st2[:]),
                                 [sbf2], [], dma=sbf2)
            p.barrier()

        tbk_sel = [BK[6], BK[7]]

        def proj_residual(wc, b_wc, nk, act_sb, b_act, ntok, T0, gcol, res_src, dst, wslot, wb, NW, ksplit, hfs=None):
            bk = Ring(BK[0:4])
            kper = nk // ksplit
            wi = proj_wi[0]
            for ct in range(D // NW):
                c0 = ct * NW
                slots = []
                for ks in range(ksplit):
                    ws, wsb = wslot[wi % len(wslot)], wb[wi % len(wslot)]
                    wi += 1
                    p.op("gpsimd", lambda e, ws=ws, ct=ct, ks=ks: e.dma_start(
                        out=ws.rearrange("p k n -> p (k n)"), in_=wc[ct * ksplit + ks]), [b_wc], [wsb], dma=wsb)
                    slots.append((ws, wsb))
                units = [(jc, hf) for jc in range(NW // 128) for hf in (hfs if hfs is not None else range(ntok // 512))]
                ubanks = [bk.next() for _ in units]
                for ks in range(ksplit):
                    ws, wsb = slots[ks]
                    for (jc, hf), (bank, bb) in zip(units, ubanks):
                        pairs = [(bank[:], ws[:, ii, jc * 128:(jc + 1) * 128], act_sb(ks * kper + ii, hf)) for ii in range(kper)]
                        mm_group(bank, pairs, [b_act, wsb], [bb], first=(ks == 0), last=(ks == ksplit - 1))
                for (jc, hf), (bank, bb) in zip(units, ubanks):
                    j = c0 // 128 + jc
                    xs, xsb = XR.next()
                    tt = T0 + hf * 512
                    p.op("sync", lambda e, xs=xs, j=j, tt=tt: e.dma_start(out=xs[:], in_=res_src[j * 128:(j + 1) * 128, tt:tt + 512]),
                         [], [xsb], dma=xsb)
                    st, sbf = STF.next()
                    p.op("vector", lambda e, st=st, bank=bank, j=j, xs=xs: e.scalar_tensor_tensor(
                        out=st[:], in0=bank[:], scalar=modt[:, gcol + j:gcol + j + 1], in1=xs[:], op0=ALU.mult, op1=ALU.add),
                        [bb, xsb, b_mod], [sbf])
                    p.op("sync", lambda e, st=st, j=j, tt=tt: e.dma_start(out=dst[j * 128:(j + 1) * 128, tt:tt + 512], in_=st[:]),
                         [sbf], [], dma=sbf)
            proj_wi[0] = wi

        proj_wi = [0]

        def phase_wo(l, cur, z1, tstart=0):
            yS = aview(0, KC * TBA * 2, BF16, "p (k t) -> p k t", t=TBA)
            b_y = buf("yS")
            WOFF = KC * TBA * 2
            wslot = [aview(WOFF + i * KC * NWB * 2, KC * NWB * 2, BF16, "p (k n) -> p k n", n=NWB) for i in range(2)]
            wb = bufs("wsl", 2)
            for tb in range(S // TBA):
                T0 = tb * TBA
                if T0 + TBA <= tstart:
                    continue
                p.op("sync", lambda e, T0=T0: e.dma_start(out=yS, in_=yT[:, T0:T0 + TBA].rearrange("(k p) t -> p k t", p=128)),
                     [], [b_y], dma=buf("d_yS"))
                proj_residual(wc_o[l], buf("wc_o%d" % l), KC, lambda kc, hf: yS[:, kc, hf * 512:(hf + 1) * 512], b_y, TBA, T0,
                              2 * KC, cur, z1, wslot, wb, NWB, 1, hfs=[hf for hf in range(TBA // 512) if T0 + hf * 512 >= tstart])
                p.barrier()

        def phase_ffn(l, z1, x1, z2, tstart=0):
            u2 = aview(0, KC * 512 * 2, BF16, "p (k t) -> p k t", t=512)
            b_u2 = buf("u2")
            hoff = KC * 512 * 2
            hT = aview(hoff, FC * 512 * 2, BF16, "p (k t) -> p k t", t=512)
            b_h = buf("hT")
            WOFF = hoff + FC * 512 * 2
            WSZ = KC * NWB * 2
            wslot_up = [aview(WOFF + i * WSZ, KC * NWB * 2, BF16, "p (k n) -> p k n", n=NWB) for i in range(2)]
            wslot_dn = [aview(WOFF + i * WSZ, KPER * NWD * 2, BF16, "p (k n) -> p k n", n=NWD) for i in range(2)]
            assert WOFF + 2 * WSZ <= C.ARENA, (WOFF + 2 * WSZ)
            wb = bufs("wsl", 2)
            x1b = bufs("x1c", KC)
            bk = Ring(BK[0:4])
            e1 = LN_EPS / (C.alpha ** 2)
            for tb in range(tstart // 512, S // 512):
                t0 = tb * 512
                ln_stats(z1, t0, e1, mean_t, rstd_t, b_mean, b_rstd)
                st2 = ln_stats_begin()
                ln_apply(z1, t0, mean_t, rstd_t, b_mean, b_rstd, 0, KC, (vect, b_vec), dst_dram=x1, stats2=st2, dst_bufs=x1b)
                stats_finish(st2[0], st2[1], LN_EPS, mean2_t, rstd2_t, b_mean2, b_rstd2)
                ln_apply(x1, t0, mean2_t, rstd2_t, b_mean2, b_rstd2, 4 * KC, 3 * KC, (modt, b_mod),
                         dst_sb=lambda kc: u2[:, kc, :], dst_sb_buf=b_u2, src_bufs=x1b)
                p.barrier()
                bk8 = Ring(BK[0:8])
                for fg in range(0, FC, PER):
                    nch = min(PER, FC - fg)
                    slots = []
                    for half in range(2):
                        ti = (fg // PER) * 2 + half
                        ws, wsb = wslot_up[proj_wi[0] % 2], wb[proj_wi[0] % 2]
                        proj_wi[0] += 1
                        p.op("gpsimd", lambda e, ws=ws, ti=ti: e.dma_start(
                            out=ws.rearrange("p k n -> p (k n)"), in_=wc_up[l][ti]), [buf("wc_up%d" % l)], [wsb], dma=wsb)
                        slots.append((ws, wsb))
                    gb, vb = [], []
                    for jc in range(nch):
                        bg, bgb = bk8.next()
                        mm_group(bg, [(bg[:], slots[0][0][:, kc, jc * 128:(jc + 1) * 128], u2[:, kc, :]) for kc in range(KC)],
                                 [b_u2, slots[0][1]], [bgb])
                        gb.append((bg, bgb))
                    for jc in range(nch):
                        bv, bvb = bk8.next()
                        mm_group(bv, [(bv[:], slots[1][0][:, kc, jc * 128:(jc + 1) * 128], u2[:, kc, :]) for kc in range(KC)],
                                 [b_u2, slots[1][1]], [bvb])
                        vb.append((bv, bvb))
                    for jc in range(nch):
                        fc = fg + jc
                        bg, bgb = gb[jc]
                        bv, bvb = vb[jc]
                        ge, geb = GEXT.next()
                        p.op("vector", lambda e, ge=ge, fc=fc, tb=tb: e.tensor_scalar(out=ge[:, 0:2], in0=halo[:, fc, :], scalar1=hmask_t[:, tb:tb + 1], scalar2=None, op0=ALU.mult), [b_halo, b_const], [geb])
                        p.op("scalar", lambda e, ge=ge, bg=bg: e.activation(out=ge[:, 2:514], in_=bg[:], func=AF.Copy), [bgb], [geb])
                        p.op("vector", lambda e, ge=ge, fc=fc: e.tensor_copy(out=halo[:, fc, :], in_=ge[:, 512:514]), [geb], [b_halo])
                        a_, ab = CA.next()
                        p.op("scalar", lambda e, a_=a_, ge=ge, fc=fc: e.activation(
                            out=a_[:], in_=ge[:, 0:512], func=AF.Identity, scale=convt[:, fc:fc + 1], bias=convt[:, 3 * FC + fc:3 * FC + fc + 1]),
                            [geb, b_vec], [ab])
                        b2, b2b = CB.next()
                        p.op("vector", lambda e, b2=b2, ge=ge, fc=fc, a_=a_: e.scalar_tensor_tensor(
                            out=b2[:], in0=ge[:, 1:513], scalar=convt[:, FC + fc:FC + fc + 1], in1=a_[:], op0=ALU.mult, op1=ALU.add),
                            [geb, ab, b_vec], [b2b])
                        a2, a2b = CA.next()
                        p.op("vector", lambda e, a2=a2, ge=ge, fc=fc, b2=b2: e.scalar_tensor_tensor(
                            out=a2[:], in0=ge[:, 2:514], scalar=convt[:, 2 * FC + fc:2 * FC + fc + 1], in1=b2[:], op0=ALU.mult, op1=ALU.add),
                            [geb, b2b, b_vec], [a2b])
                        b3, b3b = CB.next()
                        p.op("scalar", lambda e, b3=b3, a2=a2: e.activation(out=b3[:], in_=a2[:], func=AF.Silu), [a2b], [b3b])
                        p.op("vector", lambda e, fc=fc, b3=b3, bv=bv: e.tensor_tensor(out=hT[:, fc, :], in0=b3[:], in1=bv[:], op=ALU.mult),
                             [b3b, bvb], [b_h])
                proj_residual(wc_dn[l], buf("wc_dn%d" % l), FC, lambda kc, hf: hT[:, kc, :], b_h, 512, t0, 5 * KC, x1, z2,
                              wslot_dn, wb, NWD, KSPLIT)
                p.barrier()

        def phase_ln_out(l, z2, x2, tstart=0):
            e1 = LN_EPS / (C.alpha ** 2)
            for tb in range(tstart // 512, S // 512):
                t0 = tb * 512
                ln_stats(z2, t0, e1, mean_t, rstd_t, b_mean, b_rstd)
                ln_apply(z2, t0, mean_t, rstd_t, b_mean, b_rstd, 2 * KC, 3 * KC, (vect, b_vec), dst_dram=x2)
            p.barrier()

        def phase_transpose_out(src):
            bk = Ring(BK[0:4])
            xo = [aview(i * D * 4, D * 4, F32) for i in range(2)]
            xob = bufs("xo", 2)
            for g in range((S - SEG) // 512, S // 512):
                tiles = []
                for kc in range(KC):
                    pass
                for i4 in range(4):
                    pass
                for i in range(4):
                    ot, otb = xo[i % 2], xob[i % 2]
                    for kg in range(0, KC, 4):
                        bank, bb = bk.next()
                        srcs = []
                        for kk in range(4):
                            kc = kg + kk
                            xs, xsb = XR.next()
                            p.op("sync", lambda e, xs=xs, kc=kc, g=g, i=i: e.dma_start(
                                out=xs[:, 0:128], in_=src[kc * 128:(kc + 1) * 128, g * 512 + i * 128:g * 512 + (i + 1) * 128]),
                                [], [xsb], dma=xsb)
                            srcs.append((xs, xsb))

                        def fn(e, bank=bank, srcs=srcs):
                            ins = None
                            for kk in range(4):
                                ins = e.transpose(bank[:, kk * 128:(kk + 1) * 128], srcs[kk][0][:, 0:128], ident_f[:])
                            return ins
                        p.op("tensor", fn, [s_[1] for s_ in srcs] + [b_const], [bb])
                        evict(ot[:, kg * 128:(kg + 4) * 128], bank[:], [bb], [otb])
                    r0 = g * 512 + i * 128 - (S - SEG)
                    p.op("sync", lambda e, ot=ot, r0=r0: e.dma_start(out=out[r0:r0 + 128, :], in_=ot), [otb], [], dma=otb)
            p.barrier()

        conv_w_in(0)
        phase_transpose_in()
        cur = xT0
        for l in range(L):
            z1, x1, z2, x2 = zs[l]
            ts_l = TPART if l == L - 1 else 0
            phase_params(l)
            phase_qkv(l, cur, ts_l)
            conv_w_rest(l)
            if l + 1 < L:
                conv_w_in(l + 1)
            phase_attn(l, ts_l)
            phase_wo(l, cur, z1, ts_l)
            phase_ffn(l, z1, x1, z2, ts_l)
            phase_ln_out(l, z2, x2, (S - SEG) if l == L - 1 else 0)
            cur = x2
            if C.stop_after is not None and l == C.stop_after:
                break
        phase_transpose_out(cur)

        with nc.Block() as block:
            p.replay(block)
    return nc


def host_inputs(C, b, inp, j=None):
    S, D, KC, FC, L = C.S, C.D, C.KC, C.FC, C.depth
    nseg = S // C.SEG
    if j is None:
        j = nseg - 1
    npad = (nseg - 1 - j) * C.SEG
    f = lambda a: np.ascontiguousarray(a, dtype=np.float32)
    fm = lambda v, n: f(v.reshape(n, 128).T)
    d = {}
    xw = np.zeros((S, D), np.float32)
    xw[npad:] = inp["x"][b][0:S - npad]
    d["x"] = xw
    tok = np.arange(S)
    d["kmask_fm"] = f(np.where(tok < npad, NEG, 0.0).reshape(S // 128, 128).T)
    hm = np.where(np.arange(S // 512) * 512 <= npad, 0.0, 1.0).astype(np.float32)
    d["hmask"] = f(np.broadcast_to(hm[None, :], (128, S // 512)))
    d["c_fm"] = fm(inp["c"][b], KC)
    d["w_ada"] = f(inp["w_ada"])
    d["b_ada_fm"] = f(np.stack([fm(inp["b_ada"][l], 6 * KC) for l in range(L)]))
    d["w_in"] = f(inp["w_in"])
    d["lamv"] = f(np.stack([np.concatenate([inp["lambda_q1"][l], inp["lambda_k1"][l], inp["lambda_q2"][l], inp["lambda_k2"][l]])
                            for l in range(L)]))
    d["g_diff"] = f(inp["g_diff"])
    d["g_dil"] = f(inp["g_dil"])
    d["w_o"] = f(inp["w_o"])
    d["vec_fm"] = f(np.stack([np.concatenate([fm(inp[k][l], KC) for k in ("ln1_g", "ln1_b", "ln2_g", "ln2_b")], axis=1)
                              for l in range(L)]))
    d["w_up"] = f(inp["w_up"])
    d["conv_fm"] = f(np.stack([np.concatenate([fm(inp["conv_w"][l][0], FC), fm(inp["conv_w"][l][1], FC),
                                               fm(inp["conv_w"][l][2], FC), fm(inp["conv_b"][l], FC)], axis=1) for l in range(L)]))
    d["w_down"] = f(inp["w_down"])
    dist = DMAXT - np.arange(TL)
    idx = t5_bucket_np(dist)
    d["tabT"] = f(inp["rel_bias"][idx, :].T)
    cm = np.zeros((2, TL), np.float32)
    cm[0] = np.where(dist >= 0, 0.0, NEG)
    mult = ((dist >= 0) & (dist <= 128)).astype(np.float64) + ((dist >= 0) & (dist % 4 == 0) & (dist <= 512)) \
        + ((dist >= 0) & (dist % 16 == 0) & (dist <= 2048))
    cm[1] = np.where(mult > 0, np.log(np.maximum(mult, 1.0)), NEG)
    d["cmask"] = cm
    d["ident"] = np.eye(128, dtype=np.float32)
    d["jmat"] = f(np.eye(128, dtype=np.float32)[::-1])
    return d


_NC_CACHE = {}


def kernel(**inputs):
    C = Cfg()
    if "nc" not in _NC_CACHE:
        _NC_CACHE["nc"] = build(C)
    nc = _NC_CACHE["nc"]
    inp = {k: np.asarray(v) for k, v in inputs.items()}
    nseg = C.S // C.SEG
    shared = {}
    in_maps = []
    for i in range(8):
        d = host_inputs(C, i // nseg, inp, i % nseg)
        for k in list(d.keys()):
            if k in ("w_ada", "w_in", "w_o", "w_up", "w_down"):
                d[k] = shared.setdefault(k, d[k])
        in_maps.append(d)
    res = run_bass_kernel_spmd(nc, in_maps, core_ids=list(range(8)))
    outp = np.empty((2, C.S, C.D), np.float32)
    for i in range(8):
        b, j = i // nseg, i % nseg
        outp[b, j * C.SEG:(j + 1) * C.SEG] = res.results[i]["out"]
    return outp
```
